# Optimizing a Trainium2 kernel written in Bass

```python
import math
import jax, jax.numpy as jnp
from jax import lax
import numpy as np

D_MODEL = 1024
BATCH = 32
SEQ = 2048
DEPTH = 4
DEC_BATCH = 16
DEC_SEQ = 32
PAST_LEN = 2048

CHUNK = 64
PLE_DIM = 256
N_MIXERS = 2
N_SSD = (DEPTH + 1) // 2
N_MLA = DEPTH // 2
EPS = 1e-6

SSD_D_INNER = 2 * D_MODEL
SSD_HEAD_DIM = 64
SSD_HEADS = SSD_D_INNER // SSD_HEAD_DIM
SSD_GROUPS = 8
SSD_HPG = SSD_HEADS // SSD_GROUPS
SSD_STATE = 128
SSD_CONV_W = 4
SSD_CONV_DIM = SSD_D_INNER + 2 * SSD_GROUPS * SSD_STATE
SSD_IN = SSD_D_INNER + SSD_CONV_DIM + SSD_HEADS
SSD_CHUNK = CHUNK

MLA_HEADS = 16
MLA_NOPE = 64
MLA_ROPE = 32
MLA_V = 64
MLA_Q_LORA = 384
MLA_KV_LORA = 256
MLA_QK = MLA_NOPE + MLA_ROPE
MLA_WIDTH = MLA_HEADS * MLA_V
MLA_IN = MLA_Q_LORA + MLA_KV_LORA + MLA_ROPE + MLA_WIDTH
ROPE_BASE = 10000.0
Q_BLOCK = 128

kernel_name = "hybrid_ssd_mla_streaming_step"

F32 = jnp.float32


def rmsnorm(x, w):
    xf = x.astype(F32)
    y = xf * lax.rsqrt(jnp.mean(xf * xf, axis=-1, keepdims=True) + EPS)
    return (y * w.astype(F32)).astype(x.dtype)


def causal_conv(xbc, prev, w, b):
    T = xbc.shape[1]
    xp = jnp.concatenate([prev.astype(xbc.dtype), xbc], axis=1)
    out = b
    for k in range(SSD_CONV_W):
        out = out + xp[:, k:k + T] * w[k]
    return out, xp[:, xp.shape[1] - (SSD_CONV_W - 1):]


def ssd_scan(x, dt, A, Bm, Cm, h0, chunk):
    b, T = x.shape[:2]
    c = T // chunk
    x = x.reshape(b, c, chunk, SSD_GROUPS, SSD_HPG, SSD_HEAD_DIM)
    dt = dt.reshape(b, c, chunk, SSD_GROUPS, SSD_HPG)
    Bm = Bm.reshape(b, c, chunk, SSD_GROUPS, SSD_STATE)
    Cm = Cm.reshape(b, c, chunk, SSD_GROUPS, SSD_STATE)
    Acs = jnp.cumsum(dt * A, axis=2)
    tri = jnp.tril(jnp.ones((chunk, chunk), bool))[:, :, None, None]
    seg = Acs[:, :, :, None] - Acs[:, :, None, :]
    Lmat = jnp.exp(jnp.where(tri, seg, -jnp.inf))
    CB = jnp.einsum('bclgn,bcsgn->bclsg', Cm, Bm)
    M = CB[..., None] * Lmat * dt[:, :, None]
    y_diag = jnp.einsum('bclsgr,bcsgrp->bclgrp', M, x)
    decay = jnp.exp(Acs[:, :, -1:] - Acs)
    states = jnp.einsum('bclgn,bclgr,bclgrp->bcgrpn', Bm, decay * dt, x).astype(F32)
    chunk_decay = jnp.exp(Acs[:, :, -1])

    def step(h, inp):
        s, d = inp
        return h * d[..., None, None] + s, h

    hT, h_prev = lax.scan(step, h0.astype(F32),
                          (jnp.moveaxis(states, 1, 0), jnp.moveaxis(chunk_decay, 1, 0)))
    h_prev = jnp.moveaxis(h_prev, 0, 1)
    y_off = jnp.einsum('bclgn,bcgrpn,bclgr->bclgrp', Cm, h_prev, jnp.exp(Acs))
    y = (y_diag + y_off).reshape(b, T, SSD_GROUPS, SSD_HPG, SSD_HEAD_DIM)
    return y, hT


def ssd_mixer(hn, conv_prev, h0, chunk, in_w, conv_w, conv_b, dt_bias, A_log, Dp, norm_w, out_w):
    b, T, _ = hn.shape
    proj = hn @ in_w
    z, xbc, dt = jnp.split(proj, [SSD_D_INNER, SSD_D_INNER + SSD_CONV_DIM], axis=-1)
    xbc, conv_new = causal_conv(xbc, conv_prev, conv_w, conv_b)
    xbc = jax.nn.silu(xbc)
    xs, Bm, Cm = jnp.split(xbc, [SSD_D_INNER, SSD_D_INNER + SSD_GROUPS * SSD_STATE], axis=-1)
    xs = xs.reshape(b, T, SSD_GROUPS, SSD_HPG, SSD_HEAD_DIM)
    Bm = Bm.reshape(b, T, SSD_GROUPS, SSD_STATE)
    Cm = Cm.reshape(b, T, SSD_GROUPS, SSD_STATE)
    dt = jax.nn.softplus((dt + dt_bias).astype(F32)).reshape(b, T, SSD_GROUPS, SSD_HPG)
    A = -jnp.exp(A_log.astype(F32)).reshape(SSD_GROUPS, SSD_HPG)
    h0 = h0.reshape(b, SSD_GROUPS, SSD_HPG, SSD_HEAD_DIM, SSD_STATE)
    y, hT = ssd_scan(xs, dt, A, Bm, Cm, h0, chunk)
    y = y + Dp.reshape(SSD_GROUPS, SSD_HPG)[..., None] * xs
    y = y.reshape(b, T, SSD_D_INNER) * jax.nn.silu(z)
    y = rmsnorm(y.reshape(b, T, SSD_GROUPS, SSD_D_INNER // SSD_GROUPS),
                norm_w.reshape(SSD_GROUPS, SSD_D_INNER // SSD_GROUPS)).reshape(b, T, SSD_D_INNER)
    out = (y @ out_w).astype(hn.dtype)
    return out, conv_new, hT.reshape(b, SSD_HEADS, SSD_HEAD_DIM, SSD_STATE)


def rope_cs(pos):
    inv = 1.0 / (ROPE_BASE ** (jnp.arange(0, MLA_ROPE, 2, dtype=F32) / MLA_ROPE))
    ang = pos.astype(F32)[:, None] * inv[None, :]
    return jnp.cos(ang), jnp.sin(ang)


def apply_rope(x, cos, sin):
    x1, x2 = jnp.split(x, 2, axis=-1)
    return jnp.concatenate([x1 * cos - x2 * sin, x1 * sin + x2 * cos], axis=-1).astype(x.dtype)


def chunk_causal_attention(q_nope, q_rope, k_nope, k_rope, v, q_pos, k_pos):
    scale = MLA_QK ** -0.5
    k_chunk = k_pos // CHUNK

    def block(args):
        qn, qr, qp = args
        s = (jnp.einsum('bqhd,bkhd->bhqk', qn, k_nope)
             + jnp.einsum('bqhd,bkd->bhqk', qr, k_rope)).astype(F32) * scale
        mask = k_chunk[None, :] <= (qp // CHUNK)[:, None]
        p = jax.nn.softmax(jnp.where(mask, s, -jnp.inf), axis=-1).astype(v.dtype)
        return jnp.einsum('bhqk,bkhd->bqhd', p, v)

    b, Tq = q_nope.shape[:2]
    if Tq > Q_BLOCK and Tq % Q_BLOCK == 0:
        nb = Tq // Q_BLOCK
        split = lambda t: jnp.moveaxis(t.reshape(b, nb, Q_BLOCK, *t.shape[2:]), 1, 0)
        o = lax.map(block, (split(q_nope), split(q_rope), q_pos.reshape(nb, Q_BLOCK)))
        return jnp.moveaxis(o, 0, 1).reshape(b, Tq, MLA_HEADS, MLA_V)
    return block((q_nope, q_rope, q_pos))


def mla_mixer(hn, lat_past, kr_past, pos0, in_w, q_a_norm, q_b_w, kv_a_norm, kv_b_w,
              qn_norm, qr_norm, kn_norm, kr_norm, out_w):
    b, T, _ = hn.shape
    proj = hn @ in_w
    q_a, kv_a, k_r, gate = jnp.split(
        proj, [MLA_Q_LORA, MLA_Q_LORA + MLA_KV_LORA, MLA_Q_LORA + MLA_KV_LORA + MLA_ROPE], axis=-1)
    q = (rmsnorm(q_a, q_a_norm) @ q_b_w).reshape(b, T, MLA_HEADS, MLA_QK)
    q_pos = pos0 + jnp.arange(T)
    cos, sin = rope_cs(q_pos)
    q_nope = rmsnorm(q[..., :MLA_NOPE], qn_norm)
    q_rope = apply_rope(rmsnorm(q[..., MLA_NOPE:], qr_norm), cos[:, None], sin[:, None])
    lat_new = rmsnorm(kv_a, kv_a_norm)
    kr_new = apply_rope(rmsnorm(k_r, kr_norm), cos, sin)
    if lat_past is None:
        lat_all, kr_all = lat_new, kr_new
    else:
        lat_all = jnp.concatenate([lat_past.astype(lat_new.dtype), lat_new], axis=1)
        kr_all = jnp.concatenate([kr_past.astype(kr_new.dtype), kr_new], axis=1)
    Tk = lat_all.shape[1]
    kv = (lat_all @ kv_b_w).reshape(b, Tk, MLA_HEADS, MLA_NOPE + MLA_V)
    k_nope = rmsnorm(kv[..., :MLA_NOPE], kn_norm)
    v = kv[..., MLA_NOPE:]
    o = chunk_causal_attention(q_nope, q_rope, k_nope, kr_all, v, q_pos, jnp.arange(Tk))
    o = o.reshape(b, T, MLA_WIDTH) * jax.nn.silu(gate)
    return (o @ out_w).astype(hn.dtype), lat_new, kr_new


def ple(h, p, up_w, norm_w, gate_w):
    g = jax.nn.sigmoid((rmsnorm(h, norm_w) @ gate_w).astype(F32)).astype(h.dtype)
    return (p @ up_w).astype(h.dtype) * g


def setup_inputs(seed: int = 0) -> dict:
    key = jax.random.key(seed)
    ks = iter(jax.random.split(key, 48))
    nrm = lambda shape, scale: jax.random.normal(next(ks), shape, F32) * scale
    gain = lambda shape: 1.0 + 0.05 * jax.random.normal(next(ks), shape, F32)
    u = jax.random.uniform(next(ks), (N_SSD, SSD_HEADS), F32)
    dt0 = jnp.exp(u * (math.log(0.1) - math.log(0.001)) + math.log(0.001))
    dt_bias = dt0 + jnp.log(-jnp.expm1(-dt0))
    A_log = jnp.log(jax.random.uniform(next(ks), (N_SSD, SSD_HEADS), F32, 1.0, 16.0))
    return {
        "x_prompt": nrm((BATCH, SEQ, D_MODEL), 1.0),
        "x_sample": nrm((DEC_BATCH, DEC_SEQ, D_MODEL), 1.0),
        "cache_conv": nrm((N_SSD, DEC_BATCH, SSD_CONV_W - 1, SSD_CONV_DIM), 1.0),
        "state_ssm": nrm((N_SSD, DEC_BATCH, SSD_HEADS, SSD_HEAD_DIM, SSD_STATE), 0.1),
        "cache_kv_latent": nrm((N_MLA, DEC_BATCH, PAST_LEN, MLA_KV_LORA), 1.0),
        "cache_k_rope": nrm((N_MLA, DEC_BATCH, PAST_LEN, MLA_ROPE), 1.0),
        "p_prompt": nrm((DEPTH, BATCH, SEQ, PLE_DIM), 1.0),
        "p_sample": nrm((DEPTH, DEC_BATCH, DEC_SEQ, PLE_DIM), 1.0),
        "ln_w": gain((DEPTH, D_MODEL)),
        "ssd_in_w": nrm((N_SSD, D_MODEL, SSD_IN), D_MODEL ** -0.5),
        "ssd_conv_w": nrm((N_SSD, SSD_CONV_W, SSD_CONV_DIM), SSD_CONV_W ** -0.5),
        "ssd_conv_b": nrm((N_SSD, SSD_CONV_DIM), 0.02),
        "ssd_dt_bias": dt_bias,
        "ssd_A_log": A_log,
        "ssd_D": gain((N_SSD, SSD_HEADS)),
        "ssd_norm_w": gain((N_SSD, SSD_D_INNER)),
        "ssd_out_w": nrm((N_SSD, SSD_D_INNER, D_MODEL), SSD_D_INNER ** -0.5),
        "mla_in_w": nrm((N_MLA, D_MODEL, MLA_IN), D_MODEL ** -0.5),
        "mla_q_a_norm": gain((N_MLA, MLA_Q_LORA)),
        "mla_q_b_w": nrm((N_MLA, MLA_Q_LORA, MLA_HEADS * MLA_QK), MLA_Q_LORA ** -0.5),
        "mla_kv_a_norm": gain((N_MLA, MLA_KV_LORA)),
        "mla_kv_b_w": nrm((N_MLA, MLA_KV_LORA, MLA_HEADS * (MLA_NOPE + MLA_V)), MLA_KV_LORA ** -0.5),
        "mla_q_nope_norm": gain((N_MLA, MLA_NOPE)),
        "mla_q_rope_norm": gain((N_MLA, MLA_ROPE)),
        "mla_k_nope_norm": gain((N_MLA, MLA_NOPE)),
        "mla_k_rope_norm": gain((N_MLA, MLA_ROPE)),
        "mla_out_w": nrm((N_MLA, MLA_WIDTH, D_MODEL), MLA_WIDTH ** -0.5),
        "ple_up_w": nrm((DEPTH, PLE_DIM, D_MODEL), PLE_DIM ** -0.5),
        "ple_norm_w": gain((DEPTH, D_MODEL)),
        "ple_gate_w": nrm((DEPTH, D_MODEL, D_MODEL), D_MODEL ** -0.5),
    }


def reference(x_prompt, x_sample, cache_conv, state_ssm, cache_kv_latent, cache_k_rope,
              p_prompt, p_sample, ln_w, ssd_in_w, ssd_conv_w, ssd_conv_b, ssd_dt_bias, ssd_A_log,
              ssd_D, ssd_norm_w, ssd_out_w, mla_in_w, mla_q_a_norm, mla_q_b_w, mla_kv_a_norm,
              mla_kv_b_w, mla_q_nope_norm, mla_q_rope_norm, mla_k_nope_norm, mla_k_rope_norm,
              mla_out_w, ple_up_w, ple_norm_w, ple_gate_w):
    hp, hs = x_prompt, x_sample
    conv_p, ssm_p, lat_p, kr_p = [], [], [], []
    conv_s, ssm_s, lat_s, kr_s = [], [], [], []
    for i in range(DEPTH):
        j = i // N_MIXERS
        hn_p = rmsnorm(hp, ln_w[i])
        hn_s = rmsnorm(hs, ln_w[i])
        if i % N_MIXERS == 0:
            w = (ssd_in_w[j], ssd_conv_w[j], ssd_conv_b[j], ssd_dt_bias[j], ssd_A_log[j],
                 ssd_D[j], ssd_norm_w[j], ssd_out_w[j])
            b = hp.shape[0]
            conv0 = jnp.zeros((b, SSD_CONV_W - 1, SSD_CONV_DIM), hp.dtype)
            h0 = jnp.zeros((b, SSD_HEADS, SSD_HEAD_DIM, SSD_STATE), F32)
            yp, cp, sp = ssd_mixer(hn_p, conv0, h0, SSD_CHUNK, *w)
            ys, cs, ss = ssd_mixer(hn_s, cache_conv[j], state_ssm[j], hs.shape[1], *w)
            conv_p.append(cp); ssm_p.append(sp); conv_s.append(cs); ssm_s.append(ss)
        else:
            w = (mla_in_w[j], mla_q_a_norm[j], mla_q_b_w[j], mla_kv_a_norm[j], mla_kv_b_w[j],
                 mla_q_nope_norm[j], mla_q_rope_norm[j], mla_k_nope_norm[j], mla_k_rope_norm[j],
                 mla_out_w[j])
            yp, lp, rp = mla_mixer(hn_p, None, None, 0, *w)
            ys, ls, rs = mla_mixer(hn_s, cache_kv_latent[j], cache_k_rope[j], PAST_LEN, *w)
            lat_p.append(lp); kr_p.append(rp); lat_s.append(ls); kr_s.append(rs)
        hp = hp + yp
        hs = hs + ys
        hp = hp + ple(hp, p_prompt[i], ple_up_w[i], ple_norm_w[i], ple_gate_w[i])
        hs = hs + ple(hs, p_sample[i], ple_up_w[i], ple_norm_w[i], ple_gate_w[i])
    return (hp, hs, jnp.stack(conv_p), jnp.stack(ssm_p), jnp.stack(lat_p), jnp.stack(kr_p),
            jnp.stack(conv_s), jnp.stack(ssm_s), jnp.stack(lat_s), jnp.stack(kr_s))
```

```python
import math
import numpy as np
import ml_dtypes
import concourse.bass as bass
import concourse.mybir as mybir
from concourse.bass_utils import run_bass_kernel_spmd

F32 = mybir.dt.float32
BF16 = mybir.dt.bfloat16
ALU = mybir.AluOpType
AF = mybir.ActivationFunctionType
AX = mybir.AxisListType

EPS = 1e-6
D = 1024
NKC = 8
SSD_IN = 6176
MLA_IN = 1696
NH = 16
SCALE = 96 ** -0.5


class Cfg:
    def __init__(self, nseq_p=4, t_p=2048, nseq_s=2, t_s=32, past=2048, tt=512, depth=4):
        self.nseq_p, self.t_p, self.nseq_s, self.t_s, self.past, self.tt, self.depth = nseq_p, t_p, nseq_s, t_s, past, tt, depth
        self.n_ssd = (depth + 1) // 2
        self.n_mla = depth // 2


ENGS = ['pe', 'act', 'dve', 'pool', 'sp']
NDSEM = 16


class Planner:
    WINDOW = 64
    LAT = 120.0

    def __init__(self):
        self.ops = []
        self.res = {}
        self.segs = [0]
        self.streams = {e: [] for e in ENGS}
        self.cnt = {}
        self.known = {e: {} for e in ENGS}
        self.ndma = {}
        self.finalized = False
        self.labels = {}
        self.wnames = []
        self.times = {}
        self.curlabel = 'start'

    def op(self, eng, fn, r=(), w=(), dma=False, cost=100.0):
        w = list(w) + [n for n in r if n.startswith('ps')]
        r = [n for n in r if not n.startswith('ps')]
        lo = self.segs[-1]
        deps = set()
        for n in r:
            st = self.res.get(n)
            if st and st[0] is not None and st[0] >= lo:
                deps.add(st[0])
        for n in w:
            st = self.res.get(n)
            if st:
                if st[0] is not None and st[0] >= lo:
                    deps.add(st[0])
                for x in st[1]:
                    if x >= lo:
                        deps.add(x)
        i = len(self.ops)
        if i == self.segs[-1]:
            self.labels[i] = self.curlabel
        self.ops.append((eng, fn, sorted(deps), dma, cost))
        self.wnames.append((tuple(w), tuple(r)))
        for n in r:
            self.res.setdefault(n, [None, []])[1].append(i)
        for n in w:
            self.res[n] = [i, []]
        return i

    def barrier(self):
        if self.segs[-1] != len(self.ops):
            self.segs.append(len(self.ops))

    def _schedule(self, lo, hi, reorder=True):
        ops = self.ops
        by = {e: [] for e in ENGS}
        for i in range(lo, hi):
            by[ops[i][0]].append(i)
        if not reorder:
            return by
        W, LAT = self.WINDOW, self.LAT
        finish = {}
        sched = set()
        t = {e: 0.0 for e in ENGS}
        head = {e: 0 for e in ENGS}
        order = {e: [] for e in ENGS}
        cache = {e: None for e in ENGS}
        blocked = {e: set() for e in ENGS}
        left = hi - lo

        def scan(e):
            lst = by[e]
            i = head[e]
            c = 0
            best = None
            blk = set()
            te = t[e]
            n = len(lst)
            while i < n and c < W:
                oid = lst[i]
                if oid in sched:
                    i += 1
                    continue
                c += 1
                ok = True
                rdy = te
                for d in ops[oid][2]:
                    f = finish.get(d)
                    if f is None:
                        ok = False
                        blk.add(d)
                        break
                    if ops[d][0] == 'pe' and e == 'pe' and not ops[d][3]:
                        f2 = f - LAT
                    else:
                        f2 = f
                    if f2 > rdy:
                        rdy = f2
                if ok:
                    if best is None or rdy < best[0]:
                        best = (rdy, i, oid)
                    if rdy <= te:
                        break
                i += 1
            blocked[e] = blk
            cache[e] = best if best is not None else 'none'

        while left:
            bestc = None
            for e in ENGS:
                if head[e] >= len(by[e]):
                    continue
                if cache[e] is None:
                    scan(e)
                c = cache[e]
                if c == 'none':
                    continue
                if bestc is None or c[0] < bestc[0][0]:
                    bestc = (c, e)
            assert bestc is not None, "scheduler stuck"
            (rdy, idx, oid), e = bestc
            eng, fn, deps, dma, cost = ops[oid]
            if dma:
                t[e] = rdy + 60.0
                finish[oid] = rdy + cost + LAT
            else:
                t[e] = rdy + cost
                finish[oid] = t[e] + LAT
            sched.add(oid)
            order[e].append(oid)
            self.times[oid] = (rdy, t[e], lo)
            lst = by[e]
            h = head[e]
            while h < len(lst) and lst[h] in sched:
                h += 1
            head[e] = h
            cache[e] = None
            for e2 in ENGS:
                if e2 != e and oid in blocked[e2]:
                    cache[e2] = None
            left -= 1
        self.est = max(self.est if hasattr(self, 'est') else 0.0, 0.0)
        self.seg_time = getattr(self, 'seg_time', 0.0) + max(t.values())
        busy = {e2: sum(ops[i][4] for i in by[e2] if not ops[i][3]) for e2 in ENGS}
        self.seg_log = getattr(self, 'seg_log', [])
        self.seg_log.append((self.labels.get(lo, '?'), max(t.values()), busy))
        return order

    def finalize(self, reorder=True):
        if self.finalized:
            return
        self.finalized = True
        ops = self.ops
        bounds = self.segs + [len(ops)]
        ev = {}
        for si in range(len(bounds) - 1):
            lo, hi = bounds[si], bounds[si + 1]
            if lo == hi:
                continue
            order = self._schedule(lo, hi, reorder)
            pre_wait = {}
            for e in ENGS:
                for oid in order[e]:
                    dma = ops[oid][3]
                    if dma:
                        nq = self.ndma.get(e, 0)
                        self.ndma[e] = nq + 1
                        key = 'd_%s_%d' % (e, nq % NDSEM)
                        prev = self.cnt.get(key, 0)
                        if prev:
                            pre_wait[oid] = (key, prev)
                        self.cnt[key] = prev + 16
                    else:
                        key = e
                        self.cnt[key] = self.cnt.get(key, 0) + 1
                    ev[oid] = (key, self.cnt[key])
            for e in ENGS:
                kn = self.known[e]
                st = self.streams[e]
                for oid in order[e]:
                    eng, fn, deps, dma, cost = ops[oid]
                    need = {}
                    for d in deps:
                        if ops[d][0] == 'pe' and e == 'pe' and not ops[d][3] and not dma:
                            continue
                        k, v = ev[d]
                        if need.get(k, 0) < v:
                            need[k] = v
                    pw = pre_wait.get(oid)
                    if pw and need.get(pw[0], 0) < pw[1]:
                        need[pw[0]] = pw[1]
                    for k, v in need.items():
                        if kn.get(k, 0) >= v:
                            continue
                        kn[k] = v
                        st.append(('wait', k, v))
                    k, v = ev[oid]
                    st.append(('op', fn, k, 16 if dma else 1, cost, v))
            for e in ENGS:
                kn = self.known[e]
                for k, v in self.cnt.items():
                    if kn.get(k, 0) < v:
                        kn[k] = v
                        self.streams[e].append(('wait', k, v))


class Arena:
    def __init__(self, ap_f32, nwords):
        self.ap = ap_f32
        self.n = nwords
        self.off = 0
        self.peak = 0

    def mark(self):
        return self.off

    def release(self, m):
        self.off = m

    def alloc(self, shape, dt):
        ne = 1
        for s_ in shape:
            ne *= s_
        words = ne if dt == F32 else (ne + 1) // 2
        words = (words + 3) // 4 * 4
        a = self.off
        self.off += words
        self.peak = max(self.peak, self.off)
        assert self.off <= self.n, f"arena overflow {self.off} > {self.n}"
        v = self.ap[:, a:a + words]
        if dt != F32:
            v = v.bitcast(dt)[:, 0:ne]
        else:
            v = v[:, 0:ne]
        if len(shape) == 2:
            v = v.rearrange("p (a b) -> p a b", a=shape[0], b=shape[1])
        elif len(shape) == 3:
            v = v.rearrange("p (a b c) -> p a b c", a=shape[0], b=shape[1], c=shape[2])
        return v


def bc(ap, shape):
    return ap.to_broadcast(list(shape))


def build_program(cfg):
    nc = bass.Bass("TRN2", target_bir_lowering=False)
    P = Planner()
    NS, NM = cfg.n_ssd, cfg.n_mla
    TP, TS, PAST = cfg.t_p, cfg.t_s, cfg.past

    def din(name, shape, dt=F32):
        return nc.dram_tensor(name, list(shape), dt, kind="ExternalInput").ap()

    def dout(name, shape):
        return nc.dram_tensor(name, list(shape), F32, kind="ExternalOutput").ap()

    def dscr(name, shape):
        return nc.dram_tensor(name, list(shape), BF16, kind="Internal").ap()

    I = {}
    I['x_prompt'] = din('x_prompt', [cfg.nseq_p, TP, D])
    I['x_sample'] = din('x_sample', [cfg.nseq_s, TS, D])
    I['cache_conv'] = din('cache_conv', [NS, cfg.nseq_s, 3, 4096])
    I['state_ssm'] = din('state_ssm', [NS, cfg.nseq_s, 32, 64, 128])
    I['cache_kv_latent'] = din('cache_kv_latent', [NM, cfg.nseq_s, PAST, 256])
    I['cache_k_rope'] = din('cache_k_rope', [NM, cfg.nseq_s, PAST, 32])
    I['p_prompt'] = din('p_prompt', [cfg.depth, cfg.nseq_p, TP, 256])
    I['p_sample'] = din('p_sample', [cfg.depth, cfg.nseq_s, TS, 256])
    I['ln_w'] = din('ln_w', [cfg.depth, D])
    I['ssd_in_w'] = din('ssd_in_w', [NS, D, SSD_IN])
    I['ssd_conv_w'] = din('ssd_conv_w', [NS, 4, 4096])
    I['ssd_conv_b'] = din('ssd_conv_b', [NS, 4096])
    I['ssd_dt_bias'] = din('ssd_dt_bias', [NS, 32])
    I['ssd_A_log'] = din('ssd_A_log', [NS, 32])
    I['ssd_D'] = din('ssd_D', [NS, 32])
    I['ssd_norm_w'] = din('ssd_norm_w', [NS, 2048])
    I['ssd_out_w'] = din('ssd_out_w', [NS, 2048, D])
    I['mla_in_w'] = din('mla_in_w', [NM, D, MLA_IN])
    I['mla_q_a_norm'] = din('mla_q_a_norm', [NM, 384])
    I['mla_q_b_w'] = din('mla_q_b_w', [NM, 384, 1536])
    I['mla_kv_a_norm'] = din('mla_kv_a_norm', [NM, 256])
    I['mla_kv_b_w'] = din('mla_kv_b_w', [NM, 256, 2048])
    I['mla_q_nope_norm'] = din('mla_q_nope_norm', [NM, 64])
    I['mla_q_rope_norm'] = din('mla_q_rope_norm', [NM, 32])
    I['mla_k_nope_norm'] = din('mla_k_nope_norm', [NM, 64])
    I['mla_k_rope_norm'] = din('mla_k_rope_norm', [NM, 32])
    I['mla_out_w'] = din('mla_out_w', [NM, D, D])
    I['ple_up_w'] = din('ple_up_w', [cfg.depth, 256, D])
    I['ple_norm_w'] = din('ple_norm_w', [cfg.depth, D])
    I['ple_gate_w'] = din('ple_gate_w', [cfg.depth, D, D])
    I['c_identf'] = din('c_identf', [128, 2, 128])
    I['c_mats'] = din('c_mats', [128, 8, 128], BF16)
    I['c_rope'] = din('c_rope', [TP + TS, 32])

    O = {}
    O['y_prompt'] = dout('y_prompt', [cfg.nseq_p, TP, D])
    O['y_sample'] = dout('y_sample', [cfg.nseq_s, TS, D])
    O['conv_prompt'] = dout('conv_prompt', [NS, cfg.nseq_p, 3, 4096])
    O['ssm_prompt'] = dout('ssm_prompt', [NS, cfg.nseq_p, 32, 64, 128])
    O['lat_prompt'] = dout('lat_prompt', [NM, cfg.nseq_p, TP, 256])
    O['kr_prompt'] = dout('kr_prompt', [NM, cfg.nseq_p, TP, 32])
    O['conv_sample'] = dout('conv_sample', [NS, cfg.nseq_s, 3, 4096])
    O['ssm_sample'] = dout('ssm_sample', [NS, cfg.nseq_s, 32, 64, 128])
    O['lat_sample'] = dout('lat_sample', [NM, cfg.nseq_s, TS, 256])
    O['kr_sample'] = dout('kr_sample', [NM, cfg.nseq_s, TS, 32])

    S_in_ssd = [dscr(f's_in_ssd{j}', [128, 8, 8, 768]) for j in range(NS)]
    S_dt = [dscr(f's_dt{j}', [128, 8, 32]) for j in range(NS)]
    S_out_ssd = [dscr(f's_out_ssd{j}', [128, 16, D]) for j in range(NS)]
    S_in_mla = [dscr(f's_in_mla{j}', [128, 8, MLA_IN]) for j in range(NM)]
    S_qb = [dscr(f's_qb{j}', [128, 3, 1536]) for j in range(NM)]
    S_kvb = [dscr(f's_kvb{j}', [128, 2, 2048]) for j in range(NM)]
    S_out_mla = [dscr(f's_out_mla{j}', [128, 8, D]) for j in range(NM)]
    S_gate = [dscr(f's_gate{i}', [128, 8, D]) for i in range(cfg.depth)]
    S_up = [dscr(f's_up{i}', [128, 2, D]) for i in range(cfg.depth)]

    ARENA_WORDS = 53200
    ctx = []
    arena_t = nc.sbuf_tensor("arena", [128, ARENA_WORDS], F32)
    arena_ap = arena_t.__enter__()
    ctx.append(arena_t)
    psum_t = []
    PS = []
    for b in range(8):
        t = nc.psum_tensor(f"psb{b}", [128, 512], F32)
        PS.append(t.__enter__())
        ctx.append(t)
    A = Arena(arena_ap, ARENA_WORDS)

    def vcost(eng, out):
        n = out.free_size()
        if eng == 'pool':
            return 250.0 + n / 0.6
        return 180.0 + n / 0.96

    def mm(out, pairs, r, w):
        pairs = list(pairs)

        def fn(e):
            ins = None
            n = len(pairs)
            for i_, (l_, r_) in enumerate(pairs):
                ins = e.matmul(out, lhsT=l_, rhs=r_, start=(i_ == 0), stop=(i_ == n - 1))
            return ins
        c = sum(max(64, r_.free_size()) / 2.2 + 4 for (_, r_) in pairs)
        return P.op('pe', fn, r, w, cost=c)

    def mm_multi(items, r, w):
        items = [(o, list(p)) for o, p in items]

        def fn(e):
            ins = None
            for o, pairs in items:
                n = len(pairs)
                for i_, (l_, r_) in enumerate(pairs):
                    ins = e.matmul(o, lhsT=l_, rhs=r_, start=(i_ == 0), stop=(i_ == n - 1))
            return ins
        c = sum(sum(max(64, r_.free_size()) / 2.2 + 4 for (_, r_) in p) for _, p in items)
        return P.op('pe', fn, r, w, cost=c)

    def tr_multi(items, ident, r, w):
        items = list(items)

        def fn(e):
            ins = None
            for o, i_ in items:
                ins = e.transpose(o, i_, ident)
            return ins
        c = sum((max(64, i_.free_size()) / 2.2 + 4) * (4 if i_.dtype == F32 else 1) for _, i_ in items)
        return P.op('pe', fn, r, w, cost=c)

    def act(out, in_, func, r, w, bias=None, scale=None, accum=None):
        kw = {}
        if bias is not None:
            kw['bias'] = bias
        if scale is not None:
            kw['scale'] = scale
        if accum is not None:
            kw['accum_out'] = accum
        return P.op('act', lambda e: e.activation(out=out, in_=in_, func=func, **kw), r, w, cost=(out.free_size() + 330) / 1.2)

    def tt(eng, out, in0, in1, op, r, w):
        return P.op(eng, lambda e: e.tensor_tensor(out=out, in0=in0, in1=in1, op=op), r, w, cost=vcost(eng, out))

    def ts(eng, out, in0, s1, s2, op0, op1, r, w, accum=None):
        kw = {}
        if op1 is not None:
            kw['op1'] = op1
        if accum is not None:
            kw['accum_out'] = accum
        return P.op(eng, lambda e: e.tensor_scalar(out=out, in0=in0, scalar1=s1, scalar2=s2, op0=op0, **kw), r, w, cost=vcost(eng, out))

    def stt(eng, out, in0, sc, in1, op0, op1, r, w):
        return P.op(eng, lambda e: e.scalar_tensor_tensor(out=out, in0=in0, scalar=sc, in1=in1, op0=op0, op1=op1), r, w, cost=vcost(eng, out))

    def cp(eng, out, in_, r, w):
        if eng == 'act':
            return P.op('act', lambda e: e.copy(out=out, in_=in_), r, w, cost=(out.free_size() + 250) / 1.2)
        return P.op(eng, lambda e: e.tensor_copy(out=out, in_=in_), r, w, cost=vcost(eng, out))

    def recip(out, in_, r, w):
        return P.op('dve', lambda e: e.reciprocal(out=out, in_=in_), r, w, cost=vcost('dve', out))

    def mset(eng, ap, val, r, w):
        return P.op(eng, lambda e: e.memset(ap, val), r, w, cost=vcost(eng, ap))

    def dma(q, out, in_, r, w, slow=False):
        dcost = 2000.0 + out.free_size() * out.partition_size() * (4 if out.dtype == F32 else 2) / 150.0
        if slow:
            return P.op(q, lambda e: e.dma_start(out=out, in_=in_, allow_slow_non_contiguous=True), r, w, dma=True, cost=dcost)
        return P.op(q, lambda e: e.dma_start(out=out, in_=in_), r, w, dma=True, cost=dcost)

    def dma_old(q, out, in_, r, w, slow=False):
        if slow:
            return P.op(q, lambda e: e.dma_start(out=out, in_=in_, allow_slow_non_contiguous=True), r, w, dma=True)
        return P.op(q, lambda e: e.dma_start(out=out, in_=in_), r, w, dma=True)

    psc = {'n': 0}
    rings = {'mm': [0, 1], 'sc': [2, 3, 4], 'acc': [5], 'tr': [6, 7]}
    ringpos = {k: 0 for k in rings}

    def set_rings(cfgd):
        rings.clear()
        rings.update(cfgd)

    def psum(tag):
        lst = rings[tag]
        b = lst[ringpos[tag] % len(lst)]
        ringpos[tag] += 1
        return b

    cf = A.alloc([2, 128], F32)
    identf, TLf = cf[:, 0, :], cf[:, 1, :]
    cm = A.alloc([8, 128], BF16)
    identb, TU, TL, onesb = cm[:, 0, :], cm[:, 1, :], cm[:, 2, :], cm[:, 3, :]
    NEGb = cm[:, 4:8, :]
    TMAX = max(TP, TS)
    hT = A.alloc([8, TMAX], F32)
    gains = A.alloc([128], F32)
    upw = A.alloc([2, D], BF16)
    gatew = [A.alloc([8, 128], BF16) for _ in range(2)]
    gw_n = [0]
    vstage = A.alloc([128], F32)
    dma('sp', cf, I['c_identf'], [], ['identf'])

    def load_vec_cols(dst, src1d, ncol, wtag):
        dma('sp', vstage[0:ncol, :], src1d.rearrange("(c p) -> c p", p=128), [], ['vstage'])
        b = psum('tr')
        tr_multi([(PS[b][:, 0:ncol], vstage[0:ncol, :])], identf[0:ncol, 0:ncol], ['vstage', 'identf'], [f'ps{b}'])
        cp('dve', dst, PS[b][:, 0:ncol], [f'ps{b}'], ['tails%d' % c_i for c_i in range(32)] if wtag == 'tailsall' else [wtag])
    dma('sp', cm, I['c_mats'], [], ['cm'])
    CONST = ['identf', 'cm']
    persist_mark = A.mark()

    def prologue():
        P.curlabel = 'prologue'
        m0 = A.mark()
        stg = [A.alloc([SSD_IN], F32) for _ in range(2)]
        stb = [A.alloc([SSD_IN], BF16) for _ in range(2)]
        gcol = {}
        col = [0]

        def load_gain(name, vec_ap, nkc):
            c0 = col[0]
            col[0] += nkc
            load_vec_cols(gains[:, c0:c0 + nkc], vec_ap, nkc, 'gains')
            gcol[name] = c0
        for i in range(cfg.depth):
            load_gain(('ln', i), I['ln_w'][i], 8)
            load_gain(('pn', i), I['ple_norm_w'][i], 8)
        for j in range(NS):
            load_gain(('sn', j), I['ssd_norm_w'][j], 16)
        for j in range(NM):
            load_gain(('qa', j), I['mla_q_a_norm'][j], 3)
        assert col[0] <= 128
        cnt = [0]

        def do_rows(src_rows, ncols, gain_col, stores):
            k = cnt[0] % 2
            cnt[0] += 1
            dma('sp', stg[k][:, 0:ncols], src_rows, [], [f'stg{k}'])
            eng = 'act' if (cnt[0] % 2 == 0) else 'dve'
            if gain_col is None:
                cp(eng, stb[k][:, 0:ncols], stg[k][:, 0:ncols], [f'stg{k}'], [f'stb{k}'])
            elif eng == 'act':
                act(stb[k][:, 0:ncols], stg[k][:, 0:ncols], AF.Copy, [f'stg{k}', 'gains'], [f'stb{k}'],
                    scale=gains[:, gain_col:gain_col + 1])
            else:
                ts('dve', stb[k][:, 0:ncols], stg[k][:, 0:ncols], gains[:, gain_col:gain_col + 1], None, ALU.mult, None,
                   [f'stg{k}', 'gains'], [f'stb{k}'])
            for dst, srcv, wname in stores:
                dma('pool', dst, srcv(stb[k]), [f'stb{k}'], [wname])

        for j in range(NS):
            i = 2 * j
            for kc in range(8):
                rows = I['ssd_in_w'][j, kc * 128:(kc + 1) * 128, :]
                dst = S_in_ssd[j]
                stores = [
                    (dst[:, kc, :, 0:256], lambda b: b[:, 2048:4096].rearrange("p (g c) -> p g c", g=8, c=256), f'w_in_ssd{j}'),
                    (dst[:, kc, :, 256:384], lambda b: b[:, 4096:5120].rearrange("p (g c) -> p g c", g=8, c=128), f'w_in_ssd{j}'),
                    (dst[:, kc, :, 384:512], lambda b: b[:, 5120:6144].rearrange("p (g c) -> p g c", g=8, c=128), f'w_in_ssd{j}'),
                    (dst[:, kc, :, 512:768], lambda b: b[:, 0:2048].rearrange("p (g c) -> p g c", g=8, c=256), f'w_in_ssd{j}'),
                    (S_dt[j][:, kc, :], lambda b: b[:, 6144:6176], f'w_dt{j}'),
                ]
                do_rows(rows, SSD_IN, gcol[('ln', i)] + kc, stores)
            for kc in range(16):
                rows = I['ssd_out_w'][j, kc * 128:(kc + 1) * 128, :]
                do_rows(rows, D, gcol[('sn', j)] + kc, [(S_out_ssd[j][:, kc, :], lambda b: b[:, 0:D], f'w_out_ssd{j}')])
        for j in range(NM):
            i = 2 * j + 1
            for kc in range(8):
                rows = I['mla_in_w'][j, kc * 128:(kc + 1) * 128, :]
                do_rows(rows, MLA_IN, gcol[('ln', i)] + kc, [(S_in_mla[j][:, kc, :], lambda b: b[:, 0:MLA_IN], f'w_in_mla{j}')])
            for kc in range(3):
                rows = I['mla_q_b_w'][j, kc * 128:(kc + 1) * 128, :]
                do_rows(rows, 1536, gcol[('qa', j)] + kc, [(S_qb[j][:, kc, :], lambda b: b[:, 0:1536], f'w_qb{j}')])
            for kc in range(2):
                rows = I['mla_kv_b_w'][j, kc * 128:(kc + 1) * 128, :]
                do_rows(rows, 2048, None, [(S_kvb[j][:, kc, :], lambda b: b[:, 0:2048], f'w_kvb{j}')])
            for kc in range(8):
                rows = I['mla_out_w'][j, kc * 128:(kc + 1) * 128, :]
                do_rows(rows, D, None, [(S_out_mla[j][:, kc, :], lambda b: b[:, 0:D], f'w_out_mla{j}')])
        for i in range(cfg.depth):
            for kc in range(8):
                rows = I['ple_gate_w'][i, kc * 128:(kc + 1) * 128, :]
                do_rows(rows, D, gcol[('pn', i)] + kc, [(S_gate[i][:, kc, :], lambda b: b[:, 0:D], f'w_gate{i}')])
            for kc in range(2):
                rows = I['ple_up_w'][i, kc * 128:(kc + 1) * 128, :]
                do_rows(rows, D, None, [(S_up[i][:, kc, :], lambda b: b[:, 0:D], f'w_up{i}')])
        P.barrier()
        A.release(m0)

    if not getattr(cfg, 'skip_pro', False):
        prologue()

    TTG = [cfg.tt]

    def HT(jo, tok0):
        return 'hT%d_%d' % (jo, tok0 // TTG[0])

    def HTA(tok0):
        return [HT(jo, tok0) for jo in range(8)]

    class Seq:
        pass

    def rms_fm(sq, tok0, n, hn_out, tag):
        hs = hT[:, :, tok0:tok0 + n]
        sqt = sq['sqt'][:, :, 0:n]
        act(sqt, hs, AF.Square, HTA(tok0), ['sqt'])
        b = psum('mm')
        pst = PS[b][:, 0:n]
        mm(pst, [(onesb, sqt[:, kc, :]) for kc in range(8)], ['sqt', 'cm'], [f'ps{b}'])
        rs = sq['rstd'][:, 0:n]
        act(rs, pst, AF.Ln, [f'ps{b}'], ['rstd'], bias=EPS, scale=1.0 / D)
        act(rs, rs, AF.Exp, ['rstd'], ['rstd'], scale=-0.5)
        tt('dve', hn_out, hs, bc(rs.unsqueeze(1), [128, 8, n]), ALU.mult, HTA(tok0) + ['rstd'], [tag])

    def ple_tile(sq, i, pin, tok0, n, nt128, cht, hnbuf=None, hntag='hn2', ebufs=None):
        hn = (sq['hn2'] if hnbuf is None else hnbuf)[:, :, 0:n]
        rms_fm(sq, tok0, n, hn, hntag)
        ptm = sq['ptm']
        pT = sq['pT']
        for t_ in range(nt128):
            c0 = t_ * cht
            dma('sp', ptm[0:cht, :], pin[tok0 + c0:tok0 + c0 + cht, :], [], ['ptm'])
            cp('pool', sq['ptmb'][0:cht, :], ptm[0:cht, :], ['ptm'], ['ptmb'])
            b = psum('tr')
            pb = PS[b][:].bitcast(BF16)
            tr_multi([(pb[:, k * 128:k * 128 + cht], sq['ptmb'][0:cht, k * 128:(k + 1) * 128]) for k in range(2)],
                     identb[0:cht, 0:cht], ['ptmb', 'cm'], [f'ps{b}'])
            cp('act', pT[:, :, c0:c0 + cht], pb[:, 0:256].rearrange("p (k c) -> p k c", k=2, c=128)[:, :, 0:cht],
               [f'ps{b}'], ['pT'])
        for jo in range(8):
            gk = gw_n[0] % 2
            gw_n[0] += 1
            dma('sp', gatew[gk], S_gate[i][:, :, jo * 128:(jo + 1) * 128], [f'w_gate{i}'], [f'gatew{gk}'])
            bg = psum('mm')
            mm(PS[bg][:, 0:n], [(gatew[gk][:, kc, :], hn[:, kc, :]) for kc in range(8)],
               [f'gatew{gk}', hntag], [f'ps{bg}'])
            bu = psum('mm')
            mm(PS[bu][:, 0:n], [(upw[:, kc, jo * 128:(jo + 1) * 128], pT[:, kc, 0:n]) for kc in range(2)],
               ['upw', 'pT'], [f'ps{bu}'])
            if ebufs is None:
                ebufs = [(sq['ple_e'], 'ple_e'), (sq['ple_e2'], 'ple_e2')]
            eb, et = ebufs[jo % len(ebufs)]
            e_ = eb[:, 0:n]
            act(e_, PS[bg][:, 0:n], AF.Exp, [f'ps{bg}'], [et], scale=-1.0)
            act(e_, e_, AF.Ln, [et], [et], bias=1.0, scale=1.0)
            act(e_, e_, AF.Exp, [et], [et], scale=-1.0)
            tt('dve', e_, PS[bu][:, 0:n], e_, ALU.mult, [f'ps{bu}', et], [et])
            tt('dve', hT[:, jo, tok0:tok0 + n], hT[:, jo, tok0:tok0 + n], e_, ALU.add, [HT(jo, tok0), et], [HT(jo, tok0)])

    def alloc_common(T, TT, need_hn2=True):
        sq = {}
        sq['sqt'] = A.alloc([8, TT], BF16)
        sq['rstd'] = A.alloc([TT], F32)
        if need_hn2:
            sq['hn2'] = A.alloc([8, TT], BF16)
        sq['ptm'] = A.alloc([256], F32)
        sq['ptmb'] = A.alloc([256], BF16)
        sq['pT'] = A.alloc([2, TT], BF16)
        if need_hn2:
            sq['ple_e'] = A.alloc([TT], F32)
            sq['ple_e2'] = A.alloc([TT], F32)
        return sq

    def load_ple_w(i):
        dma('sp', upw, S_up[i], [f'w_up{i}'], ['upw'])

    def ssd_layer(kind, sidx, i, T, TT, CH):
        j = i // 2
        P.curlabel = 'ssd_' + kind
        set_rings({'mm': [0, 1], 'sc': [2, 3, 4], 'acc': [5], 'tr': [6, 7]})
        LVL = getattr(cfg, 'lvl', 9)
        TAP2ENG = getattr(cfg, 'tap2eng', 'dve')
        NBUF = getattr(cfg, 'nbuf', 2)
        ccnt = [0]
        m0 = A.mark()
        sq = alloc_common(T, TT, need_hn2=False)
        NCH = TT // CH
        NT = T // TT
        hn = A.alloc([8, TT], BF16)
        w_dt = A.alloc([8, 32], BF16)
        w_out = [A.alloc([16, 128], BF16) for _ in range(2)]
        wo_n = [0]
        raw = A.alloc([2, TT + 3], F32)
        cacc_l = [A.alloc([TT], F32) for _ in range(2)]
        cexp_l = [A.alloc([TT], F32) for _ in range(2)]
        xs_l = [A.alloc([4, TT], BF16) for _ in range(2)]
        gcnt = [0]
        scnt = [0]
        ynT = A.alloc([16, TT], BF16)
        St = A.alloc([2048], F32)
        Sbf = A.alloc([2048], BF16)
        tails = A.alloc([3, 32], F32)
        cw = A.alloc([4, 32], F32)
        cb = A.alloc([32], F32)
        dtb = A.alloc([32], F32)
        Abc = A.alloc([32], F32)
        Dbc = A.alloc([32], F32)
        dtx = A.alloc([NCH, 32], F32)
        dta = A.alloc([NCH, 32], F32)
        dtl = A.alloc([NCH, 32], F32)
        dt_ = A.alloc([NCH, 32], F32)
        dtA = A.alloc([NCH, 32], F32)
        dtAh = A.alloc([NCH, 32], BF16)
        dtAl = A.alloc([NCH, 32], BF16)
        Acs = A.alloc([NCH, 32], F32)
        eA = A.alloc([NCH, 32], F32)
        dec = A.alloc([NCH, 32], F32)
        ddt = A.alloc([NCH, 32], F32)
        cdk = A.alloc([NCH, 32], F32)
        xB_l = [A.alloc([384], BF16) for _ in range(NBUF)]
        xdt_l = [A.alloc([4, 64], BF16) for _ in range(NBUF)]
        xdd_l = [A.alloc([4, 64], BF16) for _ in range(NBUF)]
        Rh_l = [A.alloc([4, CH], F32) for _ in range(NBUF)]
        Lm_l = [A.alloc([4, CH], BF16) for _ in range(NBUF)]
        MT_l = [A.alloc([4, CH], BF16) for _ in range(NBUF)]
        t1_l = [A.alloc([4, 64], F32) for _ in range(NBUF)]
        t2_l = [A.alloc([4, 64], F32) for _ in range(NBUF)]
        yv_l = [A.alloc([256], F32) for _ in range(NBUF)]
        ze_l = [A.alloc([256], F32) for _ in range(NBUF)]
        ss_l = [A.alloc([2], F32) for _ in range(NBUF)]
        yn_l = [A.alloc([256], BF16) for _ in range(NBUF)]
        stg = A.alloc([16, 128], F32) if kind == 's' else None
        tlT = A.alloc([128], F32)
        NEG32 = None
        if CH != 128:
            NEG32 = A.alloc([4 * CH], BF16)
            cp('dve', NEG32[0:CH].rearrange('p (r c) -> p r c', r=4, c=CH), NEGb[0:CH, :, 0:CH], ['cm'], ['neg32'])

        m_w = A.mark()
        w_in = [A.alloc([8, 768], BF16) for _ in range(2)]

        for kk in range(4):
            load_vec_cols(cw[:, kk, :], I['ssd_conv_w'][j, kk], 32, 'cw')
        load_vec_cols(cb, I['ssd_conv_b'][j], 32, 'cb')
        dma('sp', dtb, I['ssd_dt_bias'][j].partition_broadcast(128), [], ['dtb'])
        dma('sp', Abc, I['ssd_A_log'][j].partition_broadcast(128), [], ['Abc'])
        dma('sp', Dbc, I['ssd_D'][j].partition_broadcast(128), [], ['Dbc'])
        act(Abc, Abc, AF.Exp, ['Abc'], ['Abc'])
        ts('dve', Abc, Abc, -1.0, None, ALU.mult, None, ['Abc'], ['Abc'])
        dma('sp', w_dt, S_dt[j], [f'w_dt{j}'], ['w_dt'])
        load_ple_w(i)
        if kind == 'p':
            mset('pool', St, 0.0, [], ['St'])
            mset('pool', tails, 0.0, [], ['tails%d' % c_i for c_i in range(32)])
        else:
            src = I['state_ssm'][j, sidx].rearrange("h p n -> (h p) n").rearrange("(c r) n -> r c n", r=128)
            dma('sp', stg, src, [], ['stg'])
            for q4 in range(4):
                b = psum('tr')
                tr_multi([(PS[b][:, k * 128:(k + 1) * 128], stg[:, q4 * 4 + k, :]) for k in range(4)], identf, ['stg', 'identf'], [f'ps{b}'])
                cp('dve', St[:, q4 * 512:(q4 + 1) * 512], PS[b][:, 0:512], [f'ps{b}'], ['St'])
            for kk in range(3):
                load_vec_cols(tails[:, kk, :], I['cache_conv'][j, sidx, kk], 32, 'tailsall')
        cp('act', Sbf, St, ['St'], ['Sbf'])

        nload = [0]

        def load_w_in(g):
            k = nload[0] % 2
            nload[0] += 1
            dma('sp', w_in[k], S_in_ssd[j][:, :, g, :], [f'w_in_ssd{j}'], [f'w_in{k}'])
            return k

        pin = (I['p_prompt'] if kind == 'p' else I['p_sample'])[i, sidx]
        pending = load_w_in(0)
        for ti in range(NT):
            if LVL in (-3, -1):
                break
            tok0 = ti * TT
            rms_fm(sq, tok0, TT, hn, 'hn')
            DTL = getattr(cfg, 'dtl', 9)
            b = psum('sc')
            if DTL >= 1:
              mm_multi([(PS[b][0:CH, ch * 32:(ch + 1) * 32],
                         [(hn[:, kc, ch * CH:(ch + 1) * CH], w_dt[:, kc, :]) for kc in range(8)]) for ch in range(NCH)],
                       ['hn', 'w_dt'], [f'ps{b}'])
            c_ = slice(0, CH)
            pv = PS[b][0:CH, 0:NCH * 32].rearrange("p (c h) -> p c h", c=NCH, h=32)
            tt('dve', dtx[c_], pv, bc(dtb[c_].unsqueeze(1), [CH, NCH, 32]), ALU.add, [f'ps{b}', 'dtb'], ['dtx'])
            act(dta[c_], dtx[c_], AF.Abs, ['dtx'], ['dta'])
            act(dta[c_], dta[c_], AF.Exp, ['dta'], ['dta'], scale=-1.0)
            act(dtl[c_], dta[c_], AF.Ln, ['dta'], ['dtl'], bias=1.0, scale=1.0)
            ts('dve', dtx[c_], dtx[c_], 0.0, None, ALU.max, None, ['dtx'], ['dtx'])
            tt('dve', dt_[c_], dtx[c_], dtl[c_], ALU.add, ['dtx', 'dtl'], ['dt'])
            tt('dve', dtA[c_], dt_[c_], bc(Abc[c_].unsqueeze(1), [CH, NCH, 32]), ALU.mult, ['dt', 'Abc'], ['dtA'])
            cp('dve', dtAh[c_], dtA[c_], ['dtA'], ['dtAh'])
            tt('dve', dtAl[c_], dtA[c_], dtAh[c_], ALU.subtract, ['dtA', 'dtAh'], ['dtAl'])
            b1 = psum('sc')
            mm_multi([(PS[b1][0:CH, ch * 32:(ch + 1) * 32],
                       [(TU[0:CH, 0:CH], dtAh[c_, ch, :]), (TU[0:CH, 0:CH], dtAl[c_, ch, :])]) for ch in range(NCH)],
                     ['dtAh', 'dtAl', 'cm'], [f'ps{b1}'])
            b2 = psum('sc')
            mm_multi([(PS[b2][:, ch * 32:(ch + 1) * 32],
                       [(onesb[0:CH, :], dtAh[c_, ch, :]), (onesb[0:CH, :], dtAl[c_, ch, :])]) for ch in range(NCH)],
                     ['dtAh', 'dtAl', 'cm'], [f'ps{b2}'])
            pa = PS[b1][0:CH, 0:NCH * 32].rearrange("p (c h) -> p c h", c=NCH, h=32)
            pt_ = PS[b2][:, 0:NCH * 32].rearrange("p (c h) -> p c h", c=NCH, h=32)
            cp('dve', Acs[c_], pa, [f'ps{b1}'], ['Acs'])
            act(eA[c_], pa, AF.Exp, [f'ps{b1}'], ['eA'])
            tt('dve', dec[c_], pt_[0:CH], Acs[c_], ALU.subtract, [f'ps{b2}', 'Acs'], ['dec'])
            act(dec[c_], dec[c_], AF.Exp, ['dec'], ['dec'])
            tt('dve', ddt[c_], dec[c_], dt_[c_], ALU.mult, ['dec', 'dt'], ['ddt'])
            act(cdk, pt_, AF.Exp, [f'ps{b2}'], ['cdk'])

            for g in range(8):
                if LVL < 1:
                    break
                k = pending
                wv = w_in[k]
                xp = gcnt[0] % 2
                gcnt[0] += 1
                xs = xs_l[xp]
                XS = 'xs%d' % xp
                if not (ti == NT - 1 and g == 7):
                    pending = load_w_in((g + 1) % 8)
                for s_ in range(4):
                    cidx = (2 * g + s_) if s_ < 2 else (16 + g if s_ == 2 else 24 + g)
                    RW = 'raw%d' % (s_ % 2)
                    cpp = scnt[0] % 2
                    scnt[0] += 1
                    cacc = cacc_l[cpp]
                    cexp = cexp_l[cpp]
                    CA = 'cacc%d' % cpp
                    CE = 'cexp%d' % cpp
                    b = psum('mm')
                    mm(PS[b][:, 0:TT], [(wv[:, kc, s_ * 128:(s_ + 1) * 128], hn[:, kc, :]) for kc in range(8)],
                       [f'w_in{k}', 'hn'], [f'ps{b}'])
                    cp('pool', raw[:, s_ % 2, 0:3], tails[:, :, cidx], ['tails%d' % cidx], [RW])
                    cp('act', raw[:, s_ % 2, 3:3 + TT], PS[b][:, 0:TT], [f'ps{b}'], [RW])
                    cp('pool', tails[:, :, cidx], raw[:, s_ % 2, TT:TT + 3], [RW], ['tails%d' % cidx])
                    act(cacc, PS[b][:, 0:TT], AF.Identity, [f'ps{b}', 'cw', 'cb'], [CA], bias=cb[:, cidx:cidx + 1], scale=cw[:, 3, cidx:cidx + 1])
                    for kk, eng_ in ((0, 'dve'), (1, 'dve'), (2, TAP2ENG)):
                        stt(eng_, cacc, raw[:, s_ % 2, kk:kk + TT], cw[:, kk, cidx:cidx + 1], cacc, ALU.mult, ALU.add,
                            [RW, 'cw', CA], [CA])
                    act(cexp, cacc, AF.Exp, [CA], [CE], scale=-1.0)
                    act(cexp, cexp, AF.Ln, [CE], [CE], bias=1.0, scale=1.0)
                    act(cexp, cexp, AF.Exp, [CE], [CE], scale=-1.0)
                    tt('dve', xs[:, s_, :], cacc, cexp, ALU.mult, [CA, CE], [XS])
                hs = slice(4 * g, 4 * g + 4)
                for ch in range(NCH):
                    if LVL < 2:
                        break
                    pp = ccnt[0] % NBUF
                    ccnt[0] += 1
                    N_ = lambda s_, pp=pp: s_ + '_%d' % pp
                    xB = xB_l[pp]
                    xdt = xdt_l[pp]
                    xdd = xdd_l[pp]
                    Rh = Rh_l[pp]
                    Lm = Lm_l[pp]
                    MT = MT_l[pp]
                    t1 = t1_l[pp]
                    t2 = t2_l[pp]
                    yv = yv_l[pp]
                    ze = ze_l[pp]
                    ss = ss_l[pp]
                    yn = yn_l[pp]
                    c0 = ch * CH
                    bz = psum('mm')
                    mm(PS[bz][0:CH, 0:256], [(hn[:, kc, c0:c0 + CH], wv[:, kc, 512:768]) for kc in range(8)],
                       ['hn', f'w_in{k}'], [f'ps{bz}'])
                    btr = psum('tr')
                    pb = PS[btr][:].bitcast(BF16)
                    tr_multi([(pb[0:CH, s_ * 128:(s_ + 1) * 128], xs[:, s_, c0:c0 + CH]) for s_ in range(3)],
                             identb, [XS, 'cm'], [f'ps{btr}'])
                    cp('act', xB[c_], pb[0:CH, 0:384], [f'ps{btr}'], [N_('xB')])
                    x3 = xB[c_, 0:256].rearrange("p (r d) -> p r d", r=4, d=64)
                    tt('pool', xdt[c_], x3, bc(dt_[c_, ch, hs].unsqueeze(2), [CH, 4, 64]), ALU.mult, [N_('xB'), 'dt'], [N_('xdt')])
                    tt('pool', xdd[c_], x3, bc(ddt[c_, ch, hs].unsqueeze(2), [CH, 4, 64]), ALU.mult, [N_('xB'), 'ddt'], [N_('xdd')])
                    tt('dve', Rh[c_], bc(dtA[c_, ch, hs].unsqueeze(2), [CH, 4, CH]), bc(TU[0:CH, 0:CH].unsqueeze(1), [CH, 4, CH]),
                       ALU.mult, ['dtA', 'cm'], [N_('Rh')])
                    if LVL < 3:
                        continue
                    bs = psum('sc')
                    mm(PS[bs][0:CH, 0:4 * CH], [(TLf[0:CH, 0:CH], Rh[c_].rearrange("p r c -> p (r c)")),
                                                (identb[0:CH, 0:CH], NEGb[0:CH, :, 0:CH] if CH == 128 else NEG32[0:CH])],
                       [N_('Rh'), 'cm', 'identf'], [f'ps{bs}'])
                    act(Lm[c_], PS[bs][0:CH, 0:4 * CH].rearrange("p (r c) -> p r c", r=4, c=CH), AF.Exp, [f'ps{bs}'], [N_('Lm')])
                    bcb = psum('sc')
                    mm(PS[bcb][0:CH, 0:CH], [(xs[:, 2, c0:c0 + CH], xs[:, 3, c0:c0 + CH])], [XS], [f'ps{bcb}'])
                    tt('dve', MT[c_], Lm[c_], bc(PS[bcb][0:CH, 0:CH].unsqueeze(1), [CH, 4, CH]), ALU.mult, [N_('Lm'), f'ps{bcb}'], [N_('MT')])
                    by = psum('acc')
                    mm_multi([(PS[by][0:CH, r_ * 64:(r_ + 1) * 64], [(MT[c_, r_, :], xdt[c_, r_, :])]) for r_ in range(4)]
                             + [(PS[by][0:CH, 256:512], [(xs[:, 3, c0:c0 + CH], Sbf[:, g * 256:(g + 1) * 256])])],
                             [N_('MT'), N_('xdt'), XS, 'Sbf'], [f'ps{by}'])
                    yo3 = PS[by][0:CH, 256:512].rearrange("p (r d) -> p r d", r=4, d=64)
                    tt('dve', t1[c_], yo3, bc(eA[c_, ch, hs].unsqueeze(2), [CH, 4, 64]), ALU.mult, [f'ps{by}', 'eA'], [N_('t1')])
                    tt('pool', t2[c_], x3, bc(Dbc[c_, hs].unsqueeze(2), [CH, 4, 64]), ALU.mult, [N_('xB'), 'Dbc'], [N_('t2')])
                    tt('pool', t1[c_], t1[c_], t2[c_], ALU.add, [N_('t1'), N_('t2')], [N_('t1')])
                    tt('dve', yv[c_], PS[by][0:CH, 0:256], t1[c_].rearrange("p r d -> p (r d)"), ALU.add, [f'ps{by}', N_('t1')], [N_('yv')])
                    if LVL < 4:
                        continue
                    act(ze[c_], PS[bz][0:CH, 0:256], AF.Exp, [f'ps{bz}'], [N_('ze')], scale=-1.0)
                    act(ze[c_], ze[c_], AF.Ln, [N_('ze')], [N_('ze')], bias=1.0, scale=1.0)
                    act(ze[c_], ze[c_], AF.Exp, [N_('ze')], [N_('ze')], scale=-1.0)
                    tt('dve', ze[c_], PS[bz][0:CH, 0:256], ze[c_], ALU.mult, [f'ps{bz}', N_('ze')], [N_('ze')])
                    tt('dve', yv[c_], yv[c_], ze[c_], ALU.mult, [N_('yv'), N_('ze')], [N_('yv')])
                    act(ze[c_], yv[c_], AF.Square, [N_('yv')], [N_('ze'), N_('ss')], accum=ss[c_, 0:1])
                    act(ss[c_, 1:2], ss[c_, 0:1], AF.Ln, [N_('ss')], [N_('ss')], bias=EPS, scale=1.0 / 256)
                    act(ss[c_, 1:2], ss[c_, 1:2], AF.Exp, [N_('ss')], [N_('ss')], scale=-0.5)
                    ts('dve', yn[c_], yv[c_], ss[c_, 1:2], None, ALU.mult, None, [N_('yv'), N_('ss')], [N_('yn')])
                    bt2 = psum('tr')
                    pb2 = PS[bt2][:].bitcast(BF16)
                    tr_multi([(pb2[:, k2 * 128:k2 * 128 + CH], yn[c_, k2 * 128:(k2 + 1) * 128]) for k2 in range(2)],
                             identb[0:CH, 0:CH], [N_('yn'), 'cm'], [f'ps{bt2}'])
                    cp('act', ynT[:, 2 * g:2 * g + 2, c0:c0 + CH],
                       pb2[:, 0:256].rearrange("p (k c) -> p k c", k=2, c=128)[:, :, 0:CH], [f'ps{bt2}'], ['ynT'])
                    if LVL < 5:
                        continue
                    bst = psum('sc')
                    mm(PS[bst][:, 0:256], [(xB[c_, 256:384], xdd[c_].rearrange("p r d -> p (r d)"))], [N_('xB'), N_('xdd')], [f'ps{bst}'])
                    Sg = St[:, g * 256:(g + 1) * 256]
                    tt('pool', Sg.rearrange("p (r d) -> p r d", r=4, d=64), Sg.rearrange("p (r d) -> p r d", r=4, d=64),
                       bc(cdk[:, ch, hs].unsqueeze(2), [128, 4, 64]), ALU.mult, ['St', 'cdk'], ['St'])
                    tt('dve', Sg, Sg, PS[bst][:, 0:256], ALU.add, ['St', f'ps{bst}'], ['St'])
                    cp('pool', Sbf[:, g * 256:(g + 1) * 256], Sg, ['St'], ['Sbf'])
            set_rings({'mm': [0, 1, 2, 3, 4, 5], 'sc': [5], 'acc': [5], 'tr': [6, 7]})
            for jo in range(8):
                wk = wo_n[0] % 2
                wo_n[0] += 1
                dma('sp', w_out[wk], S_out_ssd[j][:, :, jo * 128:(jo + 1) * 128], [f'w_out_ssd{j}'], [f'w_out{wk}'])
                b = psum('mm')
                mm(PS[b][:, 0:TT], [(w_out[wk][:, kc, :], ynT[:, kc, :]) for kc in range(16)],
                   [f'w_out{wk}', 'ynT'], [f'ps{b}'])
                tt('dve', hT[:, jo, tok0:tok0 + TT], hT[:, jo, tok0:tok0 + TT], PS[b][:, 0:TT], ALU.add, [HT(jo, tok0), f'ps{b}'], [HT(jo, tok0)])
            ple_tile(sq, i, pin, tok0, TT, NCH, CH, hnbuf=hn, hntag='hn', ebufs=[(cexp_l[0], 'cexp0'), (cexp_l[1], 'cexp1')])
            set_rings({'mm': [0, 1], 'sc': [2, 3, 4], 'acc': [5], 'tr': [6, 7]})
        if LVL in (-3, -2):
            P.barrier()
            A.release(m0)
            return
        if kind == 'p':
            P.barrier()
            A.release(m_w)
            stg = A.alloc([16, 128], F32)
        oc = (O['conv_prompt'] if kind == 'p' else O['conv_sample'])[j, sidx]
        b = psum('tr')
        tr_multi([(PS[b][0:96, 0:128], tails.rearrange("p k c -> p (k c)"))], identf, ['tails%d' % c_i for c_i in range(32)] + ['identf'], [f'ps{b}'])
        cp('dve', tlT[0:96], PS[b][0:96, 0:128], [f'ps{b}'], ['tlT'])
        dma('pool', oc.rearrange("k (c p) -> (k c) p", p=128), tlT[0:96, :], ['tlT'], [])
        osm = (O['ssm_prompt'] if kind == 'p' else O['ssm_sample'])[j, sidx].rearrange("h p n -> (h p) n").rearrange("(c r) n -> r c n", r=128)
        for q4 in range(4):
            b = psum('tr')
            tr_multi([(PS[b][:, k2 * 128:(k2 + 1) * 128], St[:, (q4 * 4 + k2) * 128:(q4 * 4 + k2 + 1) * 128]) for k2 in range(4)],
                     identf, ['St', 'identf'], [f'ps{b}'])
            cp('dve', stg[:, q4 * 4:(q4 + 1) * 4, :], PS[b][:, 0:512].rearrange("p (k n) -> p k n", k=4, n=128), [f'ps{b}'], ['stg'])
        dma('pool', osm, stg, ['stg'], [])
        P.barrier()
        A.release(m0)

    def mla_layer(kind, sidx, i, T, TT, CH):
        j = i // 2
        P.curlabel = 'mla_p0_' + kind
        set_rings({'mm': [0, 1, 2], 'sc': [3, 4], 'acc': [5], 'tr': [6, 7]})
        m0 = A.mark()
        past = PAST if kind == 's' else 0
        TK = past + T
        NT = T // TT
        NCT = TT // CH
        nkt_past = past // 128
        nkt_new = T // CH
        NKT = nkt_past + nkt_new
        sg = A.alloc([8, T], BF16)
        qanT = A.alloc([3, T], BF16)
        latT = A.alloc([2, TK], BF16)
        krall = A.alloc([NKT, 32], BF16)
        kvg = A.alloc([256], F32)
        qng = A.alloc([64], F32)
        qrg = A.alloc([32], F32)
        kng = A.alloc([64], F32)
        krg = A.alloc([32], F32)
        rope = A.alloc([32], F32)
        junk = A.alloc([512], F32)
        ss = A.alloc([40], F32)
        mP0 = A.mark()
        sq = alloc_common(T, TT, need_hn2=False)
        hn = A.alloc([8, TT], BF16)
        w_in = A.alloc([8, MLA_IN], BF16)
        qab = A.alloc([384], BF16)
        kva = A.alloc([288], F32)
        lat = A.alloc([256], F32)
        latb = A.alloc([256], BF16)
        krn = A.alloc([32], F32)
        kro = A.alloc([32], F32)
        ge = A.alloc([TT], F32)
        pl = A.alloc([256], F32)
        pr = A.alloc([32], F32)

        dma('sp', w_in, S_in_mla[j], [f'w_in_mla{j}'], ['w_in'])
        dma('sp', kvg, I['mla_kv_a_norm'][j].partition_broadcast(128), [], ['kvg'])
        dma('sp', qng, I['mla_q_nope_norm'][j].partition_broadcast(128), [], ['qng'])
        dma('sp', qrg, I['mla_q_rope_norm'][j].partition_broadcast(128), [], ['qrg'])
        dma('sp', kng, I['mla_k_nope_norm'][j].partition_broadcast(128), [], ['kng'])
        dma('sp', krg, I['mla_k_rope_norm'][j].partition_broadcast(128), [], ['krg'])
        load_ple_w(i)
        olat = (O['lat_prompt'] if kind == 'p' else O['lat_sample'])[j, sidx]
        okr = (O['kr_prompt'] if kind == 'p' else O['kr_sample'])[j, sidx]
        rope_row0 = 0 if kind == 'p' else TP

        def rmsn(src, n, width, scale_col, tagr):
            c_ = slice(0, n)
            act(junk[c_, 0:width], src, AF.Square, tagr, ['junk', 'ss'], accum=ss[c_, scale_col:scale_col + 1])
            act(ss[c_, scale_col:scale_col + 1], ss[c_, scale_col:scale_col + 1], AF.Ln, ['ss'], ['ss'], bias=EPS, scale=1.0 / width)
            act(ss[c_, scale_col:scale_col + 1], ss[c_, scale_col:scale_col + 1], AF.Exp, ['ss'], ['ss'], scale=-0.5)

        def do_rope(dst, srcn, n, nh, tag_r, tag_w):
            c_ = slice(0, n)
            cos = bc(rope[c_, 0:16].unsqueeze(1), [n, nh, 16])
            sin = bc(rope[c_, 16:32].unsqueeze(1), [n, nh, 16])
            x1 = srcn[:, :, 0:16]
            x2 = srcn[:, :, 16:32]
            tA = junk[c_, 0:nh * 16].rearrange("p (h d) -> p h d", h=nh, d=16)
            tB = junk[c_, 256:256 + nh * 16].rearrange("p (h d) -> p h d", h=nh, d=16)
            tt('dve', tA, x1, cos, ALU.mult, tag_r + ['rope'], ['junk'])
            tt('dve', tB, x2, sin, ALU.mult, tag_r + ['rope'], ['junk'])
            tt('dve', dst[:, :, 0:16], tA, tB, ALU.subtract, ['junk'], tag_w)
            tt('dve', tA, x1, sin, ALU.mult, tag_r + ['rope'], ['junk'])
            tt('dve', tB, x2, cos, ALU.mult, tag_r + ['rope'], ['junk'])
            tt('dve', dst[:, :, 16:32], tA, tB, ALU.add, ['junk'], tag_w)

        if past:
            for kt in range(nkt_past):
                dma('sp', pl, I['cache_kv_latent'][j, sidx, kt * 128:(kt + 1) * 128, :], [], ['pl'])
                cp('pool', latb, pl, ['pl'], ['latb'])
                b = psum('tr')
                pb = PS[b][:].bitcast(BF16)
                tr_multi([(pb[:, k * 128:(k + 1) * 128], latb[:, k * 128:(k + 1) * 128]) for k in range(2)], identb, ['latb', 'cm'], [f'ps{b}'])
                cp('act', latT[:, :, kt * 128:(kt + 1) * 128], pb[:, 0:256].rearrange("p (k c) -> p k c", k=2, c=128), [f'ps{b}'], ['latT'])
                dma('sp', pr, I['cache_k_rope'][j, sidx, kt * 128:(kt + 1) * 128, :], [], ['pr'])
                cp('pool', krall[:, kt, :], pr, ['pr'], ['krall'])

        for ti in range(NT):
            tok0 = ti * TT
            rms_fm(sq, tok0, TT, hn, 'hn')
            for jo in range(8):
                b = psum('mm')
                mm(PS[b][:, 0:TT], [(w_in[:, kc, 672 + jo * 128:672 + (jo + 1) * 128], hn[:, kc, :]) for kc in range(8)],
                   ['w_in', 'hn'], [f'ps{b}'])
                act(ge[:, 0:TT], PS[b][:, 0:TT], AF.Exp, [f'ps{b}'], ['ge'], scale=-1.0)
                act(ge[:, 0:TT], ge[:, 0:TT], AF.Ln, ['ge'], ['ge'], bias=1.0, scale=1.0)
                act(ge[:, 0:TT], ge[:, 0:TT], AF.Exp, ['ge'], ['ge'], scale=-1.0)
                tt('dve', sg[:, jo, tok0:tok0 + TT], PS[b][:, 0:TT], ge[:, 0:TT], ALU.mult, [f'ps{b}', 'ge'], ['sg'])
            for ct in range(NCT):
                c0 = ct * CH
                t0 = tok0 + c0
                c_ = slice(0, CH)
                kt_idx = nkt_past + (t0 // CH)
                dma('sp', rope[c_], I['c_rope'][rope_row0 + t0:rope_row0 + t0 + CH, :], [], ['rope'])
                b1 = psum('mm')
                mm(PS[b1][0:CH, 0:384], [(hn[:, kc, c0:c0 + CH], w_in[:, kc, 0:384]) for kc in range(8)], ['hn', 'w_in'], [f'ps{b1}'])
                b2 = psum('mm')
                mm(PS[b2][0:CH, 0:288], [(hn[:, kc, c0:c0 + CH], w_in[:, kc, 384:672]) for kc in range(8)], ['hn', 'w_in'], [f'ps{b2}'])
                rmsn(PS[b1][0:CH, 0:384], CH, 384, 0, [f'ps{b1}'])
                ts('dve', qab[c_], PS[b1][0:CH, 0:384], ss[c_, 0:1], None, ALU.mult, None, [f'ps{b1}', 'ss'], ['qab'])
                bt = psum('tr')
                pb = PS[bt][:].bitcast(BF16)
                tr_multi([(pb[:, k * 128:k * 128 + CH], qab[c_, k * 128:(k + 1) * 128]) for k in range(3)], identb[0:CH, 0:CH],
                         ['qab', 'cm'], [f'ps{bt}'])
                cp('act', qanT[:, :, t0:t0 + CH], pb[:, 0:384].rearrange("p (k c) -> p k c", k=3, c=128)[:, :, 0:CH], [f'ps{bt}'], ['qanT'])
                cp('act', kva[c_], PS[b2][0:CH, 0:288], [f'ps{b2}'], ['kva'])
                rmsn(kva[c_, 0:256], CH, 256, 1, ['kva'])
                stt('dve', lat[c_], kva[c_, 0:256], ss[c_, 1:2], kvg[c_], ALU.mult, ALU.mult, ['kva', 'ss', 'kvg'], ['lat'])
                dma('pool', olat[t0:t0 + CH, :], lat[c_], ['lat'], [])
                cp('pool', latb[c_], lat[c_], ['lat'], ['latb'])
                bt2 = psum('tr')
                pb2 = PS[bt2][:].bitcast(BF16)
                tr_multi([(pb2[:, k * 128:k * 128 + CH], latb[c_, k * 128:(k + 1) * 128]) for k in range(2)], identb[0:CH, 0:CH],
                         ['latb', 'cm'], [f'ps{bt2}'])
                cp('act', latT[:, :, past + t0:past + t0 + CH], pb2[:, 0:256].rearrange("p (k c) -> p k c", k=2, c=128)[:, :, 0:CH],
                   [f'ps{bt2}'], ['latT'])
                rmsn(kva[c_, 256:288], CH, 32, 2, ['kva'])
                stt('dve', krn[c_], kva[c_, 256:288], ss[c_, 2:3], krg[c_], ALU.mult, ALU.mult, ['kva', 'ss', 'krg'], ['krn'])
                do_rope(kro[c_].unsqueeze(1), krn[c_].unsqueeze(1), CH, 1, ['krn'], ['kro'])
                dma('pool', okr[t0:t0 + CH, :], kro[c_], ['kro'], [])
                cp('pool', krall[c_, kt_idx, :], kro[c_], ['kro'], ['krall'])
        P.barrier()
        P.curlabel = 'mla_grp_' + kind
        set_rings({'mm': [0, 1], 'sc': [2, 3, 4], 'acc': [5, 6], 'tr': [7]})
        A.release(mP0)
        G = 4
        m1 = A.mark()
        wqb = [A.alloc([3, G * 96], BF16) for _ in range(2)]
        wkvb = [A.alloc([2, G * 128], BF16) for _ in range(2)]
        KT = A.alloc([G, TK], BF16)
        vfull = A.alloc([NKT, G, 128], BF16)
        mcnt = [0]
        qcnt = [0]
        QT_l = [A.alloc([G, TT], BF16) for _ in range(2)]
        kvs_l = [A.alloc([G, 128], F32) for _ in range(2)]
        kfull_l = [A.alloc([G, 96], BF16) for _ in range(2)]
        qs_l = [A.alloc([G, 96], F32) for _ in range(2)]
        qn_l = [A.alloc([G, 96], F32) for _ in range(2)]
        qfull_l = [A.alloc([G, 96], BF16) for _ in range(2)]
        ssq_l = [A.alloc([G, 96], F32) for _ in range(2)]
        rs8_l = [A.alloc([2 * G], F32) for _ in range(2)]
        PT = [A.alloc([TT], BF16) for _ in range(2)]
        rden = A.alloc([TT], F32)
        otmp = A.alloc([TT], F32)
        for r_ in range(G):
            if r_ % 2 == 0:
                mset('pool', vfull[:, :, r_, 64:128], 1.0, [], ['vfull'])
            else:
                mset('pool', vfull[:, :, r_, 0:64], 1.0, [], ['vfull'])

        def keytile_len(kt):
            return 128 if kt < nkt_past else CH

        def keytile_off(kt):
            return kt * 128 if kt < nkt_past else past + (kt - nkt_past) * CH

        nl = [0]

        def load_grp_w(gi):
            k = nl[0] % 2
            nl[0] += 1
            dma('sp', wqb[k], S_qb[j][:, :, gi * G * 96:(gi + 1) * G * 96], [f'w_qb{j}'], [f'wqb{k}'])
            dma('sp', wkvb[k], S_kvb[j][:, :, gi * G * 128:(gi + 1) * G * 128], [f'w_kvb{j}'], [f'wkvb{k}'])
            return k

        pend = load_grp_w(0)
        for gi in range(NH // G):
            k = pend
            if gi + 1 < NH // G:
                pend = load_grp_w(gi + 1)
            for kt in range(NKT):
                mp = mcnt[0] % 2
                mcnt[0] += 1
                M_ = lambda s_, mp=mp: s_ + '_%d' % mp
                kvs = kvs_l[mp]
                kfull = kfull_l[mp]
                qs = qs_l[mp]
                qn = qn_l[mp]
                qfull = qfull_l[mp]
                ssq = ssq_l[mp]
                rs8 = rs8_l[mp]
                n = keytile_len(kt)
                off = keytile_off(kt)
                c_ = slice(0, n)
                b = psum('mm')
                mm(PS[b][0:n, 0:G * 128], [(latT[:, kc, off:off + n], wkvb[k][:, kc, :]) for kc in range(2)], ['latT', f'wkvb{k}'], [f'ps{b}'])
                cp('dve', kvs[c_], PS[b][0:n, 0:G * 128].rearrange("p (r d) -> p r d", r=G, d=128), [f'ps{b}'], [M_('kvs')])
                for r_ in range(G):
                    lo = 0 if r_ % 2 == 0 else 64
                    cp('pool', vfull[c_, kt, r_, lo:lo + 64], kvs[c_, r_, 64:128], [M_('kvs')], ['vfull'])
                tt('dve', ssq[c_, :, 0:64], kvs[c_, :, 0:64], kvs[c_, :, 0:64], ALU.mult, [M_('kvs')], [M_('ssq')])
                P.op('dve', lambda e, o=rs8[c_, 0:G], i_=ssq[c_, :, 0:64]: e.tensor_reduce(out=o, in_=i_, axis=AX.X, op=ALU.add), [M_('ssq')], [M_('rs8')])
                act(rs8[c_, 0:G], rs8[c_, 0:G], AF.Ln, [M_('rs8')], [M_('rs8')], bias=EPS, scale=1.0 / 64)
                act(rs8[c_, 0:G], rs8[c_, 0:G], AF.Exp, [M_('rs8')], [M_('rs8')], scale=-0.5)
                tt('dve', ssq[c_, :, 0:64], kvs[c_, :, 0:64], bc(rs8[c_, 0:G].unsqueeze(2), [n, G, 64]), ALU.mult, [M_('kvs'), M_('rs8')], [M_('ssq')])
                tt('dve', kfull[c_, :, 0:64], ssq[c_, :, 0:64], bc(kng[c_].unsqueeze(1), [n, G, 64]), ALU.mult, [M_('ssq'), 'kng'], [M_('kfull')])
                cp('pool', kfull[c_, :, 64:96], bc(krall[c_, kt, :].unsqueeze(1), [n, G, 32]), ['krall'], [M_('kfull')])
                bt = psum('tr')
                pb = PS[bt][:].bitcast(BF16)
                tr_multi([(pb[0:96, r_ * 128:r_ * 128 + n], kfull[c_, r_, :]) for r_ in range(G)], identb[0:n, 0:n], [M_('kfull'), 'cm'], [f'ps{bt}'])
                cp('act', KT[0:96, :, off:off + n], pb[0:96, 0:G * 128].rearrange("p (r c) -> p r c", r=G, c=128)[:, :, 0:n], [f'ps{bt}'], ['KT'])
            for ti in range(NT):
                tok0 = ti * TT
                qp = qcnt[0] % 2
                qcnt[0] += 1
                QT = QT_l[qp]
                QTN = 'QT%d' % qp
                for ct in range(NCT):
                    mp = mcnt[0] % 2
                    mcnt[0] += 1
                    M_ = lambda s_, mp=mp: s_ + '_%d' % mp
                    kvs = kvs_l[mp]
                    kfull = kfull_l[mp]
                    qs = qs_l[mp]
                    qn = qn_l[mp]
                    qfull = qfull_l[mp]
                    ssq = ssq_l[mp]
                    rs8 = rs8_l[mp]
                    c0 = ct * CH
                    t0 = tok0 + c0
                    c_ = slice(0, CH)
                    dma('sp', rope[c_], I['c_rope'][rope_row0 + t0:rope_row0 + t0 + CH, :], [], ['rope'])
                    b = psum('mm')
                    mm(PS[b][0:CH, 0:G * 96], [(qanT[:, kc, t0:t0 + CH], wqb[k][:, kc, :]) for kc in range(3)], ['qanT', f'wqb{k}'], [f'ps{b}'])
                    cp('dve', qs[c_], PS[b][0:CH, 0:G * 96].rearrange("p (r d) -> p r d", r=G, d=96), [f'ps{b}'], [M_('qs')])
                    tt('dve', ssq[c_], qs[c_], qs[c_], ALU.mult, [M_('qs')], [M_('ssq')])
                    P.op('dve', lambda e, o=rs8[c_, 0:G], i_=ssq[c_, :, 0:64]: e.tensor_reduce(out=o, in_=i_, axis=AX.X, op=ALU.add), [M_('ssq')], [M_('rs8')])
                    P.op('dve', lambda e, o=rs8[c_, G:2 * G], i_=ssq[c_, :, 64:96]: e.tensor_reduce(out=o, in_=i_, axis=AX.X, op=ALU.add), [M_('ssq')], [M_('rs8')])
                    act(rs8[c_, 0:G], rs8[c_, 0:G], AF.Ln, [M_('rs8')], [M_('rs8')], bias=EPS, scale=1.0 / 64)
                    act(rs8[c_, G:2 * G], rs8[c_, G:2 * G], AF.Ln, [M_('rs8')], [M_('rs8')], bias=EPS, scale=1.0 / 32)
                    act(rs8[c_], rs8[c_], AF.Exp, [M_('rs8')], [M_('rs8')], scale=-0.5)
                    tt('dve', qn[c_, :, 0:64], qs[c_, :, 0:64], bc(rs8[c_, 0:G].unsqueeze(2), [CH, G, 64]), ALU.mult, [M_('qs'), M_('rs8')], [M_('qn')])
                    tt('dve', qn[c_, :, 64:96], qs[c_, :, 64:96], bc(rs8[c_, G:2 * G].unsqueeze(2), [CH, G, 32]), ALU.mult, [M_('qs'), M_('rs8')], [M_('qn')])
                    tt('dve', qfull[c_, :, 0:64], qn[c_, :, 0:64], bc(qng[c_].unsqueeze(1), [CH, G, 64]), ALU.mult, [M_('qn'), 'qng'], [M_('qfull')])
                    tt('dve', qn[c_, :, 64:96], qn[c_, :, 64:96], bc(qrg[c_].unsqueeze(1), [CH, G, 32]), ALU.mult, [M_('qn'), 'qrg'], [M_('qn')])
                    do_rope(qfull[c_, :, 64:96], qn[c_, :, 64:96], CH, G, [M_('qn')], [M_('qfull')])
                    bt = psum('tr')
                    pb = PS[bt][:].bitcast(BF16)
                    tr_multi([(pb[0:96, r_ * 128:r_ * 128 + CH], qfull[c_, r_, :]) for r_ in range(G)], identb[0:CH, 0:CH], [M_('qfull'), 'cm'], [f'ps{bt}'])
                    cp('act', QT[0:96, :, c0:c0 + CH], pb[0:96, 0:G * 128].rearrange("p (r c) -> p r c", r=G, c=128)[:, :, 0:CH], [f'ps{bt}'], [QTN])
                if kind == 'p':
                    nvis = (tok0 + TT) // 128
                else:
                    nvis = NKT
                for r_ in range(G):
                    h = gi * G + r_
                    bo = psum('acc')
                    npv = 0
                    for kt in range(nvis):
                        n = keytile_len(kt)
                        off = keytile_off(kt)
                        cs = 0
                        if kind == 'p' and kt * 128 >= tok0:
                            cs = kt * 128 - tok0
                        b = psum('sc')
                        mm(PS[b][0:n, cs:TT], [(KT[0:96, r_, off:off + n], QT[0:96, r_, cs:TT])], ['KT', QTN], [f'ps{b}'])
                        pk = npv % 2
                        act(PT[pk][0:n, cs:TT], PS[b][0:n, cs:TT], AF.Exp, [f'ps{b}'], [f'PT{pk}'], scale=SCALE)
                        if kind == 'p' and kt * 128 >= tok0:
                            mset('pool', PT[pk][64:128, cs:cs + 64], 0.0, [], [f'PT{pk}'])
                        first = (kt == 0)
                        last = (kt == nvis - 1)
                        P.op('pe', lambda e, o=PS[bo][:, cs:TT], l_=vfull[0:n, kt, r_, :], rr=PT[pk][0:n, cs:TT], f=first, la=last:
                             e.matmul(o, lhsT=l_, rhs=rr, start=f, stop=la), [f'PT{pk}', 'vfull'], [f'ps{bo}'])
                        npv += 1
                    lo, dn = (0, 64) if r_ % 2 == 0 else (64, 0)
                    act(rden[lo:lo + 64, 0:TT], PS[bo][dn:dn + 64, 0:TT], AF.Ln, [f'ps{bo}'], ['rden'])
                    act(rden[lo:lo + 64, 0:TT], rden[lo:lo + 64, 0:TT], AF.Exp, ['rden'], ['rden'], scale=-1.0)
                    tt('dve', otmp[lo:lo + 64, 0:TT], PS[bo][lo:lo + 64, 0:TT], rden[lo:lo + 64, 0:TT], ALU.mult, [f'ps{bo}', 'rden'], ['otmp'])
                    tt('pool', sg[lo:lo + 64, h // 2, tok0:tok0 + TT], sg[lo:lo + 64, h // 2, tok0:tok0 + TT], otmp[lo:lo + 64, 0:TT],
                       ALU.mult, ['sg', 'otmp'], ['sg'])
        P.barrier()
        P.curlabel = 'mla_end_' + kind
        set_rings({'mm': [0, 1, 2, 3], 'sc': [4], 'acc': [5], 'tr': [6, 7]})
        A.release(mP0)
        sq = alloc_common(T, TT)
        wout = A.alloc([8, D], BF16)
        dma('sp', wout, S_out_mla[j], [f'w_out_mla{j}'], ['wout'])
        pin = (I['p_prompt'] if kind == 'p' else I['p_sample'])[i, sidx]
        for ti in range(NT):
            tok0 = ti * TT
            for jo in range(8):
                b = psum('mm')
                mm(PS[b][:, 0:TT], [(wout[:, kc, jo * 128:(jo + 1) * 128], sg[:, kc, tok0:tok0 + TT]) for kc in range(8)], ['wout', 'sg'], [f'ps{b}'])
                tt('dve', hT[:, jo, tok0:tok0 + TT], hT[:, jo, tok0:tok0 + TT], PS[b][:, 0:TT], ALU.add, [HT(jo, tok0), f'ps{b}'], [HT(jo, tok0)])
            ple_tile(sq, i, pin, tok0, TT, NCT, CH)
        P.barrier()
        A.release(m0)

    def run_seq(kind, sidx):
        P.curlabel = 'io_' + kind
        T = TP if kind == 'p' else TS
        TT = min(cfg.tt, T)
        TTG[0] = TT
        CH = min(128, T)
        xin = (I['x_prompt'] if kind == 'p' else I['x_sample'])[sidx]
        yout_d = (O['y_prompt'] if kind == 'p' else O['y_sample'])[sidx]
        m0 = A.mark()
        xt = [A.alloc([D], F32) for _ in range(2)]
        for t_ in range(T // CH):
            k = t_ % 2
            dma('sp', xt[k][0:CH], xin[t_ * CH:(t_ + 1) * CH, :], [], [f'xt{k}'])
            for hf in range(2):
                b = psum('tr')
                tr_multi([(PS[b][:, q * 128:q * 128 + CH], xt[k][0:CH, (hf * 4 + q) * 128:(hf * 4 + q + 1) * 128]) for q in range(4)],
                         identf[0:CH, 0:CH], [f'xt{k}', 'identf'], [f'ps{b}'])
                cp('act' if hf else 'dve', hT[:, hf * 4:(hf + 1) * 4, t_ * CH:(t_ + 1) * CH],
                   PS[b][:, 0:512].rearrange("p (q c) -> p q c", q=4, c=128)[:, :, 0:CH], [f'ps{b}'], [HT(hf * 4 + q, t_ * CH) for q in range(4)])
        P.barrier()
        A.release(m0)
        for i in range(cfg.depth):
            if getattr(cfg, 'skip_layers', False):
                continue
            if i % 2 == 0:
                if not getattr(cfg, 'skip_ssd', False):
                    ssd_layer(kind, sidx, i, T, TT, CH)
            else:
                if not getattr(cfg, 'skip_mla', False):
                    mla_layer(kind, sidx, i, T, TT, CH)
        m0 = A.mark()
        yt = [A.alloc([D], F32) for _ in range(2)]
        for t_ in range(T // CH):
            k = t_ % 2
            for hf in range(2):
                b = psum('tr')
                tr_multi([(PS[b][0:CH, q * 128:(q + 1) * 128], hT[:, hf * 4 + q, t_ * CH:(t_ + 1) * CH]) for q in range(4)],
                         identf, [HT(hf * 4 + q, t_ * CH) for q in range(4)] + ['identf'], [f'ps{b}'])
                cp('act' if hf else 'dve', yt[k][0:CH, hf * 512:(hf + 1) * 512], PS[b][0:CH, 0:512], [f'ps{b}'], [f'yt{k}'])
            dma('pool', yout_d[t_ * CH:(t_ + 1) * CH, :], yt[k][0:CH], [f'yt{k}'], [])
        P.barrier()
        A.release(m0)

    for s_i in range(cfg.nseq_p):
        if not getattr(cfg, 'skip_p', False):
            run_seq('p', s_i)
    for s_i in range(cfg.nseq_s):
        if not getattr(cfg, 'skip_s', False):
            run_seq('s', s_i)
    P.barrier()

    P.finalize(reorder=getattr(cfg, 'reorder', True))
    semnames = sorted(P.cnt.keys())
    sem_ctx = {k: nc.semaphore("s_" + k) for k in semnames}
    sems = {k: c.__enter__() for k, c in sem_ctx.items()}
    engobj = {'pe': 'tensor', 'act': 'scalar', 'dve': 'vector', 'pool': 'gpsimd', 'sp': 'sync'}
    with nc.Block() as block:
        def emit(engname):
            def body(e):
                for item in P.streams[engname]:
                    if item[0] == 'wait':
                        e.wait_ge(sems[item[1]], item[2])
                    else:
                        _, fn, key, inc = item[:4]
                        fn(e).then_inc(sems[key], inc)
            return body
        block.tensor(emit('pe'))
        block.scalar(emit('act'))
        block.vector(emit('dve'))
        block.gpsimd(emit('pool'))
        block.sync(emit('sp'))
    for c in sem_ctx.values():
        c.__exit__(None, None, None)
    for t in reversed(ctx):
        t.__exit__(None, None, None)
    stats = {e: len(P.streams[e]) for e in ENGS}
    build_program.planner = P
    return nc, stats, A.peak


def make_consts(cfg):
    a = np.arange(128)
    identf = np.zeros((128, 2, 128), np.float32)
    identf[:, 0, :] = np.eye(128)
    identf[:, 1, :] = (a[:, None] > a[None, :])
    mats = np.zeros((128, 8, 128), np.float32)
    mats[:, 4:8, :] = np.where(a[None, :] < a[:, None], -30000.0, 0.0)[:, None, :]
    mats[:, 0, :] = np.eye(128)
    mats[:, 1, :] = (a[:, None] <= a[None, :])
    mats[:, 2, :] = (a[:, None] > a[None, :])
    mats[:, 3, :] = 1.0
    pos = np.concatenate([np.arange(cfg.t_p), cfg.past + np.arange(cfg.t_s)]).astype(np.float32)
    inv = (1.0 / (10000.0 ** (np.arange(0, 32, 2, dtype=np.float32) / 32.0))).astype(np.float32)
    ang = pos[:, None] * inv[None, :]
    rope = np.concatenate([np.cos(ang), np.sin(ang)], axis=1).astype(np.float32)
    return {'c_identf': identf, 'c_mats': mats.astype(ml_dtypes.bfloat16), 'c_rope': rope}


_BATCH_P = ['x_prompt']
_WEIGHTS = ['ln_w', 'ssd_in_w', 'ssd_conv_w', 'ssd_conv_b', 'ssd_dt_bias', 'ssd_A_log', 'ssd_D', 'ssd_norm_w', 'ssd_out_w',
            'mla_in_w', 'mla_q_a_norm', 'mla_q_b_w', 'mla_kv_a_norm', 'mla_kv_b_w', 'mla_q_nope_norm', 'mla_q_rope_norm',
            'mla_k_nope_norm', 'mla_k_rope_norm', 'mla_out_w', 'ple_up_w', 'ple_norm_w', 'ple_gate_w']


def shard_inputs(inputs, cfg, ncores):
    consts = make_consts(cfg)
    maps = []
    f = lambda a: np.ascontiguousarray(np.asarray(a, dtype=np.float32))
    for c in range(ncores):
        p0, p1 = c * cfg.nseq_p, (c + 1) * cfg.nseq_p
        s0, s1 = c * cfg.nseq_s, (c + 1) * cfg.nseq_s
        m = {
            'x_prompt': f(inputs['x_prompt'][p0:p1]),
            'x_sample': f(inputs['x_sample'][s0:s1]),
            'cache_conv': f(inputs['cache_conv'][:, s0:s1]),
            'state_ssm': f(inputs['state_ssm'][:, s0:s1]),
            'cache_kv_latent': f(inputs['cache_kv_latent'][:, s0:s1]),
            'cache_k_rope': f(inputs['cache_k_rope'][:, s0:s1]),
            'p_prompt': f(inputs['p_prompt'][:, p0:p1]),
            'p_sample': f(inputs['p_sample'][:, s0:s1]),
        }
        for wn in _WEIGHTS:
            m[wn] = f(inputs[wn])
        m.update(consts)
        maps.append(m)
    return maps


def gather_outputs(results, cfg):
    cat = lambda k, ax: np.concatenate([np.asarray(r[k], dtype=np.float32) for r in results], axis=ax)
    return (cat('y_prompt', 0), cat('y_sample', 0), cat('conv_prompt', 1), cat('ssm_prompt', 1),
            cat('lat_prompt', 1), cat('kr_prompt', 1), cat('conv_sample', 1), cat('ssm_sample', 1),
            cat('lat_sample', 1), cat('kr_sample', 1))


def kernel(**inputs):
    cfg = Cfg()
    ncores = 8
    nc, stats, peak = build_program(cfg)
    maps = shard_inputs(inputs, cfg, ncores)
    res = run_bass_kernel_spmd(nc, maps, core_ids=list(range(ncores)))
    return gather_outputs(res.results, cfg)
```

```python
import math
import numpy as np
import ml_dtypes
import concourse.bass as bass
import concourse.mybir as mybir
from concourse.bass_utils import run_bass_kernel_spmd

F32 = mybir.dt.float32
BF16 = mybir.dt.bfloat16
ALU = mybir.AluOpType
AF = mybir.ActivationFunctionType
AX = mybir.AxisListType

EPS = 1e-6
D = 1024
NKC = 8
SSD_IN = 6176
MLA_IN = 1696
NH = 16
SCALE = 96 ** -0.5


class Cfg:
    def __init__(self, nseq_p=4, t_p=2048, nseq_s=2, t_s=32, past=2048, tt=512, depth=4):
        self.nseq_p, self.t_p, self.nseq_s, self.t_s, self.past, self.tt, self.depth = nseq_p, t_p, nseq_s, t_s, past, tt, depth
        self.n_ssd = (depth + 1) // 2
        self.n_mla = depth // 2


ENGS = ['pe', 'act', 'dve', 'pool', 'sp']
NDSEM = 16


class Planner:
    WINDOW = 160
    LAT = 120.0

    def __init__(self):
        self.ops = []
        self.res = {}
        self.segs = [0]
        self.streams = {e: [] for e in ENGS}
        self.cnt = {}
        self.known = {e: {} for e in ENGS}
        self.ndma = {}
        self.finalized = False
        self.labels = {}
        self.wnames = []
        self.times = {}
        self.curlabel = 'start'

    def op(self, eng, fn, r=(), w=(), dma=False, cost=100.0):
        w = list(w) + [n for n in r if n.startswith('ps')]
        r = [n for n in r if not n.startswith('ps')]
        lo = self.segs[-1]
        deps = set()
        for n in r:
            st = self.res.get(n)
            if st and st[0] is not None and st[0] >= lo:
                deps.add(st[0])
        for n in w:
            st = self.res.get(n)
            if st:
                if st[0] is not None and st[0] >= lo:
                    deps.add(st[0])
                for x in st[1]:
                    if x >= lo:
                        deps.add(x)
        i = len(self.ops)
        if i == self.segs[-1]:
            self.labels[i] = self.curlabel
        self.ops.append((eng, fn, sorted(deps), dma, cost))
        self.wnames.append((tuple(w), tuple(r)))
        for n in r:
            self.res.setdefault(n, [None, []])[1].append(i)
        for n in w:
            self.res[n] = [i, []]
        return i

    def barrier(self):
        if self.segs[-1] != len(self.ops):
            self.segs.append(len(self.ops))

    def _schedule(self, lo, hi, reorder=True):
        ops = self.ops
        by = {e: [] for e in ENGS}
        for i in range(lo, hi):
            by[ops[i][0]].append(i)
        if not reorder:
            return by
        W, LAT = self.WINDOW, self.LAT
        finish = {}
        sched = set()
        t = {e: 0.0 for e in ENGS}
        head = {e: 0 for e in ENGS}
        order = {e: [] for e in ENGS}
        cache = {e: None for e in ENGS}
        blocked = {e: set() for e in ENGS}
        left = hi - lo

        def scan(e):
            lst = by[e]
            i = head[e]
            c = 0
            best = None
            blk = set()
            te = t[e]
            n = len(lst)
            while i < n and c < W:
                oid = lst[i]
                if oid in sched:
                    i += 1
                    continue
                c += 1
                ok = True
                rdy = te
                for d in ops[oid][2]:
                    f = finish.get(d)
                    if f is None:
                        ok = False
                        blk.add(d)
                        break
                    if ops[d][0] == 'pe' and e == 'pe' and not ops[d][3]:
                        f2 = f - LAT
                    else:
                        f2 = f
                    if f2 > rdy:
                        rdy = f2
                if ok:
                    if best is None or rdy < best[0]:
                        best = (rdy, i, oid)
                    if rdy <= te:
                        break
                i += 1
            blocked[e] = blk
            cache[e] = best if best is not None else 'none'

        while left:
            bestc = None
            for e in ENGS:
                if head[e] >= len(by[e]):
                    continue
                if cache[e] is None:
                    scan(e)
                c = cache[e]
                if c == 'none':
                    continue
                if bestc is None or c[0] < bestc[0][0]:
                    bestc = (c, e)
            assert bestc is not None, "scheduler stuck"
            (rdy, idx, oid), e = bestc
            eng, fn, deps, dma, cost = ops[oid]
            if dma:
                t[e] = rdy + 60.0
                finish[oid] = rdy + cost + LAT
            else:
                t[e] = rdy + cost
                finish[oid] = t[e] + LAT
            sched.add(oid)
            order[e].append(oid)
            self.times[oid] = (rdy, t[e], lo)
            lst = by[e]
            h = head[e]
            while h < len(lst) and lst[h] in sched:
                h += 1
            head[e] = h
            cache[e] = None
            for e2 in ENGS:
                if e2 != e and oid in blocked[e2]:
                    cache[e2] = None
            left -= 1
        self.est = max(self.est if hasattr(self, 'est') else 0.0, 0.0)
        self.seg_time = getattr(self, 'seg_time', 0.0) + max(t.values())
        busy = {e2: sum(ops[i][4] for i in by[e2] if not ops[i][3]) for e2 in ENGS}
        self.seg_log = getattr(self, 'seg_log', [])
        self.seg_log.append((self.labels.get(lo, '?'), max(t.values()), busy))
        return order

    def finalize(self, reorder=True):
        if self.finalized:
            return
        self.finalized = True
        ops = self.ops
        bounds = self.segs + [len(ops)]
        ev = {}
        for si in range(len(bounds) - 1):
            lo, hi = bounds[si], bounds[si + 1]
            if lo == hi:
                continue
            order = self._schedule(lo, hi, reorder)
            pre_wait = {}
            for e in ENGS:
                for oid in order[e]:
                    dma = ops[oid][3]
                    if dma:
                        nq = self.ndma.get(e, 0)
                        self.ndma[e] = nq + 1
                        key = 'd_%s_%d' % (e, nq % NDSEM)
                        prev = self.cnt.get(key, 0)
                        if prev:
                            pre_wait[oid] = (key, prev)
                        self.cnt[key] = prev + 16
                    else:
                        key = e
                        self.cnt[key] = self.cnt.get(key, 0) + 1
                    ev[oid] = (key, self.cnt[key])
            for e in ENGS:
                kn = self.known[e]
                st = self.streams[e]
                for oid in order[e]:
                    eng, fn, deps, dma, cost = ops[oid]
                    need = {}
                    for d in deps:
                        if ops[d][0] == 'pe' and e == 'pe' and not ops[d][3] and not dma:
                            continue
                        k, v = ev[d]
                        if need.get(k, 0) < v:
                            need[k] = v
                    pw = pre_wait.get(oid)
                    if pw and need.get(pw[0], 0) < pw[1]:
                        need[pw[0]] = pw[1]
                    for k, v in need.items():
                        if kn.get(k, 0) >= v:
                            continue
                        kn[k] = v
                        st.append(('wait', k, v))
                    k, v = ev[oid]
                    st.append(('op', fn, k, 16 if dma else 1, cost, v))
            for e in ENGS:
                kn = self.known[e]
                for k, v in self.cnt.items():
                    if kn.get(k, 0) < v:
                        kn[k] = v
                        self.streams[e].append(('wait', k, v))


class Arena:
    def __init__(self, ap_f32, nwords):
        self.ap = ap_f32
        self.n = nwords
        self.off = 0
        self.peak = 0

    def mark(self):
        return self.off

    def release(self, m):
        self.off = m

    def alloc(self, shape, dt):
        ne = 1
        for s_ in shape:
            ne *= s_
        words = ne if dt == F32 else (ne + 1) // 2
        words = (words + 3) // 4 * 4
        a = self.off
        self.off += words
        self.peak = max(self.peak, self.off)
        assert self.off <= self.n, f"arena overflow {self.off} > {self.n}"
        v = self.ap[:, a:a + words]
        if dt != F32:
            v = v.bitcast(dt)[:, 0:ne]
        else:
            v = v[:, 0:ne]
        if len(shape) == 2:
            v = v.rearrange("p (a b) -> p a b", a=shape[0], b=shape[1])
        elif len(shape) == 3:
            v = v.rearrange("p (a b c) -> p a b c", a=shape[0], b=shape[1], c=shape[2])
        return v


def bc(ap, shape):
    return ap.to_broadcast(list(shape))


def build_program(cfg):
    nc = bass.Bass("TRN2", target_bir_lowering=False)
    P = Planner()
    NS, NM = cfg.n_ssd, cfg.n_mla
    TP, TS, PAST = cfg.t_p, cfg.t_s, cfg.past

    def din(name, shape, dt=F32):
        return nc.dram_tensor(name, list(shape), dt, kind="ExternalInput").ap()

    def dout(name, shape):
        return nc.dram_tensor(name, list(shape), F32, kind="ExternalOutput").ap()

    def dscr(name, shape):
        return nc.dram_tensor(name, list(shape), BF16, kind="Internal").ap()

    I = {}
    I['x_prompt'] = din('x_prompt', [cfg.nseq_p, TP, D])
    I['x_sample'] = din('x_sample', [cfg.nseq_s, TS, D])
    I['cache_conv'] = din('cache_conv', [NS, cfg.nseq_s, 3, 4096])
    I['state_ssm'] = din('state_ssm', [NS, cfg.nseq_s, 32, 64, 128])
    I['cache_kv_latent'] = din('cache_kv_latent', [NM, cfg.nseq_s, PAST, 256])
    I['cache_k_rope'] = din('cache_k_rope', [NM, cfg.nseq_s, PAST, 32])
    I['p_prompt'] = din('p_prompt', [cfg.depth, cfg.nseq_p, TP, 256])
    I['p_sample'] = din('p_sample', [cfg.depth, cfg.nseq_s, TS, 256])
    I['ln_w'] = din('ln_w', [cfg.depth, D])
    I['ssd_in_w'] = din('ssd_in_w', [NS, D, SSD_IN])
    I['ssd_conv_w'] = din('ssd_conv_w', [NS, 4, 4096])
    I['ssd_conv_b'] = din('ssd_conv_b', [NS, 4096])
    I['ssd_dt_bias'] = din('ssd_dt_bias', [NS, 32])
    I['ssd_A_log'] = din('ssd_A_log', [NS, 32])
    I['ssd_D'] = din('ssd_D', [NS, 32])
    I['ssd_norm_w'] = din('ssd_norm_w', [NS, 2048])
    I['ssd_out_w'] = din('ssd_out_w', [NS, 2048, D])
    I['mla_in_w'] = din('mla_in_w', [NM, D, MLA_IN])
    I['mla_q_a_norm'] = din('mla_q_a_norm', [NM, 384])
    I['mla_q_b_w'] = din('mla_q_b_w', [NM, 384, 1536])
    I['mla_kv_a_norm'] = din('mla_kv_a_norm', [NM, 256])
    I['mla_kv_b_w'] = din('mla_kv_b_w', [NM, 256, 2048])
    I['mla_q_nope_norm'] = din('mla_q_nope_norm', [NM, 64])
    I['mla_q_rope_norm'] = din('mla_q_rope_norm', [NM, 32])
    I['mla_k_nope_norm'] = din('mla_k_nope_norm', [NM, 64])
    I['mla_k_rope_norm'] = din('mla_k_rope_norm', [NM, 32])
    I['mla_out_w'] = din('mla_out_w', [NM, D, D])
    I['ple_up_w'] = din('ple_up_w', [cfg.depth, 256, D])
    I['ple_norm_w'] = din('ple_norm_w', [cfg.depth, D])
    I['ple_gate_w'] = din('ple_gate_w', [cfg.depth, D, D])
    I['c_identf'] = din('c_identf', [128, 2, 128])
    I['c_mats'] = din('c_mats', [128, 8, 128], BF16)
    I['c_rope'] = din('c_rope', [TP + TS, 32])

    O = {}
    O['y_prompt'] = dout('y_prompt', [cfg.nseq_p, TP, D])
    O['y_sample'] = dout('y_sample', [cfg.nseq_s, TS, D])
    O['conv_prompt'] = dout('conv_prompt', [NS, cfg.nseq_p, 3, 4096])
    O['ssm_prompt'] = dout('ssm_prompt', [NS, cfg.nseq_p, 32, 64, 128])
    O['lat_prompt'] = dout('lat_prompt', [NM, cfg.nseq_p, TP, 256])
    O['kr_prompt'] = dout('kr_prompt', [NM, cfg.nseq_p, TP, 32])
    O['conv_sample'] = dout('conv_sample', [NS, cfg.nseq_s, 3, 4096])
    O['ssm_sample'] = dout('ssm_sample', [NS, cfg.nseq_s, 32, 64, 128])
    O['lat_sample'] = dout('lat_sample', [NM, cfg.nseq_s, TS, 256])
    O['kr_sample'] = dout('kr_sample', [NM, cfg.nseq_s, TS, 32])

    S_in_ssd = [dscr(f's_in_ssd{j}', [128, 8, 8, 768]) for j in range(NS)]
    S_dt = [dscr(f's_dt{j}', [128, 8, 32]) for j in range(NS)]
    S_out_ssd = [dscr(f's_out_ssd{j}', [128, 16, D]) for j in range(NS)]
    S_in_mla = [dscr(f's_in_mla{j}', [128, 8, MLA_IN]) for j in range(NM)]
    S_qb = [dscr(f's_qb{j}', [128, 3, 1536]) for j in range(NM)]
    S_kvb = [dscr(f's_kvb{j}', [128, 2, 2048]) for j in range(NM)]
    S_out_mla = [dscr(f's_out_mla{j}', [128, 8, D]) for j in range(NM)]
    S_gate = [dscr(f's_gate{i}', [128, 8, D]) for i in range(cfg.depth)]
    S_up = [dscr(f's_up{i}', [128, 2, D]) for i in range(cfg.depth)]

    ARENA_WORDS = 53200
    ctx = []
    arena_t = nc.sbuf_tensor("arena", [128, ARENA_WORDS], F32)
    arena_ap = arena_t.__enter__()
    ctx.append(arena_t)
    psum_t = []
    PS = []
    for b in range(8):
        t = nc.psum_tensor(f"psb{b}", [128, 512], F32)
        PS.append(t.__enter__())
        ctx.append(t)
    A = Arena(arena_ap, ARENA_WORDS)

    def vcost(eng, out):
        n = out.free_size()
        if eng == 'pool':
            return 250.0 + n / 0.6
        return 180.0 + n / 0.96

    def mm(out, pairs, r, w):
        pairs = list(pairs)

        def fn(e):
            ins = None
            n = len(pairs)
            for i_, (l_, r_) in enumerate(pairs):
                ins = e.matmul(out, lhsT=l_, rhs=r_, start=(i_ == 0), stop=(i_ == n - 1))
            return ins
        c = sum(max(64, r_.free_size()) / 2.2 + 4 for (_, r_) in pairs)
        return P.op('pe', fn, r, w, cost=c)

    def mm_multi(items, r, w):
        items = [(o, list(p)) for o, p in items]

        def fn(e):
            ins = None
            for o, pairs in items:
                n = len(pairs)
                for i_, (l_, r_) in enumerate(pairs):
                    ins = e.matmul(o, lhsT=l_, rhs=r_, start=(i_ == 0), stop=(i_ == n - 1))
            return ins
        c = sum(sum(max(64, r_.free_size()) / 2.2 + 4 for (_, r_) in p) for _, p in items)
        return P.op('pe', fn, r, w, cost=c)

    def tr_multi(items, ident, r, w):
        items = list(items)

        def fn(e):
            ins = None
            for o, i_ in items:
                ins = e.transpose(o, i_, ident)
            return ins
        c = sum((max(64, i_.free_size()) / 2.2 + 4) * (4 if i_.dtype == F32 else 1) for _, i_ in items)
        return P.op('pe', fn, r, w, cost=c)

    def act(out, in_, func, r, w, bias=None, scale=None, accum=None):
        kw = {}
        if bias is not None:
            kw['bias'] = bias
        if scale is not None:
            kw['scale'] = scale
        if accum is not None:
            kw['accum_out'] = accum
        return P.op('act', lambda e: e.activation(out=out, in_=in_, func=func, **kw), r, w, cost=(out.free_size() + 330) / 1.2)

    def tt(eng, out, in0, in1, op, r, w):
        return P.op(eng, lambda e: e.tensor_tensor(out=out, in0=in0, in1=in1, op=op), r, w, cost=vcost(eng, out))

    def ts(eng, out, in0, s1, s2, op0, op1, r, w, accum=None):
        kw = {}
        if op1 is not None:
            kw['op1'] = op1
        if accum is not None:
            kw['accum_out'] = accum
        return P.op(eng, lambda e: e.tensor_scalar(out=out, in0=in0, scalar1=s1, scalar2=s2, op0=op0, **kw), r, w, cost=vcost(eng, out))

    def stt(eng, out, in0, sc, in1, op0, op1, r, w):
        return P.op(eng, lambda e: e.scalar_tensor_tensor(out=out, in0=in0, scalar=sc, in1=in1, op0=op0, op1=op1), r, w, cost=vcost(eng, out))

    def cp(eng, out, in_, r, w):
        if eng == 'act':
            return P.op('act', lambda e: e.copy(out=out, in_=in_), r, w, cost=(out.free_size() + 250) / 1.2)
        return P.op(eng, lambda e: e.tensor_copy(out=out, in_=in_), r, w, cost=vcost(eng, out))

    def recip(out, in_, r, w):
        return P.op('dve', lambda e: e.reciprocal(out=out, in_=in_), r, w, cost=vcost('dve', out))

    def mset(eng, ap, val, r, w):
        return P.op(eng, lambda e: e.memset(ap, val), r, w, cost=vcost(eng, ap))

    def dma(q, out, in_, r, w, slow=False):
        dcost = 2000.0 + out.free_size() * out.partition_size() * (4 if out.dtype == F32 else 2) / 150.0
        if slow:
            return P.op(q, lambda e: e.dma_start(out=out, in_=in_, allow_slow_non_contiguous=True), r, w, dma=True, cost=dcost)
        return P.op(q, lambda e: e.dma_start(out=out, in_=in_), r, w, dma=True, cost=dcost)

    def dma_old(q, out, in_, r, w, slow=False):
        if slow:
            return P.op(q, lambda e: e.dma_start(out=out, in_=in_, allow_slow_non_contiguous=True), r, w, dma=True)
        return P.op(q, lambda e: e.dma_start(out=out, in_=in_), r, w, dma=True)

    psc = {'n': 0}
    rings = {'mm': [0, 1], 'sc': [2, 3, 4], 'acc': [5], 'tr': [6, 7]}
    ringpos = {k: 0 for k in rings}

    def set_rings(cfgd):
        rings.clear()
        rings.update(cfgd)

    def psum(tag):
        lst = rings[tag]
        b = lst[ringpos[tag] % len(lst)]
        ringpos[tag] += 1
        return b

    cf = A.alloc([2, 128], F32)
    identf, TLf = cf[:, 0, :], cf[:, 1, :]
    cm = A.alloc([8, 128], BF16)
    identb, TU, TL, onesb = cm[:, 0, :], cm[:, 1, :], cm[:, 2, :], cm[:, 3, :]
    NEGb = cm[:, 4:8, :]
    TMAX = max(TP, TS)
    hT = A.alloc([8, TMAX], F32)
    gains = A.alloc([128], F32)
    upw = A.alloc([2, D], BF16)
    gatew = [A.alloc([8, 128], BF16) for _ in range(2)]
    gw_n = [0]
    vstage = A.alloc([128], F32)
    dma('sp', cf, I['c_identf'], [], ['identf'])

    def load_vec_cols(dst, src1d, ncol, wtag):
        dma('sp', vstage[0:ncol, :], src1d.rearrange("(c p) -> c p", p=128), [], ['vstage'])
        b = psum('tr')
        tr_multi([(PS[b][:, 0:ncol], vstage[0:ncol, :])], identf[0:ncol, 0:ncol], ['vstage', 'identf'], [f'ps{b}'])
        cp('dve', dst, PS[b][:, 0:ncol], [f'ps{b}'], ['tails%d' % c_i for c_i in range(32)] if wtag == 'tailsall' else [wtag])
    dma('sp', cm, I['c_mats'], [], ['cm'])
    CONST = ['identf', 'cm']
    persist_mark = A.mark()

    def prologue():
        P.curlabel = 'prologue'
        m0 = A.mark()
        stg = [A.alloc([SSD_IN], F32) for _ in range(2)]
        stb = [A.alloc([SSD_IN], BF16) for _ in range(2)]
        gcol = {}
        col = [0]

        def load_gain(name, vec_ap, nkc):
            c0 = col[0]
            col[0] += nkc
            load_vec_cols(gains[:, c0:c0 + nkc], vec_ap, nkc, 'gains')
            gcol[name] = c0
        for i in range(cfg.depth):
            load_gain(('ln', i), I['ln_w'][i], 8)
            load_gain(('pn', i), I['ple_norm_w'][i], 8)
        for j in range(NS):
            load_gain(('sn', j), I['ssd_norm_w'][j], 16)
        for j in range(NM):
            load_gain(('qa', j), I['mla_q_a_norm'][j], 3)
        assert col[0] <= 128
        cnt = [0]

        def do_rows(src_rows, ncols, gain_col, stores):
            k = cnt[0] % 2
            cnt[0] += 1
            dma('sp', stg[k][:, 0:ncols], src_rows, [], [f'stg{k}'])
            eng = 'act' if (cnt[0] % 2 == 0) else 'dve'
            if gain_col is None:
                cp(eng, stb[k][:, 0:ncols], stg[k][:, 0:ncols], [f'stg{k}'], [f'stb{k}'])
            elif eng == 'act':
                act(stb[k][:, 0:ncols], stg[k][:, 0:ncols], AF.Copy, [f'stg{k}', 'gains'], [f'stb{k}'],
                    scale=gains[:, gain_col:gain_col + 1])
            else:
                ts('dve', stb[k][:, 0:ncols], stg[k][:, 0:ncols], gains[:, gain_col:gain_col + 1], None, ALU.mult, None,
                   [f'stg{k}', 'gains'], [f'stb{k}'])
            for dst, srcv, wname in stores:
                dma('pool', dst, srcv(stb[k]), [f'stb{k}'], [wname])

        for j in range(NS):
            i = 2 * j
            for kc in range(8):
                rows = I['ssd_in_w'][j, kc * 128:(kc + 1) * 128, :]
                dst = S_in_ssd[j]
                stores = [
                    (dst[:, kc, :, 0:256], lambda b: b[:, 2048:4096].rearrange("p (g c) -> p g c", g=8, c=256), f'w_in_ssd{j}'),
                    (dst[:, kc, :, 256:384], lambda b: b[:, 4096:5120].rearrange("p (g c) -> p g c", g=8, c=128), f'w_in_ssd{j}'),
                    (dst[:, kc, :, 384:512], lambda b: b[:, 5120:6144].rearrange("p (g c) -> p g c", g=8, c=128), f'w_in_ssd{j}'),
                    (dst[:, kc, :, 512:768], lambda b: b[:, 0:2048].rearrange("p (g c) -> p g c", g=8, c=256), f'w_in_ssd{j}'),
                    (S_dt[j][:, kc, :], lambda b: b[:, 6144:6176], f'w_dt{j}'),
                ]
                do_rows(rows, SSD_IN, gcol[('ln', i)] + kc, stores)
            for kc in range(16):
                rows = I['ssd_out_w'][j, kc * 128:(kc + 1) * 128, :]
                do_rows(rows, D, gcol[('sn', j)] + kc, [(S_out_ssd[j][:, kc, :], lambda b: b[:, 0:D], f'w_out_ssd{j}')])
        for j in range(NM):
            i = 2 * j + 1
            for kc in range(8):
                rows = I['mla_in_w'][j, kc * 128:(kc + 1) * 128, :]
                do_rows(rows, MLA_IN, gcol[('ln', i)] + kc, [(S_in_mla[j][:, kc, :], lambda b: b[:, 0:MLA_IN], f'w_in_mla{j}')])
            for kc in range(3):
                rows = I['mla_q_b_w'][j, kc * 128:(kc + 1) * 128, :]
                do_rows(rows, 1536, gcol[('qa', j)] + kc, [(S_qb[j][:, kc, :], lambda b: b[:, 0:1536], f'w_qb{j}')])
            for kc in range(2):
                rows = I['mla_kv_b_w'][j, kc * 128:(kc + 1) * 128, :]
                do_rows(rows, 2048, None, [(S_kvb[j][:, kc, :], lambda b: b[:, 0:2048], f'w_kvb{j}')])
            for kc in range(8):
                rows = I['mla_out_w'][j, kc * 128:(kc + 1) * 128, :]
                do_rows(rows, D, None, [(S_out_mla[j][:, kc, :], lambda b: b[:, 0:D], f'w_out_mla{j}')])
        for i in range(cfg.depth):
            for kc in range(8):
                rows = I['ple_gate_w'][i, kc * 128:(kc + 1) * 128, :]
                do_rows(rows, D, gcol[('pn', i)] + kc, [(S_gate[i][:, kc, :], lambda b: b[:, 0:D], f'w_gate{i}')])
            for kc in range(2):
                rows = I['ple_up_w'][i, kc * 128:(kc + 1) * 128, :]
                do_rows(rows, D, None, [(S_up[i][:, kc, :], lambda b: b[:, 0:D], f'w_up{i}')])
        P.barrier()
        A.release(m0)

    if not getattr(cfg, 'skip_pro', False):
        prologue()

    TTG = [cfg.tt]

    def HT(jo, tok0):
        return 'hT%d_%d' % (jo, tok0 // TTG[0])

    def HTA(tok0):
        return [HT(jo, tok0) for jo in range(8)]

    class Seq:
        pass

    def rms_fm(sq, tok0, n, hn_out, tag):
        hs = hT[:, :, tok0:tok0 + n]
        sqt = sq['sqt'][:, :, 0:n]
        act(sqt, hs, AF.Square, HTA(tok0), ['sqt'])
        b = psum('mm')
        pst = PS[b][:, 0:n]
        mm(pst, [(onesb, sqt[:, kc, :]) for kc in range(8)], ['sqt', 'cm'], [f'ps{b}'])
        rs = sq['rstd'][:, 0:n]
        act(rs, pst, AF.Ln, [f'ps{b}'], ['rstd'], bias=EPS, scale=1.0 / D)
        act(rs, rs, AF.Exp, ['rstd'], ['rstd'], scale=-0.5)
        tt('dve', hn_out, hs, bc(rs.unsqueeze(1), [128, 8, n]), ALU.mult, HTA(tok0) + ['rstd'], [tag])

    def ple_tile(sq, i, pin, tok0, n, nt128, cht, hnbuf=None, hntag='hn2', ebufs=None):
        hn = (sq['hn2'] if hnbuf is None else hnbuf)[:, :, 0:n]
        rms_fm(sq, tok0, n, hn, hntag)
        ptm = sq['ptm']
        pT = sq['pT']
        for t_ in range(nt128):
            c0 = t_ * cht
            dma('sp', ptm[0:cht, :], pin[tok0 + c0:tok0 + c0 + cht, :], [], ['ptm'])
            cp('pool', sq['ptmb'][0:cht, :], ptm[0:cht, :], ['ptm'], ['ptmb'])
            b = psum('tr')
            pb = PS[b][:].bitcast(BF16)
            tr_multi([(pb[:, k * 128:k * 128 + cht], sq['ptmb'][0:cht, k * 128:(k + 1) * 128]) for k in range(2)],
                     identb[0:cht, 0:cht], ['ptmb', 'cm'], [f'ps{b}'])
            cp('act', pT[:, :, c0:c0 + cht], pb[:, 0:256].rearrange("p (k c) -> p k c", k=2, c=128)[:, :, 0:cht],
               [f'ps{b}'], ['pT'])
        for jo in range(8):
            gk = gw_n[0] % 2
            gw_n[0] += 1
            dma('sp', gatew[gk], S_gate[i][:, :, jo * 128:(jo + 1) * 128], [f'w_gate{i}'], [f'gatew{gk}'])
            bg = psum('mm')
            mm(PS[bg][:, 0:n], [(gatew[gk][:, kc, :], hn[:, kc, :]) for kc in range(8)],
               [f'gatew{gk}', hntag], [f'ps{bg}'])
            bu = psum('mm')
            mm(PS[bu][:, 0:n], [(upw[:, kc, jo * 128:(jo + 1) * 128], pT[:, kc, 0:n]) for kc in range(2)],
               ['upw', 'pT'], [f'ps{bu}'])
            if ebufs is None:
                ebufs = [(sq['ple_e'], 'ple_e'), (sq['ple_e2'], 'ple_e2')]
            eb, et = ebufs[jo % len(ebufs)]
            e_ = eb[:, 0:n]
            act(e_, PS[bg][:, 0:n], AF.Exp, [f'ps{bg}'], [et], scale=-1.0)
            act(e_, e_, AF.Ln, [et], [et], bias=1.0, scale=1.0)
            act(e_, e_, AF.Exp, [et], [et], scale=-1.0)
            tt('dve', e_, PS[bu][:, 0:n], e_, ALU.mult, [f'ps{bu}', et], [et])
            tt('dve', hT[:, jo, tok0:tok0 + n], hT[:, jo, tok0:tok0 + n], e_, ALU.add, [HT(jo, tok0), et], [HT(jo, tok0)])

    def alloc_common(T, TT, need_hn2=True):
        sq = {}
        sq['sqt'] = A.alloc([8, TT], BF16)
        sq['rstd'] = A.alloc([TT], F32)
        if need_hn2:
            sq['hn2'] = A.alloc([8, TT], BF16)
        sq['ptm'] = A.alloc([256], F32)
        sq['ptmb'] = A.alloc([256], BF16)
        sq['pT'] = A.alloc([2, TT], BF16)
        if need_hn2:
            sq['ple_e'] = A.alloc([TT], F32)
            sq['ple_e2'] = A.alloc([TT], F32)
        return sq

    def load_ple_w(i):
        dma('sp', upw, S_up[i], [f'w_up{i}'], ['upw'])

    def ssd_layer(kind, sidx, i, T, TT, CH):
        j = i // 2
        P.curlabel = 'ssd_' + kind
        set_rings({'mm': [0, 1, 2], 'sc': [3, 4], 'acc': [5], 'tr': [6, 7]})
        LVL = getattr(cfg, 'lvl', 9)
        TAP2ENG = getattr(cfg, 'tap2eng', 'dve')
        NBUF = getattr(cfg, 'nbuf', 2)
        ccnt = [0]
        m0 = A.mark()
        sq = alloc_common(T, TT, need_hn2=False)
        NCH = TT // CH
        NT = T // TT
        hn = A.alloc([8, TT], BF16)
        w_dt = A.alloc([8, 32], BF16)
        w_out = [A.alloc([16, 128], BF16) for _ in range(2)]
        wo_n = [0]
        raw = A.alloc([2, TT + 3], F32)
        cacc_l = [A.alloc([TT], F32) for _ in range(2)]
        cexp_l = [A.alloc([TT], F32) for _ in range(2)]
        xs_l = [A.alloc([4, TT], BF16) for _ in range(2)]
        gcnt = [0]
        scnt = [0]
        ynT = A.alloc([16, TT], BF16)
        St = A.alloc([2048], F32)
        Sbf = A.alloc([2048], BF16)
        tails = A.alloc([3, 32], F32)
        cw = A.alloc([4, 32], F32)
        cb = A.alloc([32], F32)
        dtb = A.alloc([32], F32)
        Abc = A.alloc([32], F32)
        Dbc = A.alloc([32], F32)
        dtx = A.alloc([NCH, 32], F32)
        dta = A.alloc([NCH, 32], F32)
        dtl = A.alloc([NCH, 32], F32)
        dt_ = A.alloc([NCH, 32], F32)
        dtA = A.alloc([NCH, 32], F32)
        dtAh = A.alloc([NCH, 32], BF16)
        dtAl = A.alloc([NCH, 32], BF16)
        Acs = A.alloc([NCH, 32], F32)
        eA = A.alloc([NCH, 32], F32)
        dec = A.alloc([NCH, 32], F32)
        ddt = A.alloc([NCH, 32], F32)
        cdk = A.alloc([NCH, 32], F32)
        xB_l = [A.alloc([384], BF16) for _ in range(NBUF)]
        xdt_l = [A.alloc([4, 64], BF16) for _ in range(NBUF)]
        xdd_l = [A.alloc([4, 64], BF16) for _ in range(NBUF)]
        Rh_l = [A.alloc([4, CH], F32) for _ in range(NBUF)]
        Lm_l = [A.alloc([4, CH], BF16) for _ in range(NBUF)]
        MT_l = [A.alloc([4, CH], BF16) for _ in range(NBUF)]
        t1_l = [A.alloc([4, 64], F32) for _ in range(NBUF)]
        t2_l = [A.alloc([4, 64], F32) for _ in range(NBUF)]
        yv_l = [A.alloc([256], F32) for _ in range(NBUF)]
        ze_l = [A.alloc([256], F32) for _ in range(NBUF)]
        ss_l = [A.alloc([2], F32) for _ in range(NBUF)]
        yn_l = [A.alloc([256], BF16) for _ in range(NBUF)]
        stg = A.alloc([16, 128], F32) if kind == 's' else None
        tlT = A.alloc([128], F32)
        NEG32 = None
        if CH != 128:
            NEG32 = A.alloc([4 * CH], BF16)
            cp('dve', NEG32[0:CH].rearrange('p (r c) -> p r c', r=4, c=CH), NEGb[0:CH, :, 0:CH], ['cm'], ['neg32'])

        m_w = A.mark()
        w_in = [A.alloc([8, 768], BF16) for _ in range(2)]

        for kk in range(4):
            load_vec_cols(cw[:, kk, :], I['ssd_conv_w'][j, kk], 32, 'cw')
        load_vec_cols(cb, I['ssd_conv_b'][j], 32, 'cb')
        dma('sp', dtb, I['ssd_dt_bias'][j].partition_broadcast(128), [], ['dtb'])
        dma('sp', Abc, I['ssd_A_log'][j].partition_broadcast(128), [], ['Abc'])
        dma('sp', Dbc, I['ssd_D'][j].partition_broadcast(128), [], ['Dbc'])
        act(Abc, Abc, AF.Exp, ['Abc'], ['Abc'])
        ts('dve', Abc, Abc, -1.0, None, ALU.mult, None, ['Abc'], ['Abc'])
        dma('sp', w_dt, S_dt[j], [f'w_dt{j}'], ['w_dt'])
        load_ple_w(i)
        if kind == 'p':
            mset('pool', St, 0.0, [], ['St'])
            mset('pool', tails, 0.0, [], ['tails%d' % c_i for c_i in range(32)])
        else:
            src = I['state_ssm'][j, sidx].rearrange("h p n -> (h p) n").rearrange("(c r) n -> r c n", r=128)
            dma('sp', stg, src, [], ['stg'])
            for q4 in range(4):
                b = psum('tr')
                tr_multi([(PS[b][:, k * 128:(k + 1) * 128], stg[:, q4 * 4 + k, :]) for k in range(4)], identf, ['stg', 'identf'], [f'ps{b}'])
                cp('dve', St[:, q4 * 512:(q4 + 1) * 512], PS[b][:, 0:512], [f'ps{b}'], ['St'])
            for kk in range(3):
                load_vec_cols(tails[:, kk, :], I['cache_conv'][j, sidx, kk], 32, 'tailsall')
        cp('act', Sbf, St, ['St'], ['Sbf'])

        nload = [0]

        def load_w_in(g):
            k = nload[0] % 2
            nload[0] += 1
            dma('sp', w_in[k], S_in_ssd[j][:, :, g, :], [f'w_in_ssd{j}'], [f'w_in{k}'])
            return k

        pin = (I['p_prompt'] if kind == 'p' else I['p_sample'])[i, sidx]
        pending = load_w_in(0)
        for ti in range(NT):
            if LVL in (-3, -1):
                break
            tok0 = ti * TT
            rms_fm(sq, tok0, TT, hn, 'hn')
            DTL = getattr(cfg, 'dtl', 9)
            b = psum('sc')
            if DTL >= 1:
              mm_multi([(PS[b][0:CH, ch * 32:(ch + 1) * 32],
                         [(hn[:, kc, ch * CH:(ch + 1) * CH], w_dt[:, kc, :]) for kc in range(8)]) for ch in range(NCH)],
                       ['hn', 'w_dt'], [f'ps{b}'])
            c_ = slice(0, CH)
            pv = PS[b][0:CH, 0:NCH * 32].rearrange("p (c h) -> p c h", c=NCH, h=32)
            tt('dve', dtx[c_], pv, bc(dtb[c_].unsqueeze(1), [CH, NCH, 32]), ALU.add, [f'ps{b}', 'dtb'], ['dtx'])
            act(dta[c_], dtx[c_], AF.Abs, ['dtx'], ['dta'])
            act(dta[c_], dta[c_], AF.Exp, ['dta'], ['dta'], scale=-1.0)
            act(dtl[c_], dta[c_], AF.Ln, ['dta'], ['dtl'], bias=1.0, scale=1.0)
            ts('dve', dtx[c_], dtx[c_], 0.0, None, ALU.max, None, ['dtx'], ['dtx'])
            tt('dve', dt_[c_], dtx[c_], dtl[c_], ALU.add, ['dtx', 'dtl'], ['dt'])
            tt('dve', dtA[c_], dt_[c_], bc(Abc[c_].unsqueeze(1), [CH, NCH, 32]), ALU.mult, ['dt', 'Abc'], ['dtA'])
            cp('dve', dtAh[c_], dtA[c_], ['dtA'], ['dtAh'])
            tt('dve', dtAl[c_], dtA[c_], dtAh[c_], ALU.subtract, ['dtA', 'dtAh'], ['dtAl'])
            b1 = psum('sc')
            mm_multi([(PS[b1][0:CH, ch * 32:(ch + 1) * 32],
                       [(TU[0:CH, 0:CH], dtAh[c_, ch, :]), (TU[0:CH, 0:CH], dtAl[c_, ch, :])]) for ch in range(NCH)],
                     ['dtAh', 'dtAl', 'cm'], [f'ps{b1}'])
            b2 = psum('sc')
            mm_multi([(PS[b2][:, ch * 32:(ch + 1) * 32],
                       [(onesb[0:CH, :], dtAh[c_, ch, :]), (onesb[0:CH, :], dtAl[c_, ch, :])]) for ch in range(NCH)],
                     ['dtAh', 'dtAl', 'cm'], [f'ps{b2}'])
            pa = PS[b1][0:CH, 0:NCH * 32].rearrange("p (c h) -> p c h", c=NCH, h=32)
            pt_ = PS[b2][:, 0:NCH * 32].rearrange("p (c h) -> p c h", c=NCH, h=32)
            cp('dve', Acs[c_], pa, [f'ps{b1}'], ['Acs'])
            act(eA[c_], pa, AF.Exp, [f'ps{b1}'], ['eA'])
            tt('dve', dec[c_], pt_[0:CH], Acs[c_], ALU.subtract, [f'ps{b2}', 'Acs'], ['dec'])
            act(dec[c_], dec[c_], AF.Exp, ['dec'], ['dec'])
            tt('dve', ddt[c_], dec[c_], dt_[c_], ALU.mult, ['dec', 'dt'], ['ddt'])
            act(cdk, pt_, AF.Exp, [f'ps{b2}'], ['cdk'])

            for g in range(8):
                if LVL < 1:
                    break
                k = pending
                wv = w_in[k]
                xp = gcnt[0] % 2
                gcnt[0] += 1
                xs = xs_l[xp]
                XS = 'xs%d' % xp
                if not (ti == NT - 1 and g == 7):
                    pending = load_w_in((g + 1) % 8)
                for s_ in range(4):
                    cidx = (2 * g + s_) if s_ < 2 else (16 + g if s_ == 2 else 24 + g)
                    RW = 'raw%d' % (s_ % 2)
                    cpp = scnt[0] % 2
                    scnt[0] += 1
                    cacc = cacc_l[cpp]
                    cexp = cexp_l[cpp]
                    CA = 'cacc%d' % cpp
                    CE = 'cexp%d' % cpp
                    b = psum('mm')
                    mm(PS[b][:, 0:TT], [(wv[:, kc, s_ * 128:(s_ + 1) * 128], hn[:, kc, :]) for kc in range(8)],
                       [f'w_in{k}', 'hn'], [f'ps{b}'])
                    cp('pool', raw[:, s_ % 2, 0:3], tails[:, :, cidx], ['tails%d' % cidx], [RW])
                    cp('act', raw[:, s_ % 2, 3:3 + TT], PS[b][:, 0:TT], [f'ps{b}'], [RW])
                    cp('pool', tails[:, :, cidx], raw[:, s_ % 2, TT:TT + 3], [RW], ['tails%d' % cidx])
                    act(cacc, PS[b][:, 0:TT], AF.Identity, [f'ps{b}', 'cw', 'cb'], [CA], bias=cb[:, cidx:cidx + 1], scale=cw[:, 3, cidx:cidx + 1])
                    for kk, eng_ in ((0, 'dve'), (1, 'dve'), (2, TAP2ENG)):
                        stt(eng_, cacc, raw[:, s_ % 2, kk:kk + TT], cw[:, kk, cidx:cidx + 1], cacc, ALU.mult, ALU.add,
                            [RW, 'cw', CA], [CA])
                    act(cexp, cacc, AF.Exp, [CA], [CE], scale=-1.0)
                    act(cexp, cexp, AF.Ln, [CE], [CE], bias=1.0, scale=1.0)
                    act(cexp, cexp, AF.Exp, [CE], [CE], scale=-1.0)
                    tt('dve', xs[:, s_, :], cacc, cexp, ALU.mult, [CA, CE], [XS])
                hs = slice(4 * g, 4 * g + 4)
                for ch in range(NCH):
                    if LVL < 2:
                        break
                    pp = ccnt[0] % NBUF
                    ccnt[0] += 1
                    N_ = lambda s_, pp=pp: s_ + '_%d' % pp
                    xB = xB_l[pp]
                    xdt = xdt_l[pp]
                    xdd = xdd_l[pp]
                    Rh = Rh_l[pp]
                    Lm = Lm_l[pp]
                    MT = MT_l[pp]
                    t1 = t1_l[pp]
                    t2 = t2_l[pp]
                    yv = yv_l[pp]
                    ze = ze_l[pp]
                    ss = ss_l[pp]
                    yn = yn_l[pp]
                    c0 = ch * CH
                    bz = psum('mm')
                    mm(PS[bz][0:CH, 0:256], [(hn[:, kc, c0:c0 + CH], wv[:, kc, 512:768]) for kc in range(8)],
                       ['hn', f'w_in{k}'], [f'ps{bz}'])
                    btr = psum('tr')
                    pb = PS[btr][:].bitcast(BF16)
                    tr_multi([(pb[0:CH, s_ * 128:(s_ + 1) * 128], xs[:, s_, c0:c0 + CH]) for s_ in range(3)],
                             identb, [XS, 'cm'], [f'ps{btr}'])
                    cp('act', xB[c_], pb[0:CH, 0:384], [f'ps{btr}'], [N_('xB')])
                    x3 = xB[c_, 0:256].rearrange("p (r d) -> p r d", r=4, d=64)
                    tt('pool', xdt[c_], x3, bc(dt_[c_, ch, hs].unsqueeze(2), [CH, 4, 64]), ALU.mult, [N_('xB'), 'dt'], [N_('xdt')])
                    tt('pool', xdd[c_], x3, bc(ddt[c_, ch, hs].unsqueeze(2), [CH, 4, 64]), ALU.mult, [N_('xB'), 'ddt'], [N_('xdd')])
                    tt('dve', Rh[c_], bc(dtA[c_, ch, hs].unsqueeze(2), [CH, 4, CH]), bc(TU[0:CH, 0:CH].unsqueeze(1), [CH, 4, CH]),
                       ALU.mult, ['dtA', 'cm'], [N_('Rh')])
                    if LVL < 3:
                        continue
                    bs = psum('sc')
                    mm(PS[bs][0:CH, 0:4 * CH], [(TLf[0:CH, 0:CH], Rh[c_].rearrange("p r c -> p (r c)")),
                                                (identb[0:CH, 0:CH], NEGb[0:CH, :, 0:CH] if CH == 128 else NEG32[0:CH])],
                       [N_('Rh'), 'cm', 'identf'], [f'ps{bs}'])
                    act(Lm[c_], PS[bs][0:CH, 0:4 * CH].rearrange("p (r c) -> p r c", r=4, c=CH), AF.Exp, [f'ps{bs}'], [N_('Lm')])
                    bcb = psum('sc')
                    mm(PS[bcb][0:CH, 0:CH], [(xs[:, 2, c0:c0 + CH], xs[:, 3, c0:c0 + CH])], [XS], [f'ps{bcb}'])
                    tt('dve', MT[c_], Lm[c_], bc(PS[bcb][0:CH, 0:CH].unsqueeze(1), [CH, 4, CH]), ALU.mult, [N_('Lm'), f'ps{bcb}'], [N_('MT')])
                    by = psum('acc')
                    mm_multi([(PS[by][0:CH, r_ * 64:(r_ + 1) * 64], [(MT[c_, r_, :], xdt[c_, r_, :])]) for r_ in range(4)]
                             + [(PS[by][0:CH, 256:512], [(xs[:, 3, c0:c0 + CH], Sbf[:, g * 256:(g + 1) * 256])])],
                             [N_('MT'), N_('xdt'), XS, 'Sbf'], [f'ps{by}'])
                    yo3 = PS[by][0:CH, 256:512].rearrange("p (r d) -> p r d", r=4, d=64)
                    tt('dve', t1[c_], yo3, bc(eA[c_, ch, hs].unsqueeze(2), [CH, 4, 64]), ALU.mult, [f'ps{by}', 'eA'], [N_('t1')])
                    tt('pool', t2[c_], x3, bc(Dbc[c_, hs].unsqueeze(2), [CH, 4, 64]), ALU.mult, [N_('xB'), 'Dbc'], [N_('t2')])
                    tt('pool', t1[c_], t1[c_], t2[c_], ALU.add, [N_('t1'), N_('t2')], [N_('t1')])
                    tt('dve', yv[c_], PS[by][0:CH, 0:256], t1[c_].rearrange("p r d -> p (r d)"), ALU.add, [f'ps{by}', N_('t1')], [N_('yv')])
                    if LVL < 4:
                        continue
                    act(ze[c_], PS[bz][0:CH, 0:256], AF.Exp, [f'ps{bz}'], [N_('ze')], scale=-1.0)
                    act(ze[c_], ze[c_], AF.Ln, [N_('ze')], [N_('ze')], bias=1.0, scale=1.0)
                    act(ze[c_], ze[c_], AF.Exp, [N_('ze')], [N_('ze')], scale=-1.0)
                    tt('dve', ze[c_], PS[bz][0:CH, 0:256], ze[c_], ALU.mult, [f'ps{bz}', N_('ze')], [N_('ze')])
                    tt('dve', yv[c_], yv[c_], ze[c_], ALU.mult, [N_('yv'), N_('ze')], [N_('yv')])
                    act(ze[c_], yv[c_], AF.Square, [N_('yv')], [N_('ze'), N_('ss')], accum=ss[c_, 0:1])
                    act(ss[c_, 1:2], ss[c_, 0:1], AF.Ln, [N_('ss')], [N_('ss')], bias=EPS, scale=1.0 / 256)
                    act(ss[c_, 1:2], ss[c_, 1:2], AF.Exp, [N_('ss')], [N_('ss')], scale=-0.5)
                    ts('dve', yn[c_], yv[c_], ss[c_, 1:2], None, ALU.mult, None, [N_('yv'), N_('ss')], [N_('yn')])
                    bt2 = psum('tr')
                    pb2 = PS[bt2][:].bitcast(BF16)
                    tr_multi([(pb2[:, k2 * 128:k2 * 128 + CH], yn[c_, k2 * 128:(k2 + 1) * 128]) for k2 in range(2)],
                             identb[0:CH, 0:CH], [N_('yn'), 'cm'], [f'ps{bt2}'])
                    cp('act', ynT[:, 2 * g:2 * g + 2, c0:c0 + CH],
                       pb2[:, 0:256].rearrange("p (k c) -> p k c", k=2, c=128)[:, :, 0:CH], [f'ps{bt2}'], ['ynT'])
                    if LVL < 5:
                        continue
                    bst = psum('sc')
                    mm(PS[bst][:, 0:256], [(xB[c_, 256:384], xdd[c_].rearrange("p r d -> p (r d)"))], [N_('xB'), N_('xdd')], [f'ps{bst}'])
                    Sg = St[:, g * 256:(g + 1) * 256]
                    tt('pool', Sg.rearrange("p (r d) -> p r d", r=4, d=64), Sg.rearrange("p (r d) -> p r d", r=4, d=64),
                       bc(cdk[:, ch, hs].unsqueeze(2), [128, 4, 64]), ALU.mult, ['St', 'cdk'], ['St'])
                    tt('dve', Sg, Sg, PS[bst][:, 0:256], ALU.add, ['St', f'ps{bst}'], ['St'])
                    cp('pool', Sbf[:, g * 256:(g + 1) * 256], Sg, ['St'], ['Sbf'])
            set_rings({'mm': [0, 1, 2, 3, 4, 5], 'sc': [5], 'acc': [5], 'tr': [6, 7]})
            for jo in range(8):
                wk = wo_n[0] % 2
                wo_n[0] += 1
                dma('sp', w_out[wk], S_out_ssd[j][:, :, jo * 128:(jo + 1) * 128], [f'w_out_ssd{j}'], [f'w_out{wk}'])
                b = psum('mm')
                mm(PS[b][:, 0:TT], [(w_out[wk][:, kc, :], ynT[:, kc, :]) for kc in range(16)],
                   [f'w_out{wk}', 'ynT'], [f'ps{b}'])
                tt('dve', hT[:, jo, tok0:tok0 + TT], hT[:, jo, tok0:tok0 + TT], PS[b][:, 0:TT], ALU.add, [HT(jo, tok0), f'ps{b}'], [HT(jo, tok0)])
            ple_tile(sq, i, pin, tok0, TT, NCH, CH, hnbuf=hn, hntag='hn', ebufs=[(cexp_l[0], 'cexp0'), (cexp_l[1], 'cexp1')])
            set_rings({'mm': [0, 1, 2], 'sc': [3, 4], 'acc': [5], 'tr': [6, 7]})
        if LVL in (-3, -2):
            P.barrier()
            A.release(m0)
            return
        if kind == 'p':
            P.barrier()
            A.release(m_w)
            stg = A.alloc([16, 128], F32)
        oc = (O['conv_prompt'] if kind == 'p' else O['conv_sample'])[j, sidx]
        b = psum('tr')
        tr_multi([(PS[b][0:96, 0:128], tails.rearrange("p k c -> p (k c)"))], identf, ['tails%d' % c_i for c_i in range(32)] + ['identf'], [f'ps{b}'])
        cp('dve', tlT[0:96], PS[b][0:96, 0:128], [f'ps{b}'], ['tlT'])
        dma('pool', oc.rearrange("k (c p) -> (k c) p", p=128), tlT[0:96, :], ['tlT'], [])
        osm = (O['ssm_prompt'] if kind == 'p' else O['ssm_sample'])[j, sidx].rearrange("h p n -> (h p) n").rearrange("(c r) n -> r c n", r=128)
        for q4 in range(4):
            b = psum('tr')
            tr_multi([(PS[b][:, k2 * 128:(k2 + 1) * 128], St[:, (q4 * 4 + k2) * 128:(q4 * 4 + k2 + 1) * 128]) for k2 in range(4)],
                     identf, ['St', 'identf'], [f'ps{b}'])
            cp('dve', stg[:, q4 * 4:(q4 + 1) * 4, :], PS[b][:, 0:512].rearrange("p (k n) -> p k n", k=4, n=128), [f'ps{b}'], ['stg'])
        dma('pool', osm, stg, ['stg'], [])
        P.barrier()
        A.release(m0)

    def mla_layer(kind, sidx, i, T, TT, CH):
        j = i // 2
        P.curlabel = 'mla_p0_' + kind
        set_rings({'mm': [0, 1, 2], 'sc': [3, 4], 'acc': [5], 'tr': [6, 7]})
        m0 = A.mark()
        past = PAST if kind == 's' else 0
        TK = past + T
        NT = T // TT
        NCT = TT // CH
        nkt_past = past // 128
        nkt_new = T // CH
        NKT = nkt_past + nkt_new
        sg = A.alloc([8, T], BF16)
        qanT = A.alloc([3, T], BF16)
        latT = A.alloc([2, TK], BF16)
        krall = A.alloc([NKT, 32], BF16)
        kvg = A.alloc([256], F32)
        qng = A.alloc([64], F32)
        qrg = A.alloc([32], F32)
        kng = A.alloc([64], F32)
        krg = A.alloc([32], F32)
        rope = A.alloc([32], F32)
        junk = A.alloc([512], F32)
        ss = A.alloc([40], F32)
        mP0 = A.mark()
        sq = alloc_common(T, TT, need_hn2=False)
        hn = A.alloc([8, TT], BF16)
        w_in = A.alloc([8, MLA_IN], BF16)
        qab = A.alloc([384], BF16)
        kva = A.alloc([288], F32)
        lat = A.alloc([256], F32)
        latb = A.alloc([256], BF16)
        krn = A.alloc([32], F32)
        kro = A.alloc([32], F32)
        ge_l = [A.alloc([TT], F32) for _ in range(2)]
        pl = A.alloc([256], F32)
        pr = A.alloc([32], F32)

        dma('sp', w_in, S_in_mla[j], [f'w_in_mla{j}'], ['w_in'])
        dma('sp', kvg, I['mla_kv_a_norm'][j].partition_broadcast(128), [], ['kvg'])
        dma('sp', qng, I['mla_q_nope_norm'][j].partition_broadcast(128), [], ['qng'])
        dma('sp', qrg, I['mla_q_rope_norm'][j].partition_broadcast(128), [], ['qrg'])
        dma('sp', kng, I['mla_k_nope_norm'][j].partition_broadcast(128), [], ['kng'])
        dma('sp', krg, I['mla_k_rope_norm'][j].partition_broadcast(128), [], ['krg'])
        load_ple_w(i)
        olat = (O['lat_prompt'] if kind == 'p' else O['lat_sample'])[j, sidx]
        okr = (O['kr_prompt'] if kind == 'p' else O['kr_sample'])[j, sidx]
        rope_row0 = 0 if kind == 'p' else TP

        def rmsn(src, n, width, scale_col, tagr):
            c_ = slice(0, n)
            act(junk[c_, 0:width], src, AF.Square, tagr, ['junk', 'ss'], accum=ss[c_, scale_col:scale_col + 1])
            act(ss[c_, scale_col:scale_col + 1], ss[c_, scale_col:scale_col + 1], AF.Ln, ['ss'], ['ss'], bias=EPS, scale=1.0 / width)
            act(ss[c_, scale_col:scale_col + 1], ss[c_, scale_col:scale_col + 1], AF.Exp, ['ss'], ['ss'], scale=-0.5)

        def do_rope(dst, srcn, n, nh, tag_r, tag_w):
            c_ = slice(0, n)
            cos = bc(rope[c_, 0:16].unsqueeze(1), [n, nh, 16])
            sin = bc(rope[c_, 16:32].unsqueeze(1), [n, nh, 16])
            x1 = srcn[:, :, 0:16]
            x2 = srcn[:, :, 16:32]
            tA = junk[c_, 0:nh * 16].rearrange("p (h d) -> p h d", h=nh, d=16)
            tB = junk[c_, 256:256 + nh * 16].rearrange("p (h d) -> p h d", h=nh, d=16)
            tt('dve', tA, x1, cos, ALU.mult, tag_r + ['rope'], ['junk'])
            tt('dve', tB, x2, sin, ALU.mult, tag_r + ['rope'], ['junk'])
            tt('dve', dst[:, :, 0:16], tA, tB, ALU.subtract, ['junk'], tag_w)
            tt('dve', tA, x1, sin, ALU.mult, tag_r + ['rope'], ['junk'])
            tt('dve', tB, x2, cos, ALU.mult, tag_r + ['rope'], ['junk'])
            tt('dve', dst[:, :, 16:32], tA, tB, ALU.add, ['junk'], tag_w)

        if past:
            for kt in range(nkt_past):
                dma('sp', pl, I['cache_kv_latent'][j, sidx, kt * 128:(kt + 1) * 128, :], [], ['pl'])
                cp('pool', latb, pl, ['pl'], ['latb'])
                b = psum('tr')
                pb = PS[b][:].bitcast(BF16)
                tr_multi([(pb[:, k * 128:(k + 1) * 128], latb[:, k * 128:(k + 1) * 128]) for k in range(2)], identb, ['latb', 'cm'], [f'ps{b}'])
                cp('act', latT[:, :, kt * 128:(kt + 1) * 128], pb[:, 0:256].rearrange("p (k c) -> p k c", k=2, c=128), [f'ps{b}'], ['latT'])
                dma('sp', pr, I['cache_k_rope'][j, sidx, kt * 128:(kt + 1) * 128, :], [], ['pr'])
                cp('pool', krall[:, kt, :], pr, ['pr'], ['krall'])

        for ti in range(NT):
            tok0 = ti * TT
            rms_fm(sq, tok0, TT, hn, 'hn')
            for jo in range(8):
                ge = ge_l[jo % 2]
                GE = 'ge%d' % (jo % 2)
                b = psum('mm')
                mm(PS[b][:, 0:TT], [(w_in[:, kc, 672 + jo * 128:672 + (jo + 1) * 128], hn[:, kc, :]) for kc in range(8)],
                   ['w_in', 'hn'], [f'ps{b}'])
                act(ge[:, 0:TT], PS[b][:, 0:TT], AF.Exp, [f'ps{b}'], [GE], scale=-1.0)
                act(ge[:, 0:TT], ge[:, 0:TT], AF.Ln, [GE], [GE], bias=1.0, scale=1.0)
                act(ge[:, 0:TT], ge[:, 0:TT], AF.Exp, [GE], [GE], scale=-1.0)
                tt('dve', sg[:, jo, tok0:tok0 + TT], PS[b][:, 0:TT], ge[:, 0:TT], ALU.mult, [f'ps{b}', GE], ['sg'])
            for ct in range(NCT):
                c0 = ct * CH
                t0 = tok0 + c0
                c_ = slice(0, CH)
                kt_idx = nkt_past + (t0 // CH)
                dma('sp', rope[c_], I['c_rope'][rope_row0 + t0:rope_row0 + t0 + CH, :], [], ['rope'])
                b1 = psum('mm')
                mm(PS[b1][0:CH, 0:384], [(hn[:, kc, c0:c0 + CH], w_in[:, kc, 0:384]) for kc in range(8)], ['hn', 'w_in'], [f'ps{b1}'])
                b2 = psum('mm')
                mm(PS[b2][0:CH, 0:288], [(hn[:, kc, c0:c0 + CH], w_in[:, kc, 384:672]) for kc in range(8)], ['hn', 'w_in'], [f'ps{b2}'])
                rmsn(PS[b1][0:CH, 0:384], CH, 384, 0, [f'ps{b1}'])
                ts('dve', qab[c_], PS[b1][0:CH, 0:384], ss[c_, 0:1], None, ALU.mult, None, [f'ps{b1}', 'ss'], ['qab'])
                bt = psum('tr')
                pb = PS[bt][:].bitcast(BF16)
                tr_multi([(pb[:, k * 128:k * 128 + CH], qab[c_, k * 128:(k + 1) * 128]) for k in range(3)], identb[0:CH, 0:CH],
                         ['qab', 'cm'], [f'ps{bt}'])
                cp('act', qanT[:, :, t0:t0 + CH], pb[:, 0:384].rearrange("p (k c) -> p k c", k=3, c=128)[:, :, 0:CH], [f'ps{bt}'], ['qanT'])
                cp('act', kva[c_], PS[b2][0:CH, 0:288], [f'ps{b2}'], ['kva'])
                rmsn(kva[c_, 0:256], CH, 256, 1, ['kva'])
                stt('dve', lat[c_], kva[c_, 0:256], ss[c_, 1:2], kvg[c_], ALU.mult, ALU.mult, ['kva', 'ss', 'kvg'], ['lat'])
                dma('pool', olat[t0:t0 + CH, :], lat[c_], ['lat'], [])
                cp('pool', latb[c_], lat[c_], ['lat'], ['latb'])
                bt2 = psum('tr')
                pb2 = PS[bt2][:].bitcast(BF16)
                tr_multi([(pb2[:, k * 128:k * 128 + CH], latb[c_, k * 128:(k + 1) * 128]) for k in range(2)], identb[0:CH, 0:CH],
                         ['latb', 'cm'], [f'ps{bt2}'])
                cp('act', latT[:, :, past + t0:past + t0 + CH], pb2[:, 0:256].rearrange("p (k c) -> p k c", k=2, c=128)[:, :, 0:CH],
                   [f'ps{bt2}'], ['latT'])
                rmsn(kva[c_, 256:288], CH, 32, 2, ['kva'])
                stt('dve', krn[c_], kva[c_, 256:288], ss[c_, 2:3], krg[c_], ALU.mult, ALU.mult, ['kva', 'ss', 'krg'], ['krn'])
                do_rope(kro[c_].unsqueeze(1), krn[c_].unsqueeze(1), CH, 1, ['krn'], ['kro'])
                dma('pool', okr[t0:t0 + CH, :], kro[c_], ['kro'], [])
                cp('pool', krall[c_, kt_idx, :], kro[c_], ['kro'], ['krall'])
        P.barrier()
        P.curlabel = 'mla_grp_' + kind
        set_rings({'mm': [0, 1], 'sc': [2, 3, 4], 'acc': [5, 6], 'tr': [7]})
        A.release(mP0)
        G = getattr(cfg, 'mla_g', 4)
        NKB = getattr(cfg, 'mla_kbuf', 1)
        m1 = A.mark()
        wqb = [A.alloc([3, G * 96], BF16) for _ in range(2)]
        wkvb = [A.alloc([2, G * 128], BF16) for _ in range(2)]
        KT_l = [A.alloc([G, TK], BF16) for _ in range(NKB)]
        vfull_l = [A.alloc([NKT, G, 128], BF16) for _ in range(NKB)]
        mcnt = [0]
        qcnt = [0]
        QT_l = [A.alloc([G, TT], BF16) for _ in range(2)]
        kvs_l = [A.alloc([G, 128], F32) for _ in range(2)]
        kfull_l = [A.alloc([G, 96], BF16) for _ in range(2)]
        qs_l = [A.alloc([G, 96], F32) for _ in range(2)]
        qn_l = [A.alloc([G, 96], F32) for _ in range(2)]
        qfull_l = [A.alloc([G, 96], BF16) for _ in range(2)]
        ssq_l = [A.alloc([G, 96], F32) for _ in range(2)]
        rs8_l = [A.alloc([2 * G], F32) for _ in range(2)]
        PT = [A.alloc([TT], BF16) for _ in range(2)]
        rden = A.alloc([TT], F32)
        for kb_ in range(NKB):
            for r_ in range(G):
                if r_ % 2 == 0:
                    mset('pool', vfull_l[kb_][:, :, r_, 64:128], 1.0, [], ['vfull%d' % kb_])
                else:
                    mset('pool', vfull_l[kb_][:, :, r_, 0:64], 1.0, [], ['vfull%d' % kb_])

        def keytile_len(kt):
            return 128 if kt < nkt_past else CH

        def keytile_off(kt):
            return kt * 128 if kt < nkt_past else past + (kt - nkt_past) * CH

        nl = [0]

        def load_grp_w(gi):
            k = nl[0] % 2
            nl[0] += 1
            dma('sp', wqb[k], S_qb[j][:, :, gi * G * 96:(gi + 1) * G * 96], [f'w_qb{j}'], [f'wqb{k}'])
            dma('sp', wkvb[k], S_kvb[j][:, :, gi * G * 128:(gi + 1) * G * 128], [f'w_kvb{j}'], [f'wkvb{k}'])
            return k

        pend = load_grp_w(0)
        for gi in range(NH // G):
            k = pend
            if gi + 1 < NH // G:
                pend = load_grp_w(gi + 1)
            KT = KT_l[gi % NKB]
            vfull = vfull_l[gi % NKB]
            KTN = 'KT%d' % (gi % NKB)
            VFN = 'vfull%d' % (gi % NKB)
            for kt in range(NKT):
                mp = mcnt[0] % 2
                mcnt[0] += 1
                M_ = lambda s_, mp=mp: s_ + '_%d' % mp
                kvs = kvs_l[mp]
                kfull = kfull_l[mp]
                qs = qs_l[mp]
                qn = qn_l[mp]
                qfull = qfull_l[mp]
                ssq = ssq_l[mp]
                rs8 = rs8_l[mp]
                n = keytile_len(kt)
                off = keytile_off(kt)
                c_ = slice(0, n)
                b = psum('mm')
                mm(PS[b][0:n, 0:G * 128], [(latT[:, kc, off:off + n], wkvb[k][:, kc, :]) for kc in range(2)], ['latT', f'wkvb{k}'], [f'ps{b}'])
                cp('dve', kvs[c_], PS[b][0:n, 0:G * 128].rearrange("p (r d) -> p r d", r=G, d=128), [f'ps{b}'], [M_('kvs')])
                for r_ in range(G):
                    lo = 0 if r_ % 2 == 0 else 64
                    cp('pool', vfull[c_, kt, r_, lo:lo + 64], kvs[c_, r_, 64:128], [M_('kvs')], [VFN])
                tt('dve', ssq[c_, :, 0:64], kvs[c_, :, 0:64], kvs[c_, :, 0:64], ALU.mult, [M_('kvs')], [M_('ssq')])
                P.op('dve', lambda e, o=rs8[c_, 0:G], i_=ssq[c_, :, 0:64]: e.tensor_reduce(out=o, in_=i_, axis=AX.X, op=ALU.add), [M_('ssq')], [M_('rs8')])
                act(rs8[c_, 0:G], rs8[c_, 0:G], AF.Ln, [M_('rs8')], [M_('rs8')], bias=EPS, scale=1.0 / 64)
                act(rs8[c_, 0:G], rs8[c_, 0:G], AF.Exp, [M_('rs8')], [M_('rs8')], scale=-0.5)
                tt('dve', ssq[c_, :, 0:64], kvs[c_, :, 0:64], bc(rs8[c_, 0:G].unsqueeze(2), [n, G, 64]), ALU.mult, [M_('kvs'), M_('rs8')], [M_('ssq')])
                tt('dve', kfull[c_, :, 0:64], ssq[c_, :, 0:64], bc(kng[c_].unsqueeze(1), [n, G, 64]), ALU.mult, [M_('ssq'), 'kng'], [M_('kfull')])
                cp('pool', kfull[c_, :, 64:96], bc(krall[c_, kt, :].unsqueeze(1), [n, G, 32]), ['krall'], [M_('kfull')])
                bt = psum('tr')
                pb = PS[bt][:].bitcast(BF16)
                tr_multi([(pb[0:96, r_ * 128:r_ * 128 + n], kfull[c_, r_, :]) for r_ in range(G)], identb[0:n, 0:n], [M_('kfull'), 'cm'], [f'ps{bt}'])
                cp('act', KT[0:96, :, off:off + n], pb[0:96, 0:G * 128].rearrange("p (r c) -> p r c", r=G, c=128)[:, :, 0:n], [f'ps{bt}'], [KTN])
            for ti in range(NT):
                tok0 = ti * TT
                qp = qcnt[0] % 2
                qcnt[0] += 1
                QT = QT_l[qp]
                QTN = 'QT%d' % qp
                for ct in range(NCT):
                    mp = mcnt[0] % 2
                    mcnt[0] += 1
                    M_ = lambda s_, mp=mp: s_ + '_%d' % mp
                    kvs = kvs_l[mp]
                    kfull = kfull_l[mp]
                    qs = qs_l[mp]
                    qn = qn_l[mp]
                    qfull = qfull_l[mp]
                    ssq = ssq_l[mp]
                    rs8 = rs8_l[mp]
                    c0 = ct * CH
                    t0 = tok0 + c0
                    c_ = slice(0, CH)
                    dma('sp', rope[c_], I['c_rope'][rope_row0 + t0:rope_row0 + t0 + CH, :], [], ['rope'])
                    b = psum('mm')
                    mm(PS[b][0:CH, 0:G * 96], [(qanT[:, kc, t0:t0 + CH], wqb[k][:, kc, :]) for kc in range(3)], ['qanT', f'wqb{k}'], [f'ps{b}'])
                    cp('dve', qs[c_], PS[b][0:CH, 0:G * 96].rearrange("p (r d) -> p r d", r=G, d=96), [f'ps{b}'], [M_('qs')])
                    tt('dve', ssq[c_], qs[c_], qs[c_], ALU.mult, [M_('qs')], [M_('ssq')])
                    P.op('dve', lambda e, o=rs8[c_, 0:G], i_=ssq[c_, :, 0:64]: e.tensor_reduce(out=o, in_=i_, axis=AX.X, op=ALU.add), [M_('ssq')], [M_('rs8')])
                    P.op('dve', lambda e, o=rs8[c_, G:2 * G], i_=ssq[c_, :, 64:96]: e.tensor_reduce(out=o, in_=i_, axis=AX.X, op=ALU.add), [M_('ssq')], [M_('rs8')])
                    act(rs8[c_, 0:G], rs8[c_, 0:G], AF.Ln, [M_('rs8')], [M_('rs8')], bias=EPS, scale=1.0 / 64)
                    act(rs8[c_, G:2 * G], rs8[c_, G:2 * G], AF.Ln, [M_('rs8')], [M_('rs8')], bias=EPS, scale=1.0 / 32)
                    act(rs8[c_], rs8[c_], AF.Exp, [M_('rs8')], [M_('rs8')], scale=-0.5)
                    tt('dve', qn[c_, :, 0:64], qs[c_, :, 0:64], bc(rs8[c_, 0:G].unsqueeze(2), [CH, G, 64]), ALU.mult, [M_('qs'), M_('rs8')], [M_('qn')])
                    tt('dve', qn[c_, :, 64:96], qs[c_, :, 64:96], bc(rs8[c_, G:2 * G].unsqueeze(2), [CH, G, 32]), ALU.mult, [M_('qs'), M_('rs8')], [M_('qn')])
                    tt('dve', qfull[c_, :, 0:64], qn[c_, :, 0:64], bc(qng[c_].unsqueeze(1), [CH, G, 64]), ALU.mult, [M_('qn'), 'qng'], [M_('qfull')])
                    tt('dve', qn[c_, :, 64:96], qn[c_, :, 64:96], bc(qrg[c_].unsqueeze(1), [CH, G, 32]), ALU.mult, [M_('qn'), 'qrg'], [M_('qn')])
                    do_rope(qfull[c_, :, 64:96], qn[c_, :, 64:96], CH, G, [M_('qn')], [M_('qfull')])
                    bt = psum('tr')
                    pb = PS[bt][:].bitcast(BF16)
                    tr_multi([(pb[0:96, r_ * 128:r_ * 128 + CH], qfull[c_, r_, :]) for r_ in range(G)], identb[0:CH, 0:CH], [M_('qfull'), 'cm'], [f'ps{bt}'])
                    cp('act', QT[0:96, :, c0:c0 + CH], pb[0:96, 0:G * 128].rearrange("p (r c) -> p r c", r=G, c=128)[:, :, 0:CH], [f'ps{bt}'], [QTN])
                if kind == 'p':
                    nvis = (tok0 + TT) // 128
                else:
                    nvis = NKT
                for r_ in range(G):
                    h = gi * G + r_
                    bo = psum('acc')
                    npv = 0
                    for kt in range(nvis):
                        n = keytile_len(kt)
                        off = keytile_off(kt)
                        cs = 0
                        if kind == 'p' and kt * 128 >= tok0:
                            cs = kt * 128 - tok0
                        b = psum('sc')
                        mm(PS[b][0:n, cs:TT], [(KT[0:96, r_, off:off + n], QT[0:96, r_, cs:TT])], [KTN, QTN], [f'ps{b}'])
                        pk = npv % 2
                        act(PT[pk][0:n, cs:TT], PS[b][0:n, cs:TT], AF.Exp, [f'ps{b}'], [f'PT{pk}'], scale=SCALE)
                        if kind == 'p' and kt * 128 >= tok0:
                            mset('pool', PT[pk][64:128, cs:cs + 64], 0.0, [], [f'PT{pk}'])
                        first = (kt == 0)
                        last = (kt == nvis - 1)
                        P.op('pe', lambda e, o=PS[bo][:, cs:TT], l_=vfull[0:n, kt, r_, :], rr=PT[pk][0:n, cs:TT], f=first, la=last:
                             e.matmul(o, lhsT=l_, rhs=rr, start=f, stop=la), [f'PT{pk}', VFN], [f'ps{bo}'])
                        npv += 1
                    lo, dn = (0, 64) if r_ % 2 == 0 else (64, 0)
                    act(rden[lo:lo + 64, 0:TT], PS[bo][dn:dn + 64, 0:TT], AF.Ln, [f'ps{bo}'], ['rden'])
                    act(rden[lo:lo + 64, 0:TT], rden[lo:lo + 64, 0:TT], AF.Exp, ['rden'], ['rden'], scale=-1.0)
                    tt('dve', rden[lo:lo + 64, 0:TT], PS[bo][lo:lo + 64, 0:TT], rden[lo:lo + 64, 0:TT], ALU.mult, [f'ps{bo}', 'rden'], ['rden'])
                    tt('pool', sg[lo:lo + 64, h // 2, tok0:tok0 + TT], sg[lo:lo + 64, h // 2, tok0:tok0 + TT], rden[lo:lo + 64, 0:TT],
                       ALU.mult, ['sg', 'rden'], ['sg'])
        P.barrier()
        P.curlabel = 'mla_end_' + kind
        set_rings({'mm': [0, 1, 2, 3], 'sc': [4], 'acc': [5], 'tr': [6, 7]})
        A.release(mP0)
        sq = alloc_common(T, TT)
        wout = A.alloc([8, D], BF16)
        dma('sp', wout, S_out_mla[j], [f'w_out_mla{j}'], ['wout'])
        pin = (I['p_prompt'] if kind == 'p' else I['p_sample'])[i, sidx]
        for ti in range(NT):
            tok0 = ti * TT
            for jo in range(8):
                b = psum('mm')
                mm(PS[b][:, 0:TT], [(wout[:, kc, jo * 128:(jo + 1) * 128], sg[:, kc, tok0:tok0 + TT]) for kc in range(8)], ['wout', 'sg'], [f'ps{b}'])
                tt('dve', hT[:, jo, tok0:tok0 + TT], hT[:, jo, tok0:tok0 + TT], PS[b][:, 0:TT], ALU.add, [HT(jo, tok0), f'ps{b}'], [HT(jo, tok0)])
            ple_tile(sq, i, pin, tok0, TT, NCT, CH)
        P.barrier()
        A.release(m0)

    def run_seq(kind, sidx):
        P.curlabel = 'io_' + kind
        T = TP if kind == 'p' else TS
        TT = min(cfg.tt, T)
        TTG[0] = TT
        CH = min(128, T)
        xin = (I['x_prompt'] if kind == 'p' else I['x_sample'])[sidx]
        yout_d = (O['y_prompt'] if kind == 'p' else O['y_sample'])[sidx]
        m0 = A.mark()
        xt = [A.alloc([D], F32) for _ in range(2)]
        for t_ in range(T // CH):
            k = t_ % 2
            dma('sp', xt[k][0:CH], xin[t_ * CH:(t_ + 1) * CH, :], [], [f'xt{k}'])
            for hf in range(2):
                b = psum('tr')
                tr_multi([(PS[b][:, q * 128:q * 128 + CH], xt[k][0:CH, (hf * 4 + q) * 128:(hf * 4 + q + 1) * 128]) for q in range(4)],
                         identf[0:CH, 0:CH], [f'xt{k}', 'identf'], [f'ps{b}'])
                cp('act' if hf else 'dve', hT[:, hf * 4:(hf + 1) * 4, t_ * CH:(t_ + 1) * CH],
                   PS[b][:, 0:512].rearrange("p (q c) -> p q c", q=4, c=128)[:, :, 0:CH], [f'ps{b}'], [HT(hf * 4 + q, t_ * CH) for q in range(4)])
        P.barrier()
        A.release(m0)
        for i in range(cfg.depth):
            if getattr(cfg, 'skip_layers', False):
                continue
            if i % 2 == 0:
                if not getattr(cfg, 'skip_ssd', False):
                    ssd_layer(kind, sidx, i, T, TT, CH)
            else:
                if not getattr(cfg, 'skip_mla', False):
                    mla_layer(kind, sidx, i, T, TT, CH)
        m0 = A.mark()
        yt = [A.alloc([D], F32) for _ in range(2)]
        for t_ in range(T // CH):
            k = t_ % 2
            for hf in range(2):
                b = psum('tr')
                tr_multi([(PS[b][0:CH, q * 128:(q + 1) * 128], hT[:, hf * 4 + q, t_ * CH:(t_ + 1) * CH]) for q in range(4)],
                         identf, [HT(hf * 4 + q, t_ * CH) for q in range(4)] + ['identf'], [f'ps{b}'])
                cp('act' if hf else 'dve', yt[k][0:CH, hf * 512:(hf + 1) * 512], PS[b][0:CH, 0:512], [f'ps{b}'], [f'yt{k}'])
            dma('pool', yout_d[t_ * CH:(t_ + 1) * CH, :], yt[k][0:CH], [f'yt{k}'], [])
        P.barrier()
        A.release(m0)

    for s_i in range(cfg.nseq_p):
        if not getattr(cfg, 'skip_p', False):
            run_seq('p', s_i)
    for s_i in range(cfg.nseq_s):
        if not getattr(cfg, 'skip_s', False):
            run_seq('s', s_i)
    P.barrier()

    P.finalize(reorder=getattr(cfg, 'reorder', True))
    semnames = sorted(P.cnt.keys())
    sem_ctx = {k: nc.semaphore("s_" + k) for k in semnames}
    sems = {k: c.__enter__() for k, c in sem_ctx.items()}
    engobj = {'pe': 'tensor', 'act': 'scalar', 'dve': 'vector', 'pool': 'gpsimd', 'sp': 'sync'}
    with nc.Block() as block:
        def emit(engname):
            def body(e):
                for item in P.streams[engname]:
                    if item[0] == 'wait':
                        e.wait_ge(sems[item[1]], item[2])
                    else:
                        _, fn, key, inc = item[:4]
                        fn(e).then_inc(sems[key], inc)
            return body
        block.tensor(emit('pe'))
        block.scalar(emit('act'))
        block.vector(emit('dve'))
        block.gpsimd(emit('pool'))
        block.sync(emit('sp'))
    for c in sem_ctx.values():
        c.__exit__(None, None, None)
    for t in reversed(ctx):
        t.__exit__(None, None, None)
    stats = {e: len(P.streams[e]) for e in ENGS}
    build_program.planner = P
    return nc, stats, A.peak


def make_consts(cfg):
    a = np.arange(128)
    identf = np.zeros((128, 2, 128), np.float32)
    identf[:, 0, :] = np.eye(128)
    identf[:, 1, :] = (a[:, None] > a[None, :])
    mats = np.zeros((128, 8, 128), np.float32)
    mats[:, 4:8, :] = np.where(a[None, :] < a[:, None], -30000.0, 0.0)[:, None, :]
    mats[:, 0, :] = np.eye(128)
    mats[:, 1, :] = (a[:, None] <= a[None, :])
    mats[:, 2, :] = (a[:, None] > a[None, :])
    mats[:, 3, :] = 1.0
    pos = np.concatenate([np.arange(cfg.t_p), cfg.past + np.arange(cfg.t_s)]).astype(np.float32)
    inv = (1.0 / (10000.0 ** (np.arange(0, 32, 2, dtype=np.float32) / 32.0))).astype(np.float32)
    ang = pos[:, None] * inv[None, :]
    rope = np.concatenate([np.cos(ang), np.sin(ang)], axis=1).astype(np.float32)
    return {'c_identf': identf, 'c_mats': mats.astype(ml_dtypes.bfloat16), 'c_rope': rope}


_BATCH_P = ['x_prompt']
_WEIGHTS = ['ln_w', 'ssd_in_w', 'ssd_conv_w', 'ssd_conv_b', 'ssd_dt_bias', 'ssd_A_log', 'ssd_D', 'ssd_norm_w', 'ssd_out_w',
            'mla_in_w', 'mla_q_a_norm', 'mla_q_b_w', 'mla_kv_a_norm', 'mla_kv_b_w', 'mla_q_nope_norm', 'mla_q_rope_norm',
            'mla_k_nope_norm', 'mla_k_rope_norm', 'mla_out_w', 'ple_up_w', 'ple_norm_w', 'ple_gate_w']


def shard_inputs(inputs, cfg, ncores):
    consts = make_consts(cfg)
    maps = []
    f = lambda a: np.ascontiguousarray(np.asarray(a, dtype=np.float32))
    for c in range(ncores):
        p0, p1 = c * cfg.nseq_p, (c + 1) * cfg.nseq_p
        s0, s1 = c * cfg.nseq_s, (c + 1) * cfg.nseq_s
        m = {
            'x_prompt': f(inputs['x_prompt'][p0:p1]),
            'x_sample': f(inputs['x_sample'][s0:s1]),
            'cache_conv': f(inputs['cache_conv'][:, s0:s1]),
            'state_ssm': f(inputs['state_ssm'][:, s0:s1]),
            'cache_kv_latent': f(inputs['cache_kv_latent'][:, s0:s1]),
            'cache_k_rope': f(inputs['cache_k_rope'][:, s0:s1]),
            'p_prompt': f(inputs['p_prompt'][:, p0:p1]),
            'p_sample': f(inputs['p_sample'][:, s0:s1]),
        }
        for wn in _WEIGHTS:
            m[wn] = f(inputs[wn])
        m.update(consts)
        maps.append(m)
    return maps


def gather_outputs(results, cfg):
    cat = lambda k, ax: np.concatenate([np.asarray(r[k], dtype=np.float32) for r in results], axis=ax)
    return (cat('y_prompt', 0), cat('y_sample', 0), cat('conv_prompt', 1), cat('ssm_prompt', 1),
            cat('lat_prompt', 1), cat('kr_prompt', 1), cat('conv_sample', 1), cat('ssm_sample', 1),
            cat('lat_sample', 1), cat('kr_sample', 1))


def kernel(**inputs):
    cfg = Cfg()
    ncores = 8
    nc, stats, peak = build_program(cfg)
    maps = shard_inputs(inputs, cfg, ncores)
    res = run_bass_kernel_spmd(nc, maps, core_ids=list(range(ncores)))
    return gather_outputs(res.results, cfg)
```

```python
import math
import numpy as np
import ml_dtypes
import concourse.bass as bass
import concourse.mybir as mybir
from concourse.bass_utils import run_bass_kernel_spmd

F32 = mybir.dt.float32
BF16 = mybir.dt.bfloat16
ALU = mybir.AluOpType
AF = mybir.ActivationFunctionType
AX = mybir.AxisListType

EPS = 1e-6
D = 1024
NKC = 8
SSD_IN = 6176
MLA_IN = 1696
NH = 16
SCALE = 96 ** -0.5


class Cfg:
    def __init__(self, nseq_p=4, t_p=2048, nseq_s=2, t_s=32, past=2048, tt=512, depth=4):
        self.nseq_p, self.t_p, self.nseq_s, self.t_s, self.past, self.tt, self.depth = nseq_p, t_p, nseq_s, t_s, past, tt, depth
        self.n_ssd = (depth + 1) // 2
        self.n_mla = depth // 2


ENGS = ['pe', 'act', 'dve', 'pool', 'sp']
NDSEM = 16


class Planner:
    WINDOW = 160
    LAT = 120.0

    def __init__(self):
        self.ops = []
        self.res = {}
        self.segs = [0]
        self.streams = {e: [] for e in ENGS}
        self.cnt = {}
        self.known = {e: {} for e in ENGS}
        self.ndma = {}
        self.finalized = False
        self.labels = {}
        self.wnames = []
        self.times = {}
        self.curlabel = 'start'

    def op(self, eng, fn, r=(), w=(), dma=False, cost=100.0):
        w = list(w) + [n for n in r if n.startswith('ps')]
        r = [n for n in r if not n.startswith('ps')]
        lo = self.segs[-1]
        deps = set()
        for n in r:
            st = self.res.get(n)
            if st and st[0] is not None and st[0] >= lo:
                deps.add(st[0])
        for n in w:
            st = self.res.get(n)
            if st:
                if st[0] is not None and st[0] >= lo:
                    deps.add(st[0])
                for x in st[1]:
                    if x >= lo:
                        deps.add(x)
        i = len(self.ops)
        if i == self.segs[-1]:
            self.labels[i] = self.curlabel
        self.ops.append((eng, fn, sorted(deps), dma, cost))
        self.wnames.append((tuple(w), tuple(r)))
        for n in r:
            self.res.setdefault(n, [None, []])[1].append(i)
        for n in w:
            self.res[n] = [i, []]
        return i

    def barrier(self):
        if self.segs[-1] != len(self.ops):
            self.segs.append(len(self.ops))

    def _schedule(self, lo, hi, reorder=True):
        ops = self.ops
        by = {e: [] for e in ENGS}
        for i in range(lo, hi):
            by[ops[i][0]].append(i)
        if not reorder:
            return by
        W, LAT = self.WINDOW, self.LAT
        finish = {}
        sched = set()
        t = {e: 0.0 for e in ENGS}
        head = {e: 0 for e in ENGS}
        order = {e: [] for e in ENGS}
        cache = {e: None for e in ENGS}
        blocked = {e: set() for e in ENGS}
        left = hi - lo

        def scan(e):
            lst = by[e]
            i = head[e]
            c = 0
            best = None
            blk = set()
            te = t[e]
            n = len(lst)
            while i < n and c < W:
                oid = lst[i]
                if oid in sched:
                    i += 1
                    continue
                c += 1
                ok = True
                rdy = te
                for d in ops[oid][2]:
                    f = finish.get(d)
                    if f is None:
                        ok = False
                        blk.add(d)
                        break
                    if ops[d][0] == 'pe' and e == 'pe' and not ops[d][3]:
                        f2 = f - LAT
                    else:
                        f2 = f
                    if f2 > rdy:
                        rdy = f2
                if ok:
                    if best is None or rdy < best[0]:
                        best = (rdy, i, oid)
                    if rdy <= te:
                        break
                i += 1
            blocked[e] = blk
            cache[e] = best if best is not None else 'none'

        while left:
            bestc = None
            for e in ENGS:
                if head[e] >= len(by[e]):
                    continue
                if cache[e] is None:
                    scan(e)
                c = cache[e]
                if c == 'none':
                    continue
                if bestc is None or c[0] < bestc[0][0]:
                    bestc = (c, e)
            assert bestc is not None, "scheduler stuck"
            (rdy, idx, oid), e = bestc
            eng, fn, deps, dma, cost = ops[oid]
            if dma:
                t[e] = rdy + 60.0
                finish[oid] = rdy + cost + LAT
            else:
                t[e] = rdy + cost
                finish[oid] = t[e] + LAT
            sched.add(oid)
            order[e].append(oid)
            self.times[oid] = (rdy, t[e], lo)
            lst = by[e]
            h = head[e]
            while h < len(lst) and lst[h] in sched:
                h += 1
            head[e] = h
            cache[e] = None
            for e2 in ENGS:
                if e2 != e and oid in blocked[e2]:
                    cache[e2] = None
            left -= 1
        self.est = max(self.est if hasattr(self, 'est') else 0.0, 0.0)
        self.seg_time = getattr(self, 'seg_time', 0.0) + max(t.values())
        busy = {e2: sum(ops[i][4] for i in by[e2] if not ops[i][3]) for e2 in ENGS}
        self.seg_log = getattr(self, 'seg_log', [])
        self.seg_log.append((self.labels.get(lo, '?'), max(t.values()), busy))
        return order

    def finalize(self, reorder=True):
        if self.finalized:
            return
        self.finalized = True
        ops = self.ops
        bounds = self.segs + [len(ops)]
        ev = {}
        for si in range(len(bounds) - 1):
            lo, hi = bounds[si], bounds[si + 1]
            if lo == hi:
                continue
            order = self._schedule(lo, hi, reorder)
            pre_wait = {}
            for e in ENGS:
                for oid in order[e]:
                    dma = ops[oid][3]
                    if dma:
                        nq = self.ndma.get(e, 0)
                        self.ndma[e] = nq + 1
                        key = 'd_%s_%d' % (e, nq % NDSEM)
                        prev = self.cnt.get(key, 0)
                        if prev:
                            pre_wait[oid] = (key, prev)
                        self.cnt[key] = prev + 16
                    else:
                        key = e
                        self.cnt[key] = self.cnt.get(key, 0) + 1
                    ev[oid] = (key, self.cnt[key])
            for e in ENGS:
                kn = self.known[e]
                st = self.streams[e]
                for oid in order[e]:
                    eng, fn, deps, dma, cost = ops[oid]
                    need = {}
                    for d in deps:
                        if ops[d][0] == 'pe' and e == 'pe' and not ops[d][3] and not dma:
                            continue
                        k, v = ev[d]
                        if need.get(k, 0) < v:
                            need[k] = v
                    pw = pre_wait.get(oid)
                    if pw and need.get(pw[0], 0) < pw[1]:
                        need[pw[0]] = pw[1]
                    for k, v in need.items():
                        if kn.get(k, 0) >= v:
                            continue
                        kn[k] = v
                        st.append(('wait', k, v))
                    k, v = ev[oid]
                    st.append(('op', fn, k, 16 if dma else 1, cost, v))
            for e in ENGS:
                kn = self.known[e]
                for k, v in self.cnt.items():
                    if kn.get(k, 0) < v:
                        kn[k] = v
                        self.streams[e].append(('wait', k, v))


class Arena:
    def __init__(self, ap_f32, nwords):
        self.ap = ap_f32
        self.n = nwords
        self.off = 0
        self.peak = 0

    def mark(self):
        return self.off

    def release(self, m):
        self.off = m

    def alloc(self, shape, dt):
        ne = 1
        for s_ in shape:
            ne *= s_
        words = ne if dt == F32 else (ne + 1) // 2
        words = (words + 3) // 4 * 4
        a = self.off
        self.off += words
        self.peak = max(self.peak, self.off)
        assert self.off <= self.n, f"arena overflow {self.off} > {self.n}"
        v = self.ap[:, a:a + words]
        if dt != F32:
            v = v.bitcast(dt)[:, 0:ne]
        else:
            v = v[:, 0:ne]
        if len(shape) == 2:
            v = v.rearrange("p (a b) -> p a b", a=shape[0], b=shape[1])
        elif len(shape) == 3:
            v = v.rearrange("p (a b c) -> p a b c", a=shape[0], b=shape[1], c=shape[2])
        return v


def bc(ap, shape):
    return ap.to_broadcast(list(shape))


def build_program(cfg):
    nc = bass.Bass("TRN2", target_bir_lowering=False)
    P = Planner()
    NS, NM = cfg.n_ssd, cfg.n_mla
    TP, TS, PAST = cfg.t_p, cfg.t_s, cfg.past

    def din(name, shape, dt=F32):
        return nc.dram_tensor(name, list(shape), dt, kind="ExternalInput").ap()

    def dout(name, shape):
        return nc.dram_tensor(name, list(shape), F32, kind="ExternalOutput").ap()

    def dscr(name, shape):
        return nc.dram_tensor(name, list(shape), BF16, kind="Internal").ap()

    I = {}
    I['x_prompt'] = din('x_prompt', [cfg.nseq_p, TP, D])
    I['x_sample'] = din('x_sample', [cfg.nseq_s, TS, D])
    I['cache_conv'] = din('cache_conv', [NS, cfg.nseq_s, 3, 4096])
    I['state_ssm'] = din('state_ssm', [NS, cfg.nseq_s, 32, 64, 128])
    I['cache_kv_latent'] = din('cache_kv_latent', [NM, cfg.nseq_s, PAST, 256])
    I['cache_k_rope'] = din('cache_k_rope', [NM, cfg.nseq_s, PAST, 32])
    I['p_prompt'] = din('p_prompt', [cfg.depth, cfg.nseq_p, TP, 256])
    I['p_sample'] = din('p_sample', [cfg.depth, cfg.nseq_s, TS, 256])
    I['ln_w'] = din('ln_w', [cfg.depth, D])
    I['ssd_in_w'] = din('ssd_in_w', [NS, D, SSD_IN])
    I['ssd_conv_w'] = din('ssd_conv_w', [NS, 4, 4096])
    I['ssd_conv_b'] = din('ssd_conv_b', [NS, 4096])
    I['ssd_dt_bias'] = din('ssd_dt_bias', [NS, 32])
    I['ssd_A_log'] = din('ssd_A_log', [NS, 32])
    I['ssd_D'] = din('ssd_D', [NS, 32])
    I['ssd_norm_w'] = din('ssd_norm_w', [NS, 2048])
    I['ssd_out_w'] = din('ssd_out_w', [NS, 2048, D])
    I['mla_in_w'] = din('mla_in_w', [NM, D, MLA_IN])
    I['mla_q_a_norm'] = din('mla_q_a_norm', [NM, 384])
    I['mla_q_b_w'] = din('mla_q_b_w', [NM, 384, 1536])
    I['mla_kv_a_norm'] = din('mla_kv_a_norm', [NM, 256])
    I['mla_kv_b_w'] = din('mla_kv_b_w', [NM, 256, 2048])
    I['mla_q_nope_norm'] = din('mla_q_nope_norm', [NM, 64])
    I['mla_q_rope_norm'] = din('mla_q_rope_norm', [NM, 32])
    I['mla_k_nope_norm'] = din('mla_k_nope_norm', [NM, 64])
    I['mla_k_rope_norm'] = din('mla_k_rope_norm', [NM, 32])
    I['mla_out_w'] = din('mla_out_w', [NM, D, D])
    I['ple_up_w'] = din('ple_up_w', [cfg.depth, 256, D])
    I['ple_norm_w'] = din('ple_norm_w', [cfg.depth, D])
    I['ple_gate_w'] = din('ple_gate_w', [cfg.depth, D, D])
    I['c_identf'] = din('c_identf', [128, 2, 128])
    I['c_mats'] = din('c_mats', [128, 8, 128], BF16)
    I['c_rope'] = din('c_rope', [TP + TS, 32])

    O = {}
    O['y_prompt'] = dout('y_prompt', [cfg.nseq_p, TP, D])
    O['y_sample'] = dout('y_sample', [cfg.nseq_s, TS, D])
    O['conv_prompt'] = dout('conv_prompt', [NS, cfg.nseq_p, 3, 4096])
    O['ssm_prompt'] = dout('ssm_prompt', [NS, cfg.nseq_p, 32, 64, 128])
    O['lat_prompt'] = dout('lat_prompt', [NM, cfg.nseq_p, TP, 256])
    O['kr_prompt'] = dout('kr_prompt', [NM, cfg.nseq_p, TP, 32])
    O['conv_sample'] = dout('conv_sample', [NS, cfg.nseq_s, 3, 4096])
    O['ssm_sample'] = dout('ssm_sample', [NS, cfg.nseq_s, 32, 64, 128])
    O['lat_sample'] = dout('lat_sample', [NM, cfg.nseq_s, TS, 256])
    O['kr_sample'] = dout('kr_sample', [NM, cfg.nseq_s, TS, 32])

    S_in_ssd = [dscr(f's_in_ssd{j}', [128, 8, 8, 768]) for j in range(NS)]
    S_dt = [dscr(f's_dt{j}', [128, 8, 32]) for j in range(NS)]
    S_out_ssd = [dscr(f's_out_ssd{j}', [128, 16, D]) for j in range(NS)]
    S_in_mla = [dscr(f's_in_mla{j}', [128, 8, MLA_IN]) for j in range(NM)]
    S_qb = [dscr(f's_qb{j}', [128, 3, 1536]) for j in range(NM)]
    S_kvb = [dscr(f's_kvb{j}', [128, 2, 2048]) for j in range(NM)]
    S_out_mla = [dscr(f's_out_mla{j}', [128, 8, D]) for j in range(NM)]
    S_gate = [dscr(f's_gate{i}', [128, 8, D]) for i in range(cfg.depth)]
    S_up = [dscr(f's_up{i}', [128, 2, D]) for i in range(cfg.depth)]

    ARENA_WORDS = 53200
    ctx = []
    arena_t = nc.sbuf_tensor("arena", [128, ARENA_WORDS], F32)
    arena_ap = arena_t.__enter__()
    ctx.append(arena_t)
    psum_t = []
    PS = []
    for b in range(8):
        t = nc.psum_tensor(f"psb{b}", [128, 512], F32)
        PS.append(t.__enter__())
        ctx.append(t)
    A = Arena(arena_ap, ARENA_WORDS)

    def vcost(eng, out):
        n = out.free_size()
        if eng == 'pool':
            return 250.0 + n / 0.6
        return 180.0 + n / 0.96

    def mm(out, pairs, r, w):
        pairs = list(pairs)

        def fn(e):
            ins = None
            n = len(pairs)
            for i_, (l_, r_) in enumerate(pairs):
                ins = e.matmul(out, lhsT=l_, rhs=r_, start=(i_ == 0), stop=(i_ == n - 1))
            return ins
        c = sum(max(64, r_.free_size()) / 2.2 + 4 for (_, r_) in pairs)
        return P.op('pe', fn, r, w, cost=c)

    def mm_multi(items, r, w):
        items = [(o, list(p)) for o, p in items]

        def fn(e):
            ins = None
            for o, pairs in items:
                n = len(pairs)
                for i_, (l_, r_) in enumerate(pairs):
                    ins = e.matmul(o, lhsT=l_, rhs=r_, start=(i_ == 0), stop=(i_ == n - 1))
            return ins
        c = sum(sum(max(64, r_.free_size()) / 2.2 + 4 for (_, r_) in p) for _, p in items)
        return P.op('pe', fn, r, w, cost=c)

    def tr_multi(items, ident, r, w):
        items = list(items)

        def fn(e):
            ins = None
            for o, i_ in items:
                ins = e.transpose(o, i_, ident)
            return ins
        c = sum((max(64, i_.free_size()) / 2.2 + 4) * (4 if i_.dtype == F32 else 1) for _, i_ in items)
        return P.op('pe', fn, r, w, cost=c)

    def act(out, in_, func, r, w, bias=None, scale=None, accum=None):
        kw = {}
        if bias is not None:
            kw['bias'] = bias
        if scale is not None:
            kw['scale'] = scale
        if accum is not None:
            kw['accum_out'] = accum
        return P.op('act', lambda e: e.activation(out=out, in_=in_, func=func, **kw), r, w, cost=(out.free_size() + 330) / 1.2)

    def tt(eng, out, in0, in1, op, r, w):
        return P.op(eng, lambda e: e.tensor_tensor(out=out, in0=in0, in1=in1, op=op), r, w, cost=vcost(eng, out))

    def ts(eng, out, in0, s1, s2, op0, op1, r, w, accum=None):
        kw = {}
        if op1 is not None:
            kw['op1'] = op1
        if accum is not None:
            kw['accum_out'] = accum
        return P.op(eng, lambda e: e.tensor_scalar(out=out, in0=in0, scalar1=s1, scalar2=s2, op0=op0, **kw), r, w, cost=vcost(eng, out))

    def stt(eng, out, in0, sc, in1, op0, op1, r, w):
        return P.op(eng, lambda e: e.scalar_tensor_tensor(out=out, in0=in0, scalar=sc, in1=in1, op0=op0, op1=op1), r, w, cost=vcost(eng, out))

    def cp(eng, out, in_, r, w):
        if eng == 'act':
            return P.op('act', lambda e: e.copy(out=out, in_=in_), r, w, cost=(out.free_size() + 250) / 1.2)
        return P.op(eng, lambda e: e.tensor_copy(out=out, in_=in_), r, w, cost=vcost(eng, out))

    def recip(out, in_, r, w):
        return P.op('dve', lambda e: e.reciprocal(out=out, in_=in_), r, w, cost=vcost('dve', out))

    def mset(eng, ap, val, r, w):
        return P.op(eng, lambda e: e.memset(ap, val), r, w, cost=vcost(eng, ap))

    def dma(q, out, in_, r, w, slow=False):
        dcost = 2000.0 + out.free_size() * out.partition_size() * (4 if out.dtype == F32 else 2) / 150.0
        if slow:
            return P.op(q, lambda e: e.dma_start(out=out, in_=in_, allow_slow_non_contiguous=True), r, w, dma=True, cost=dcost)
        return P.op(q, lambda e: e.dma_start(out=out, in_=in_), r, w, dma=True, cost=dcost)

    def dma_old(q, out, in_, r, w, slow=False):
        if slow:
            return P.op(q, lambda e: e.dma_start(out=out, in_=in_, allow_slow_non_contiguous=True), r, w, dma=True)
        return P.op(q, lambda e: e.dma_start(out=out, in_=in_), r, w, dma=True)

    psc = {'n': 0}
    rings = {'mm': [0, 1], 'sc': [2, 3, 4], 'acc': [5], 'tr': [6, 7]}
    ringpos = {k: 0 for k in rings}

    def set_rings(cfgd):
        rings.clear()
        rings.update(cfgd)

    def psum(tag):
        lst = rings[tag]
        b = lst[ringpos[tag] % len(lst)]
        ringpos[tag] += 1
        return b

    cf = A.alloc([2, 128], F32)
    identf, TLf = cf[:, 0, :], cf[:, 1, :]
    cm = A.alloc([8, 128], BF16)
    identb, TU, TL, onesb = cm[:, 0, :], cm[:, 1, :], cm[:, 2, :], cm[:, 3, :]
    NEGb = cm[:, 4:8, :]
    TMAX = max(TP, TS)
    hT = A.alloc([8, TMAX], F32)
    gains = A.alloc([128], F32)
    upw = A.alloc([2, D], BF16)
    gatew = [A.alloc([8, 128], BF16) for _ in range(2)]
    gw_n = [0]
    vstage = A.alloc([128], F32)
    dma('sp', cf, I['c_identf'], [], ['identf'])

    def load_vec_cols(dst, src1d, ncol, wtag):
        dma('sp', vstage[0:ncol, :], src1d.rearrange("(c p) -> c p", p=128), [], ['vstage'])
        b = psum('tr')
        tr_multi([(PS[b][:, 0:ncol], vstage[0:ncol, :])], identf[0:ncol, 0:ncol], ['vstage', 'identf'], [f'ps{b}'])
        cp('dve', dst, PS[b][:, 0:ncol], [f'ps{b}'], ['tails%d' % c_i for c_i in range(32)] if wtag == 'tailsall' else [wtag])
    dma('sp', cm, I['c_mats'], [], ['cm'])
    CONST = ['identf', 'cm']
    persist_mark = A.mark()

    def prologue():
        P.curlabel = 'prologue'
        m0 = A.mark()
        stg = [A.alloc([SSD_IN], F32) for _ in range(2)]
        stb = [A.alloc([SSD_IN], BF16) for _ in range(2)]
        gcol = {}
        col = [0]

        def load_gain(name, vec_ap, nkc):
            c0 = col[0]
            col[0] += nkc
            load_vec_cols(gains[:, c0:c0 + nkc], vec_ap, nkc, 'gains')
            gcol[name] = c0
        for i in range(cfg.depth):
            load_gain(('ln', i), I['ln_w'][i], 8)
            load_gain(('pn', i), I['ple_norm_w'][i], 8)
        for j in range(NS):
            load_gain(('sn', j), I['ssd_norm_w'][j], 16)
        for j in range(NM):
            load_gain(('qa', j), I['mla_q_a_norm'][j], 3)
        assert col[0] <= 128
        cnt = [0]

        def do_rows(src_rows, ncols, gain_col, stores):
            k = cnt[0] % 2
            cnt[0] += 1
            dma('sp', stg[k][:, 0:ncols], src_rows, [], [f'stg{k}'])
            eng = 'act' if (cnt[0] % 2 == 0) else 'dve'
            if gain_col is None:
                cp(eng, stb[k][:, 0:ncols], stg[k][:, 0:ncols], [f'stg{k}'], [f'stb{k}'])
            elif eng == 'act':
                act(stb[k][:, 0:ncols], stg[k][:, 0:ncols], AF.Copy, [f'stg{k}', 'gains'], [f'stb{k}'],
                    scale=gains[:, gain_col:gain_col + 1])
            else:
                ts('dve', stb[k][:, 0:ncols], stg[k][:, 0:ncols], gains[:, gain_col:gain_col + 1], None, ALU.mult, None,
                   [f'stg{k}', 'gains'], [f'stb{k}'])
            for dst, srcv, wname in stores:
                dma('pool', dst, srcv(stb[k]), [f'stb{k}'], [wname])

        for j in range(NS):
            i = 2 * j
            for kc in range(8):
                rows = I['ssd_in_w'][j, kc * 128:(kc + 1) * 128, :]
                dst = S_in_ssd[j]
                stores = [
                    (dst[:, kc, :, 0:256], lambda b: b[:, 2048:4096].rearrange("p (g c) -> p g c", g=8, c=256), f'w_in_ssd{j}'),
                    (dst[:, kc, :, 256:384], lambda b: b[:, 4096:5120].rearrange("p (g c) -> p g c", g=8, c=128), f'w_in_ssd{j}'),
                    (dst[:, kc, :, 384:512], lambda b: b[:, 5120:6144].rearrange("p (g c) -> p g c", g=8, c=128), f'w_in_ssd{j}'),
                    (dst[:, kc, :, 512:768], lambda b: b[:, 0:2048].rearrange("p (g c) -> p g c", g=8, c=256), f'w_in_ssd{j}'),
                    (S_dt[j][:, kc, :], lambda b: b[:, 6144:6176], f'w_dt{j}'),
                ]
                do_rows(rows, SSD_IN, gcol[('ln', i)] + kc, stores)
            for kc in range(16):
                rows = I['ssd_out_w'][j, kc * 128:(kc + 1) * 128, :]
                do_rows(rows, D, gcol[('sn', j)] + kc, [(S_out_ssd[j][:, kc, :], lambda b: b[:, 0:D], f'w_out_ssd{j}')])
        for j in range(NM):
            i = 2 * j + 1
            for kc in range(8):
                rows = I['mla_in_w'][j, kc * 128:(kc + 1) * 128, :]
                do_rows(rows, MLA_IN, gcol[('ln', i)] + kc, [(S_in_mla[j][:, kc, :], lambda b: b[:, 0:MLA_IN], f'w_in_mla{j}')])
            for kc in range(3):
                rows = I['mla_q_b_w'][j, kc * 128:(kc + 1) * 128, :]
                do_rows(rows, 1536, gcol[('qa', j)] + kc, [(S_qb[j][:, kc, :], lambda b: b[:, 0:1536], f'w_qb{j}')])
            for kc in range(2):
                rows = I['mla_kv_b_w'][j, kc * 128:(kc + 1) * 128, :]
                do_rows(rows, 2048, None, [(S_kvb[j][:, kc, :], lambda b: b[:, 0:2048], f'w_kvb{j}')])
            for kc in range(8):
                rows = I['mla_out_w'][j, kc * 128:(kc + 1) * 128, :]
                do_rows(rows, D, None, [(S_out_mla[j][:, kc, :], lambda b: b[:, 0:D], f'w_out_mla{j}')])
        for i in range(cfg.depth):
            for kc in range(8):
                rows = I['ple_gate_w'][i, kc * 128:(kc + 1) * 128, :]
                do_rows(rows, D, gcol[('pn', i)] + kc, [(S_gate[i][:, kc, :], lambda b: b[:, 0:D], f'w_gate{i}')])
            for kc in range(2):
                rows = I['ple_up_w'][i, kc * 128:(kc + 1) * 128, :]
                do_rows(rows, D, None, [(S_up[i][:, kc, :], lambda b: b[:, 0:D], f'w_up{i}')])
        P.barrier()
        A.release(m0)

    if not getattr(cfg, 'skip_pro', False):
        prologue()

    TTG = [cfg.tt]

    def HT(jo, tok0):
        return 'hT%d_%d' % (jo, tok0 // TTG[0])

    def HTA(tok0):
        return [HT(jo, tok0) for jo in range(8)]

    class Seq:
        pass

    def rms_fm(sq, tok0, n, hn_out, tag):
        hs = hT[:, :, tok0:tok0 + n]
        b = psum('mm')
        pst = PS[b][:, 0:n]
        if 'sqs' in sq:
            for kc in range(8):
                sb_ = sq['sqs'][kc % 2][:, 0:n]
                stag = 'sqs%d' % (kc % 2)
                act(sb_, hT[:, kc, tok0:tok0 + n], AF.Square, [HT(kc, tok0)], [stag])
                P.op('pe', lambda e, o=pst, r_=sb_, f=(kc == 0), la=(kc == 7): e.matmul(o, lhsT=onesb, rhs=r_, start=f, stop=la),
                     [stag, 'cm'], [f'ps{b}'], cost=n / 2.2 + 4)
        else:
            sqt = sq['sqt'][:, :, 0:n]
            act(sqt, hs, AF.Square, HTA(tok0), ['sqt'])
            mm(pst, [(onesb, sqt[:, kc, :]) for kc in range(8)], ['sqt', 'cm'], [f'ps{b}'])
        rs = sq['rstd'][:, 0:n]
        act(rs, pst, AF.Ln, [f'ps{b}'], ['rstd'], bias=EPS, scale=1.0 / D)
        act(rs, rs, AF.Exp, ['rstd'], ['rstd'], scale=-0.5)
        tt('dve', hn_out, hs, bc(rs.unsqueeze(1), [128, 8, n]), ALU.mult, HTA(tok0) + ['rstd'], [tag])

    def ple_tile(sq, i, pin, tok0, n, nt128, cht, hnbuf=None, hntag='hn2', ebufs=None):
        hn = (sq['hn2'] if hnbuf is None else hnbuf)[:, :, 0:n]
        rms_fm(sq, tok0, n, hn, hntag)
        ptm = sq['ptm']
        pT = sq['pT']
        for t_ in range(nt128):
            c0 = t_ * cht
            dma('sp', ptm[0:cht, :], pin[tok0 + c0:tok0 + c0 + cht, :], [], ['ptm'])
            cp('pool', sq['ptmb'][0:cht, :], ptm[0:cht, :], ['ptm'], ['ptmb'])
            b = psum('tr')
            pb = PS[b][:].bitcast(BF16)
            tr_multi([(pb[:, k * 128:k * 128 + cht], sq['ptmb'][0:cht, k * 128:(k + 1) * 128]) for k in range(2)],
                     identb[0:cht, 0:cht], ['ptmb', 'cm'], [f'ps{b}'])
            cp('act', pT[:, :, c0:c0 + cht], pb[:, 0:256].rearrange("p (k c) -> p k c", k=2, c=128)[:, :, 0:cht],
               [f'ps{b}'], ['pT'])
        for jo in range(8):
            gk = gw_n[0] % 2
            gw_n[0] += 1
            dma('sp', gatew[gk], S_gate[i][:, :, jo * 128:(jo + 1) * 128], [f'w_gate{i}'], [f'gatew{gk}'])
            bg = psum('mm')
            mm(PS[bg][:, 0:n], [(gatew[gk][:, kc, :], hn[:, kc, :]) for kc in range(8)],
               [f'gatew{gk}', hntag], [f'ps{bg}'])
            bu = psum('mm')
            mm(PS[bu][:, 0:n], [(upw[:, kc, jo * 128:(jo + 1) * 128], pT[:, kc, 0:n]) for kc in range(2)],
               ['upw', 'pT'], [f'ps{bu}'])
            if ebufs is None:
                ebufs = [(sq['ple_e'], 'ple_e'), (sq['ple_e2'], 'ple_e2')]
            eb, et = ebufs[jo % len(ebufs)]
            e_ = eb[:, 0:n]
            act(e_, PS[bg][:, 0:n], AF.Exp, [f'ps{bg}'], [et], scale=-1.0)
            act(e_, e_, AF.Ln, [et], [et], bias=1.0, scale=1.0)
            act(e_, e_, AF.Exp, [et], [et], scale=-1.0)
            tt('dve', e_, PS[bu][:, 0:n], e_, ALU.mult, [f'ps{bu}', et], [et])
            tt('dve', hT[:, jo, tok0:tok0 + n], hT[:, jo, tok0:tok0 + n], e_, ALU.add, [HT(jo, tok0), et], [HT(jo, tok0)])

    def alloc_common(T, TT, need_hn2=True, small_sqt=False):
        sq = {}
        if small_sqt:
            sq['sqs'] = [A.alloc([TT], BF16) for _ in range(2)]
        else:
            sq['sqt'] = A.alloc([8, TT], BF16)
        sq['rstd'] = A.alloc([TT], F32)
        if need_hn2:
            sq['hn2'] = A.alloc([8, TT], BF16)
        sq['ptm'] = A.alloc([256], F32)
        sq['ptmb'] = A.alloc([256], BF16)
        sq['pT'] = A.alloc([2, TT], BF16)
        if need_hn2:
            sq['ple_e'] = A.alloc([TT], F32)
            sq['ple_e2'] = A.alloc([TT], F32)
        return sq

    def load_ple_w(i):
        dma('sp', upw, S_up[i], [f'w_up{i}'], ['upw'])

    def ssd_layer(kind, sidx, i, T, TT, CH):
        j = i // 2
        P.curlabel = 'ssd_' + kind
        set_rings({'mm': [0, 1, 2], 'sc': [3, 4], 'acc': [5], 'tr': [6, 7]})
        LVL = getattr(cfg, 'lvl', 9)
        TAP2ENG = getattr(cfg, 'tap2eng', 'dve')
        NBUF = getattr(cfg, 'nbuf', 2)
        ccnt = [0]
        m0 = A.mark()
        sq = alloc_common(T, TT, need_hn2=False, small_sqt=True)
        NCH = TT // CH
        NT = T // TT
        hn_l = [A.alloc([8, TT], BF16) for _ in range(2)]
        w_dt = A.alloc([8, 32], BF16)
        w_out = [A.alloc([16, 128], BF16) for _ in range(2)]
        wo_n = [0]
        raw = A.alloc([2, TT + 3], F32)
        cacc_l = [A.alloc([TT], F32) for _ in range(2)]
        cexp_l = [A.alloc([TT], F32) for _ in range(2)]
        xs_l = [A.alloc([4, TT], BF16) for _ in range(2)]
        gcnt = [0]
        scnt = [0]
        ynT = A.alloc([16, TT], BF16)
        St = A.alloc([2048], F32)
        Sbf = A.alloc([2048], BF16)
        tails = A.alloc([3, 32], F32)
        cw = A.alloc([4, 32], F32)
        cb = A.alloc([32], F32)
        dtb = A.alloc([32], F32)
        Abc = A.alloc([32], F32)
        Dbc = A.alloc([32], F32)
        dtx = A.alloc([NCH, 32], F32)
        dta = A.alloc([NCH, 32], F32)
        dtl = A.alloc([NCH, 32], F32)
        dt_ = A.alloc([NCH, 32], F32)
        dtA = A.alloc([NCH, 32], F32)
        dtAh = A.alloc([NCH, 32], BF16)
        dtAl = A.alloc([NCH, 32], BF16)
        Acs = A.alloc([NCH, 32], F32)
        eA = A.alloc([NCH, 32], F32)
        dec = A.alloc([NCH, 32], F32)
        ddt = A.alloc([NCH, 32], F32)
        cdk = A.alloc([NCH, 32], F32)
        xB_l = [A.alloc([384], BF16) for _ in range(NBUF)]
        xdt_l = [A.alloc([4, 64], BF16) for _ in range(NBUF)]
        xdd_l = [A.alloc([4, 64], BF16) for _ in range(NBUF)]
        Rh_l = [A.alloc([4, CH], F32) for _ in range(NBUF)]
        Lm_l = [A.alloc([4, CH], BF16) for _ in range(NBUF)]
        MT_l = [A.alloc([4, CH], BF16) for _ in range(NBUF)]
        t1_l = [A.alloc([4, 64], F32) for _ in range(NBUF)]
        yv_l = [A.alloc([256], F32) for _ in range(NBUF)]
        ze_l = [A.alloc([256], F32) for _ in range(NBUF)]
        ss_l = [A.alloc([2], F32) for _ in range(NBUF)]
        yn_l = [A.alloc([256], BF16) for _ in range(NBUF)]
        stg = A.alloc([16, 128], F32) if kind == 's' else None
        tlT = A.alloc([128], F32)
        NEG32 = None
        if CH != 128:
            NEG32 = A.alloc([4 * CH], BF16)
            cp('dve', NEG32[0:CH].rearrange('p (r c) -> p r c', r=4, c=CH), NEGb[0:CH, :, 0:CH], ['cm'], ['neg32'])

        m_w = A.mark()
        w_in = [A.alloc([8, 768], BF16) for _ in range(2)]

        for kk in range(4):
            load_vec_cols(cw[:, kk, :], I['ssd_conv_w'][j, kk], 32, 'cw')
        load_vec_cols(cb, I['ssd_conv_b'][j], 32, 'cb')
        dma('sp', dtb, I['ssd_dt_bias'][j].partition_broadcast(128), [], ['dtb'])
        dma('sp', Abc, I['ssd_A_log'][j].partition_broadcast(128), [], ['Abc'])
        dma('sp', Dbc, I['ssd_D'][j].partition_broadcast(128), [], ['Dbc'])
        act(Abc, Abc, AF.Exp, ['Abc'], ['Abc'])
        ts('dve', Abc, Abc, -1.0, None, ALU.mult, None, ['Abc'], ['Abc'])
        dma('sp', w_dt, S_dt[j], [f'w_dt{j}'], ['w_dt'])
        load_ple_w(i)
        if kind == 'p':
            mset('pool', St, 0.0, [], ['St'])
            mset('pool', tails, 0.0, [], ['tails%d' % c_i for c_i in range(32)])
        else:
            src = I['state_ssm'][j, sidx].rearrange("h p n -> (h p) n").rearrange("(c r) n -> r c n", r=128)
            dma('sp', stg, src, [], ['stg'])
            for q4 in range(4):
                b = psum('tr')
                tr_multi([(PS[b][:, k * 128:(k + 1) * 128], stg[:, q4 * 4 + k, :]) for k in range(4)], identf, ['stg', 'identf'], [f'ps{b}'])
                cp('dve', St[:, q4 * 512:(q4 + 1) * 512], PS[b][:, 0:512], [f'ps{b}'], ['St'])
            for kk in range(3):
                load_vec_cols(tails[:, kk, :], I['cache_conv'][j, sidx, kk], 32, 'tailsall')
        cp('act', Sbf, St, ['St'], ['Sbf'])

        nload = [0]

        def load_w_in(g):
            k = nload[0] % 2
            nload[0] += 1
            dma('sp', w_in[k], S_in_ssd[j][:, :, g, :], [f'w_in_ssd{j}'], [f'w_in{k}'])
            return k

        pin = (I['p_prompt'] if kind == 'p' else I['p_sample'])[i, sidx]
        pending = load_w_in(0)
        for ti in range(NT):
            if LVL in (-3, -1):
                break
            tok0 = ti * TT
            hn = hn_l[ti % 2]
            HN = 'hn%d' % (ti % 2)
            rms_fm(sq, tok0, TT, hn, HN)
            DTL = getattr(cfg, 'dtl', 9)
            b = psum('sc')
            if DTL >= 1:
              mm_multi([(PS[b][0:CH, ch * 32:(ch + 1) * 32],
                         [(hn[:, kc, ch * CH:(ch + 1) * CH], w_dt[:, kc, :]) for kc in range(8)]) for ch in range(NCH)],
                       [HN, 'w_dt'], [f'ps{b}'])
            c_ = slice(0, CH)
            pv = PS[b][0:CH, 0:NCH * 32].rearrange("p (c h) -> p c h", c=NCH, h=32)
            tt('dve', dtx[c_], pv, bc(dtb[c_].unsqueeze(1), [CH, NCH, 32]), ALU.add, [f'ps{b}', 'dtb'], ['dtx'])
            act(dta[c_], dtx[c_], AF.Abs, ['dtx'], ['dta'])
            act(dta[c_], dta[c_], AF.Exp, ['dta'], ['dta'], scale=-1.0)
            act(dtl[c_], dta[c_], AF.Ln, ['dta'], ['dtl'], bias=1.0, scale=1.0)
            ts('dve', dtx[c_], dtx[c_], 0.0, None, ALU.max, None, ['dtx'], ['dtx'])
            tt('dve', dt_[c_], dtx[c_], dtl[c_], ALU.add, ['dtx', 'dtl'], ['dt'])
            tt('dve', dtA[c_], dt_[c_], bc(Abc[c_].unsqueeze(1), [CH, NCH, 32]), ALU.mult, ['dt', 'Abc'], ['dtA'])
            cp('dve', dtAh[c_], dtA[c_], ['dtA'], ['dtAh'])
            tt('dve', dtAl[c_], dtA[c_], dtAh[c_], ALU.subtract, ['dtA', 'dtAh'], ['dtAl'])
            b1 = psum('sc')
            mm_multi([(PS[b1][0:CH, ch * 32:(ch + 1) * 32],
                       [(TU[0:CH, 0:CH], dtAh[c_, ch, :]), (TU[0:CH, 0:CH], dtAl[c_, ch, :])]) for ch in range(NCH)],
                     ['dtAh', 'dtAl', 'cm'], [f'ps{b1}'])
            b2 = psum('sc')
            mm_multi([(PS[b2][:, ch * 32:(ch + 1) * 32],
                       [(onesb[0:CH, :], dtAh[c_, ch, :]), (onesb[0:CH, :], dtAl[c_, ch, :])]) for ch in range(NCH)],
                     ['dtAh', 'dtAl', 'cm'], [f'ps{b2}'])
            pa = PS[b1][0:CH, 0:NCH * 32].rearrange("p (c h) -> p c h", c=NCH, h=32)
            pt_ = PS[b2][:, 0:NCH * 32].rearrange("p (c h) -> p c h", c=NCH, h=32)
            cp('dve', Acs[c_], pa, [f'ps{b1}'], ['Acs'])
            act(eA[c_], pa, AF.Exp, [f'ps{b1}'], ['eA'])
            tt('dve', dec[c_], pt_[0:CH], Acs[c_], ALU.subtract, [f'ps{b2}', 'Acs'], ['dec'])
            act(dec[c_], dec[c_], AF.Exp, ['dec'], ['dec'])
            tt('dve', ddt[c_], dec[c_], dt_[c_], ALU.mult, ['dec', 'dt'], ['ddt'])
            act(cdk, pt_, AF.Exp, [f'ps{b2}'], ['cdk'])

            for g in range(8):
                if LVL < 1:
                    break
                k = pending
                wv = w_in[k]
                xp = gcnt[0] % 2
                gcnt[0] += 1
                xs = xs_l[xp]
                XS = 'xs%d' % xp
                if not (ti == NT - 1 and g == 7):
                    pending = load_w_in((g + 1) % 8)
                for s_ in range(4):
                    cidx = (2 * g + s_) if s_ < 2 else (16 + g if s_ == 2 else 24 + g)
                    RW = 'raw%d' % (s_ % 2)
                    cpp = scnt[0] % 2
                    scnt[0] += 1
                    cacc = cacc_l[cpp]
                    cexp = cexp_l[cpp]
                    CA = 'cacc%d' % cpp
                    CE = 'cexp%d' % cpp
                    b = psum('mm')
                    mm(PS[b][:, 0:TT], [(wv[:, kc, s_ * 128:(s_ + 1) * 128], hn[:, kc, :]) for kc in range(8)],
                       [f'w_in{k}', HN], [f'ps{b}'])
                    cp('pool', raw[:, s_ % 2, 0:3], tails[:, :, cidx], ['tails%d' % cidx], [RW])
                    cp('act', raw[:, s_ % 2, 3:3 + TT], PS[b][:, 0:TT], [f'ps{b}'], [RW])
                    cp('pool', tails[:, :, cidx], raw[:, s_ % 2, TT:TT + 3], [RW], ['tails%d' % cidx])
                    act(cacc, PS[b][:, 0:TT], AF.Identity, [f'ps{b}', 'cw', 'cb'], [CA], bias=cb[:, cidx:cidx + 1], scale=cw[:, 3, cidx:cidx + 1])
                    for kk, eng_ in ((0, 'dve'), (1, 'dve'), (2, TAP2ENG)):
                        stt(eng_, cacc, raw[:, s_ % 2, kk:kk + TT], cw[:, kk, cidx:cidx + 1], cacc, ALU.mult, ALU.add,
                            [RW, 'cw', CA], [CA])
                    act(cexp, cacc, AF.Exp, [CA], [CE], scale=-1.0)
                    act(cexp, cexp, AF.Ln, [CE], [CE], bias=1.0, scale=1.0)
                    act(cexp, cexp, AF.Exp, [CE], [CE], scale=-1.0)
                    tt('dve', xs[:, s_, :], cacc, cexp, ALU.mult, [CA, CE], [XS])
                hs = slice(4 * g, 4 * g + 4)
                for ch in range(NCH):
                    if LVL < 2:
                        break
                    pp = ccnt[0] % NBUF
                    ccnt[0] += 1
                    N_ = lambda s_, pp=pp: s_ + '_%d' % pp
                    xB = xB_l[pp]
                    xdt = xdt_l[pp]
                    xdd = xdd_l[pp]
                    Rh = Rh_l[pp]
                    Lm = Lm_l[pp]
                    MT = MT_l[pp]
                    t1 = t1_l[pp]
                    yv = yv_l[pp]
                    ze = ze_l[pp]
                    ss = ss_l[pp]
                    yn = yn_l[pp]
                    c0 = ch * CH
                    bz = psum('mm')
                    mm(PS[bz][0:CH, 0:256], [(hn[:, kc, c0:c0 + CH], wv[:, kc, 512:768]) for kc in range(8)],
                       [HN, f'w_in{k}'], [f'ps{bz}'])
                    btr = psum('tr')
                    pb = PS[btr][:].bitcast(BF16)
                    tr_multi([(pb[0:CH, s_ * 128:(s_ + 1) * 128], xs[:, s_, c0:c0 + CH]) for s_ in range(3)],
                             identb, [XS, 'cm'], [f'ps{btr}'])
                    cp('act', xB[c_], pb[0:CH, 0:384], [f'ps{btr}'], [N_('xB')])
                    x3 = xB[c_, 0:256].rearrange("p (r d) -> p r d", r=4, d=64)
                    tt('pool', xdt[c_], x3, bc(dt_[c_, ch, hs].unsqueeze(2), [CH, 4, 64]), ALU.mult, [N_('xB'), 'dt'], [N_('xdt')])
                    tt('pool', xdd[c_], x3, bc(ddt[c_, ch, hs].unsqueeze(2), [CH, 4, 64]), ALU.mult, [N_('xB'), 'ddt'], [N_('xdd')])
                    tt('dve', Rh[c_], bc(dtA[c_, ch, hs].unsqueeze(2), [CH, 4, CH]), bc(TU[0:CH, 0:CH].unsqueeze(1), [CH, 4, CH]),
                       ALU.mult, ['dtA', 'cm'], [N_('Rh')])
                    if LVL < 3:
                        continue
                    bs = psum('sc')
                    mm(PS[bs][0:CH, 0:4 * CH], [(TLf[0:CH, 0:CH], Rh[c_].rearrange("p r c -> p (r c)")),
                                                (identb[0:CH, 0:CH], NEGb[0:CH, :, 0:CH] if CH == 128 else NEG32[0:CH])],
                       [N_('Rh'), 'cm', 'identf'], [f'ps{bs}'])
                    act(Lm[c_], PS[bs][0:CH, 0:4 * CH].rearrange("p (r c) -> p r c", r=4, c=CH), AF.Exp, [f'ps{bs}'], [N_('Lm')])
                    bcb = psum('sc')
                    mm(PS[bcb][0:CH, 0:CH], [(xs[:, 2, c0:c0 + CH], xs[:, 3, c0:c0 + CH])], [XS], [f'ps{bcb}'])
                    tt('dve', MT[c_], Lm[c_], bc(PS[bcb][0:CH, 0:CH].unsqueeze(1), [CH, 4, CH]), ALU.mult, [N_('Lm'), f'ps{bcb}'], [N_('MT')])
                    by = psum('acc')
                    mm_multi([(PS[by][0:CH, r_ * 64:(r_ + 1) * 64], [(MT[c_, r_, :], xdt[c_, r_, :])]) for r_ in range(4)]
                             + [(PS[by][0:CH, 256:512], [(xs[:, 3, c0:c0 + CH], Sbf[:, g * 256:(g + 1) * 256])])],
                             [N_('MT'), N_('xdt'), XS, 'Sbf'], [f'ps{by}'])
                    yo3 = PS[by][0:CH, 256:512].rearrange("p (r d) -> p r d", r=4, d=64)
                    tt('dve', t1[c_], yo3, bc(eA[c_, ch, hs].unsqueeze(2), [CH, 4, 64]), ALU.mult, [f'ps{by}', 'eA'], [N_('t1')])
                    yv3 = yv[c_].rearrange('p (r d) -> p r d', r=4, d=64)
                    tt('pool', yv3, x3, bc(Dbc[c_, hs].unsqueeze(2), [CH, 4, 64]), ALU.mult, [N_('xB'), 'Dbc'], [N_('yv')])
                    tt('pool', t1[c_], t1[c_], yv3, ALU.add, [N_('t1'), N_('yv')], [N_('t1')])
                    tt('dve', yv[c_], PS[by][0:CH, 0:256], t1[c_].rearrange("p r d -> p (r d)"), ALU.add, [f'ps{by}', N_('t1')], [N_('yv')])
                    if LVL < 4:
                        continue
                    act(ze[c_], PS[bz][0:CH, 0:256], AF.Exp, [f'ps{bz}'], [N_('ze')], scale=-1.0)
                    act(ze[c_], ze[c_], AF.Ln, [N_('ze')], [N_('ze')], bias=1.0, scale=1.0)
                    act(ze[c_], ze[c_], AF.Exp, [N_('ze')], [N_('ze')], scale=-1.0)
                    tt('dve', ze[c_], PS[bz][0:CH, 0:256], ze[c_], ALU.mult, [f'ps{bz}', N_('ze')], [N_('ze')])
                    tt('dve', yv[c_], yv[c_], ze[c_], ALU.mult, [N_('yv'), N_('ze')], [N_('yv')])
                    act(ze[c_], yv[c_], AF.Square, [N_('yv')], [N_('ze'), N_('ss')], accum=ss[c_, 0:1])
                    act(ss[c_, 1:2], ss[c_, 0:1], AF.Ln, [N_('ss')], [N_('ss')], bias=EPS, scale=1.0 / 256)
                    act(ss[c_, 1:2], ss[c_, 1:2], AF.Exp, [N_('ss')], [N_('ss')], scale=-0.5)
                    ts('dve', yn[c_], yv[c_], ss[c_, 1:2], None, ALU.mult, None, [N_('yv'), N_('ss')], [N_('yn')])
                    bt2 = psum('tr')
                    pb2 = PS[bt2][:].bitcast(BF16)
                    tr_multi([(pb2[:, k2 * 128:k2 * 128 + CH], yn[c_, k2 * 128:(k2 + 1) * 128]) for k2 in range(2)],
                             identb[0:CH, 0:CH], [N_('yn'), 'cm'], [f'ps{bt2}'])
                    cp('act', ynT[:, 2 * g:2 * g + 2, c0:c0 + CH],
                       pb2[:, 0:256].rearrange("p (k c) -> p k c", k=2, c=128)[:, :, 0:CH], [f'ps{bt2}'], ['ynT'])
                    if LVL < 5:
                        continue
                    bst = psum('sc')
                    mm(PS[bst][:, 0:256], [(xB[c_, 256:384], xdd[c_].rearrange("p r d -> p (r d)"))], [N_('xB'), N_('xdd')], [f'ps{bst}'])
                    Sg = St[:, g * 256:(g + 1) * 256]
                    tt('pool', Sg.rearrange("p (r d) -> p r d", r=4, d=64), Sg.rearrange("p (r d) -> p r d", r=4, d=64),
                       bc(cdk[:, ch, hs].unsqueeze(2), [128, 4, 64]), ALU.mult, ['St', 'cdk'], ['St'])
                    tt('dve', Sg, Sg, PS[bst][:, 0:256], ALU.add, ['St', f'ps{bst}'], ['St'])
                    cp('pool', Sbf[:, g * 256:(g + 1) * 256], Sg, ['St'], ['Sbf'])
            set_rings({'mm': [0, 1, 2, 3, 4, 5], 'sc': [5], 'acc': [5], 'tr': [6, 7]})
            for jo in range(8):
                wk = wo_n[0] % 2
                wo_n[0] += 1
                dma('sp', w_out[wk], S_out_ssd[j][:, :, jo * 128:(jo + 1) * 128], [f'w_out_ssd{j}'], [f'w_out{wk}'])
                b = psum('mm')
                mm(PS[b][:, 0:TT], [(w_out[wk][:, kc, :], ynT[:, kc, :]) for kc in range(16)],
                   [f'w_out{wk}', 'ynT'], [f'ps{b}'])
                tt('dve', hT[:, jo, tok0:tok0 + TT], hT[:, jo, tok0:tok0 + TT], PS[b][:, 0:TT], ALU.add, [HT(jo, tok0), f'ps{b}'], [HT(jo, tok0)])
            ple_tile(sq, i, pin, tok0, TT, NCH, CH, hnbuf=hn, hntag=HN, ebufs=[(cexp_l[0], 'cexp0'), (cexp_l[1], 'cexp1')])
            set_rings({'mm': [0, 1, 2], 'sc': [3, 4], 'acc': [5], 'tr': [6, 7]})
        if LVL in (-3, -2):
            P.barrier()
            A.release(m0)
            return
        if kind == 'p':
            P.barrier()
            A.release(m_w)
            stg = A.alloc([16, 128], F32)
        oc = (O['conv_prompt'] if kind == 'p' else O['conv_sample'])[j, sidx]
        b = psum('tr')
        tr_multi([(PS[b][0:96, 0:128], tails.rearrange("p k c -> p (k c)"))], identf, ['tails%d' % c_i for c_i in range(32)] + ['identf'], [f'ps{b}'])
        cp('dve', tlT[0:96], PS[b][0:96, 0:128], [f'ps{b}'], ['tlT'])
        dma('pool', oc.rearrange("k (c p) -> (k c) p", p=128), tlT[0:96, :], ['tlT'], [])
        osm = (O['ssm_prompt'] if kind == 'p' else O['ssm_sample'])[j, sidx].rearrange("h p n -> (h p) n").rearrange("(c r) n -> r c n", r=128)
        for q4 in range(4):
            b = psum('tr')
            tr_multi([(PS[b][:, k2 * 128:(k2 + 1) * 128], St[:, (q4 * 4 + k2) * 128:(q4 * 4 + k2 + 1) * 128]) for k2 in range(4)],
                     identf, ['St', 'identf'], [f'ps{b}'])
            cp('dve', stg[:, q4 * 4:(q4 + 1) * 4, :], PS[b][:, 0:512].rearrange("p (k n) -> p k n", k=4, n=128), [f'ps{b}'], ['stg'])
        dma('pool', osm, stg, ['stg'], [])
        P.barrier()
        A.release(m0)

    def mla_layer(kind, sidx, i, T, TT, CH):
        j = i // 2
        P.curlabel = 'mla_p0_' + kind
        set_rings({'mm': [0, 1, 2], 'sc': [3, 4], 'acc': [5], 'tr': [6, 7]})
        m0 = A.mark()
        past = PAST if kind == 's' else 0
        TK = past + T
        NT = T // TT
        NCT = TT // CH
        nkt_past = past // 128
        nkt_new = T // CH
        NKT = nkt_past + nkt_new
        sg = A.alloc([8, T], BF16)
        qanT = A.alloc([3, T], BF16)
        latT = A.alloc([2, TK], BF16)
        krall = A.alloc([NKT, 32], BF16)
        kvg = A.alloc([256], F32)
        qng = A.alloc([64], F32)
        qrg = A.alloc([32], F32)
        kng = A.alloc([64], F32)
        krg = A.alloc([32], F32)
        rope = A.alloc([32], F32)
        junk = A.alloc([512], F32)
        ss = A.alloc([40], F32)
        mP0 = A.mark()
        sq = alloc_common(T, TT, need_hn2=False)
        hn = A.alloc([8, TT], BF16)
        w_in = A.alloc([8, MLA_IN], BF16)
        qab = A.alloc([384], BF16)
        kva = A.alloc([288], F32)
        lat = A.alloc([256], F32)
        latb = A.alloc([256], BF16)
        krn = A.alloc([32], F32)
        kro = A.alloc([32], F32)
        ge_l = [A.alloc([TT], F32) for _ in range(2)]
        pl = A.alloc([256], F32)
        pr = A.alloc([32], F32)

        dma('sp', w_in, S_in_mla[j], [f'w_in_mla{j}'], ['w_in'])
        dma('sp', kvg, I['mla_kv_a_norm'][j].partition_broadcast(128), [], ['kvg'])
        dma('sp', qng, I['mla_q_nope_norm'][j].partition_broadcast(128), [], ['qng'])
        dma('sp', qrg, I['mla_q_rope_norm'][j].partition_broadcast(128), [], ['qrg'])
        dma('sp', kng, I['mla_k_nope_norm'][j].partition_broadcast(128), [], ['kng'])
        dma('sp', krg, I['mla_k_rope_norm'][j].partition_broadcast(128), [], ['krg'])
        load_ple_w(i)
        olat = (O['lat_prompt'] if kind == 'p' else O['lat_sample'])[j, sidx]
        okr = (O['kr_prompt'] if kind == 'p' else O['kr_sample'])[j, sidx]
        rope_row0 = 0 if kind == 'p' else TP

        def rmsn(src, n, width, scale_col, tagr):
            c_ = slice(0, n)
            act(junk[c_, 0:width], src, AF.Square, tagr, ['junk', 'ss'], accum=ss[c_, scale_col:scale_col + 1])
            act(ss[c_, scale_col:scale_col + 1], ss[c_, scale_col:scale_col + 1], AF.Ln, ['ss'], ['ss'], bias=EPS, scale=1.0 / width)
            act(ss[c_, scale_col:scale_col + 1], ss[c_, scale_col:scale_col + 1], AF.Exp, ['ss'], ['ss'], scale=-0.5)

        def do_rope(dst, srcn, n, nh, tag_r, tag_w):
            c_ = slice(0, n)
            cos = bc(rope[c_, 0:16].unsqueeze(1), [n, nh, 16])
            sin = bc(rope[c_, 16:32].unsqueeze(1), [n, nh, 16])
            x1 = srcn[:, :, 0:16]
            x2 = srcn[:, :, 16:32]
            tA = junk[c_, 0:nh * 16].rearrange("p (h d) -> p h d", h=nh, d=16)
            tB = junk[c_, 256:256 + nh * 16].rearrange("p (h d) -> p h d", h=nh, d=16)
            tt('dve', tA, x1, cos, ALU.mult, tag_r + ['rope'], ['junk'])
            tt('dve', tB, x2, sin, ALU.mult, tag_r + ['rope'], ['junk'])
            tt('dve', dst[:, :, 0:16], tA, tB, ALU.subtract, ['junk'], tag_w)
            tt('dve', tA, x1, sin, ALU.mult, tag_r + ['rope'], ['junk'])
            tt('dve', tB, x2, cos, ALU.mult, tag_r + ['rope'], ['junk'])
            tt('dve', dst[:, :, 16:32], tA, tB, ALU.add, ['junk'], tag_w)

        if past:
            for kt in range(nkt_past):
                dma('sp', pl, I['cache_kv_latent'][j, sidx, kt * 128:(kt + 1) * 128, :], [], ['pl'])
                cp('pool', latb, pl, ['pl'], ['latb'])
                b = psum('tr')
                pb = PS[b][:].bitcast(BF16)
                tr_multi([(pb[:, k * 128:(k + 1) * 128], latb[:, k * 128:(k + 1) * 128]) for k in range(2)], identb, ['latb', 'cm'], [f'ps{b}'])
                cp('act', latT[:, :, kt * 128:(kt + 1) * 128], pb[:, 0:256].rearrange("p (k c) -> p k c", k=2, c=128), [f'ps{b}'], ['latT'])
                dma('sp', pr, I['cache_k_rope'][j, sidx, kt * 128:(kt + 1) * 128, :], [], ['pr'])
                cp('pool', krall[:, kt, :], pr, ['pr'], ['krall'])

        for ti in range(NT):
            tok0 = ti * TT
            rms_fm(sq, tok0, TT, hn, 'hn')
            for jo in range(8):
                ge = ge_l[jo % 2]
                GE = 'ge%d' % (jo % 2)
                b = psum('mm')
                mm(PS[b][:, 0:TT], [(w_in[:, kc, 672 + jo * 128:672 + (jo + 1) * 128], hn[:, kc, :]) for kc in range(8)],
                   ['w_in', 'hn'], [f'ps{b}'])
                act(ge[:, 0:TT], PS[b][:, 0:TT], AF.Exp, [f'ps{b}'], [GE], scale=-1.0)
                act(ge[:, 0:TT], ge[:, 0:TT], AF.Ln, [GE], [GE], bias=1.0, scale=1.0)
                act(ge[:, 0:TT], ge[:, 0:TT], AF.Exp, [GE], [GE], scale=-1.0)
                tt('dve', sg[:, jo, tok0:tok0 + TT], PS[b][:, 0:TT], ge[:, 0:TT], ALU.mult, [f'ps{b}', GE], ['sg'])
            for ct in range(NCT):
                c0 = ct * CH
                t0 = tok0 + c0
                c_ = slice(0, CH)
                kt_idx = nkt_past + (t0 // CH)
                dma('sp', rope[c_], I['c_rope'][rope_row0 + t0:rope_row0 + t0 + CH, :], [], ['rope'])
                b1 = psum('mm')
                mm(PS[b1][0:CH, 0:384], [(hn[:, kc, c0:c0 + CH], w_in[:, kc, 0:384]) for kc in range(8)], ['hn', 'w_in'], [f'ps{b1}'])
                b2 = psum('mm')
                mm(PS[b2][0:CH, 0:288], [(hn[:, kc, c0:c0 + CH], w_in[:, kc, 384:672]) for kc in range(8)], ['hn', 'w_in'], [f'ps{b2}'])
                rmsn(PS[b1][0:CH, 0:384], CH, 384, 0, [f'ps{b1}'])
                ts('dve', qab[c_], PS[b1][0:CH, 0:384], ss[c_, 0:1], None, ALU.mult, None, [f'ps{b1}', 'ss'], ['qab'])
                bt = psum('tr')
                pb = PS[bt][:].bitcast(BF16)
                tr_multi([(pb[:, k * 128:k * 128 + CH], qab[c_, k * 128:(k + 1) * 128]) for k in range(3)], identb[0:CH, 0:CH],
                         ['qab', 'cm'], [f'ps{bt}'])
                cp('act', qanT[:, :, t0:t0 + CH], pb[:, 0:384].rearrange("p (k c) -> p k c", k=3, c=128)[:, :, 0:CH], [f'ps{bt}'], ['qanT'])
                cp('act', kva[c_], PS[b2][0:CH, 0:288], [f'ps{b2}'], ['kva'])
                rmsn(kva[c_, 0:256], CH, 256, 1, ['kva'])
                stt('dve', lat[c_], kva[c_, 0:256], ss[c_, 1:2], kvg[c_], ALU.mult, ALU.mult, ['kva', 'ss', 'kvg'], ['lat'])
                dma('pool', olat[t0:t0 + CH, :], lat[c_], ['lat'], [])
                cp('pool', latb[c_], lat[c_], ['lat'], ['latb'])
                bt2 = psum('tr')
                pb2 = PS[bt2][:].bitcast(BF16)
                tr_multi([(pb2[:, k * 128:k * 128 + CH], latb[c_, k * 128:(k + 1) * 128]) for k in range(2)], identb[0:CH, 0:CH],
                         ['latb', 'cm'], [f'ps{bt2}'])
                cp('act', latT[:, :, past + t0:past + t0 + CH], pb2[:, 0:256].rearrange("p (k c) -> p k c", k=2, c=128)[:, :, 0:CH],
                   [f'ps{bt2}'], ['latT'])
                rmsn(kva[c_, 256:288], CH, 32, 2, ['kva'])
                stt('dve', krn[c_], kva[c_, 256:288], ss[c_, 2:3], krg[c_], ALU.mult, ALU.mult, ['kva', 'ss', 'krg'], ['krn'])
                do_rope(kro[c_].unsqueeze(1), krn[c_].unsqueeze(1), CH, 1, ['krn'], ['kro'])
                dma('pool', okr[t0:t0 + CH, :], kro[c_], ['kro'], [])
                cp('pool', krall[c_, kt_idx, :], kro[c_], ['kro'], ['krall'])
        P.barrier()
        P.curlabel = 'mla_grp_' + kind
        set_rings({'mm': [0, 1], 'sc': [2, 3, 4], 'acc': [5, 6], 'tr': [7]})
        A.release(mP0)
        G = getattr(cfg, 'mla_g', 4)
        NKB = getattr(cfg, 'mla_kbuf', 1)
        m1 = A.mark()
        wqb = [A.alloc([3, G * 96], BF16) for _ in range(2)]
        wkvb = [A.alloc([2, G * 128], BF16) for _ in range(2)]
        KT_l = [A.alloc([G, TK], BF16) for _ in range(NKB)]
        vfull_l = [A.alloc([NKT, G, 128], BF16) for _ in range(NKB)]
        mcnt = [0]
        qcnt = [0]
        QT_l = [A.alloc([G, TT], BF16) for _ in range(2)]
        kvs_l = [A.alloc([G, 128], F32) for _ in range(2)]
        kfull_l = [A.alloc([G, 96], BF16) for _ in range(2)]
        qs_l = [A.alloc([G, 96], F32) for _ in range(2)]
        qn_l = [A.alloc([G, 96], F32) for _ in range(2)]
        qfull_l = [A.alloc([G, 96], BF16) for _ in range(2)]
        ssq_l = [A.alloc([G, 96], F32) for _ in range(2)]
        rs8_l = [A.alloc([2 * G], F32) for _ in range(2)]
        PT = [A.alloc([TT], BF16) for _ in range(2)]
        rden = A.alloc([TT], F32)
        for kb_ in range(NKB):
            for r_ in range(G):
                if r_ % 2 == 0:
                    mset('pool', vfull_l[kb_][:, :, r_, 64:128], 1.0, [], ['vfull%d' % kb_])
                else:
                    mset('pool', vfull_l[kb_][:, :, r_, 0:64], 1.0, [], ['vfull%d' % kb_])

        def keytile_len(kt):
            return 128 if kt < nkt_past else CH

        def keytile_off(kt):
            return kt * 128 if kt < nkt_past else past + (kt - nkt_past) * CH

        nl = [0]

        def load_grp_w(gi):
            k = nl[0] % 2
            nl[0] += 1
            dma('sp', wqb[k], S_qb[j][:, :, gi * G * 96:(gi + 1) * G * 96], [f'w_qb{j}'], [f'wqb{k}'])
            dma('sp', wkvb[k], S_kvb[j][:, :, gi * G * 128:(gi + 1) * G * 128], [f'w_kvb{j}'], [f'wkvb{k}'])
            return k

        pend = load_grp_w(0)
        for gi in range(NH // G):
            k = pend
            if gi + 1 < NH // G:
                pend = load_grp_w(gi + 1)
            KT = KT_l[gi % NKB]
            vfull = vfull_l[gi % NKB]
            KTN = 'KT%d' % (gi % NKB)
            VFN = 'vfull%d' % (gi % NKB)
            for kt in range(NKT):
                mp = mcnt[0] % 2
                mcnt[0] += 1
                M_ = lambda s_, mp=mp: s_ + '_%d' % mp
                kvs = kvs_l[mp]
                kfull = kfull_l[mp]
                qs = qs_l[mp]
                qn = qn_l[mp]
                qfull = qfull_l[mp]
                ssq = ssq_l[mp]
                rs8 = rs8_l[mp]
                n = keytile_len(kt)
                off = keytile_off(kt)
                c_ = slice(0, n)
                b = psum('mm')
                mm(PS[b][0:n, 0:G * 128], [(latT[:, kc, off:off + n], wkvb[k][:, kc, :]) for kc in range(2)], ['latT', f'wkvb{k}'], [f'ps{b}'])
                cp('dve', kvs[c_], PS[b][0:n, 0:G * 128].rearrange("p (r d) -> p r d", r=G, d=128), [f'ps{b}'], [M_('kvs')])
                for r_ in range(G):
                    lo = 0 if r_ % 2 == 0 else 64
                    cp('pool', vfull[c_, kt, r_, lo:lo + 64], kvs[c_, r_, 64:128], [M_('kvs')], [VFN])
                tt('dve', ssq[c_, :, 0:64], kvs[c_, :, 0:64], kvs[c_, :, 0:64], ALU.mult, [M_('kvs')], [M_('ssq')])
                P.op('dve', lambda e, o=rs8[c_, 0:G], i_=ssq[c_, :, 0:64]: e.tensor_reduce(out=o, in_=i_, axis=AX.X, op=ALU.add), [M_('ssq')], [M_('rs8')])
                act(rs8[c_, 0:G], rs8[c_, 0:G], AF.Ln, [M_('rs8')], [M_('rs8')], bias=EPS, scale=1.0 / 64)
                act(rs8[c_, 0:G], rs8[c_, 0:G], AF.Exp, [M_('rs8')], [M_('rs8')], scale=-0.5)
                tt('dve', ssq[c_, :, 0:64], kvs[c_, :, 0:64], bc(rs8[c_, 0:G].unsqueeze(2), [n, G, 64]), ALU.mult, [M_('kvs'), M_('rs8')], [M_('ssq')])
                tt('dve', kfull[c_, :, 0:64], ssq[c_, :, 0:64], bc(kng[c_].unsqueeze(1), [n, G, 64]), ALU.mult, [M_('ssq'), 'kng'], [M_('kfull')])
                cp('pool', kfull[c_, :, 64:96], bc(krall[c_, kt, :].unsqueeze(1), [n, G, 32]), ['krall'], [M_('kfull')])
                bt = psum('tr')
                pb = PS[bt][:].bitcast(BF16)
                tr_multi([(pb[0:96, r_ * 128:r_ * 128 + n], kfull[c_, r_, :]) for r_ in range(G)], identb[0:n, 0:n], [M_('kfull'), 'cm'], [f'ps{bt}'])
                cp('act', KT[0:96, :, off:off + n], pb[0:96, 0:G * 128].rearrange("p (r c) -> p r c", r=G, c=128)[:, :, 0:n], [f'ps{bt}'], [KTN])
            for ti in range(NT):
                tok0 = ti * TT
                qp = qcnt[0] % 2
                qcnt[0] += 1
                QT = QT_l[qp]
                QTN = 'QT%d' % qp
                for ct in range(NCT):
                    mp = mcnt[0] % 2
                    mcnt[0] += 1
                    M_ = lambda s_, mp=mp: s_ + '_%d' % mp
                    kvs = kvs_l[mp]
                    kfull = kfull_l[mp]
                    qs = qs_l[mp]
                    qn = qn_l[mp]
                    qfull = qfull_l[mp]
                    ssq = ssq_l[mp]
                    rs8 = rs8_l[mp]
                    c0 = ct * CH
                    t0 = tok0 + c0
                    c_ = slice(0, CH)
                    dma('sp', rope[c_], I['c_rope'][rope_row0 + t0:rope_row0 + t0 + CH, :], [], ['rope'])
                    b = psum('mm')
                    mm(PS[b][0:CH, 0:G * 96], [(qanT[:, kc, t0:t0 + CH], wqb[k][:, kc, :]) for kc in range(3)], ['qanT', f'wqb{k}'], [f'ps{b}'])
                    cp('dve', qs[c_], PS[b][0:CH, 0:G * 96].rearrange("p (r d) -> p r d", r=G, d=96), [f'ps{b}'], [M_('qs')])
                    tt('dve', ssq[c_], qs[c_], qs[c_], ALU.mult, [M_('qs')], [M_('ssq')])
                    P.op('dve', lambda e, o=rs8[c_, 0:G], i_=ssq[c_, :, 0:64]: e.tensor_reduce(out=o, in_=i_, axis=AX.X, op=ALU.add), [M_('ssq')], [M_('rs8')])
                    P.op('dve', lambda e, o=rs8[c_, G:2 * G], i_=ssq[c_, :, 64:96]: e.tensor_reduce(out=o, in_=i_, axis=AX.X, op=ALU.add), [M_('ssq')], [M_('rs8')])
                    act(rs8[c_, 0:G], rs8[c_, 0:G], AF.Ln, [M_('rs8')], [M_('rs8')], bias=EPS, scale=1.0 / 64)
                    act(rs8[c_, G:2 * G], rs8[c_, G:2 * G], AF.Ln, [M_('rs8')], [M_('rs8')], bias=EPS, scale=1.0 / 32)
                    act(rs8[c_], rs8[c_], AF.Exp, [M_('rs8')], [M_('rs8')], scale=-0.5)
                    tt('dve', qn[c_, :, 0:64], qs[c_, :, 0:64], bc(rs8[c_, 0:G].unsqueeze(2), [CH, G, 64]), ALU.mult, [M_('qs'), M_('rs8')], [M_('qn')])
                    tt('dve', qn[c_, :, 64:96], qs[c_, :, 64:96], bc(rs8[c_, G:2 * G].unsqueeze(2), [CH, G, 32]), ALU.mult, [M_('qs'), M_('rs8')], [M_('qn')])
                    tt('dve', qfull[c_, :, 0:64], qn[c_, :, 0:64], bc(qng[c_].unsqueeze(1), [CH, G, 64]), ALU.mult, [M_('qn'), 'qng'], [M_('qfull')])
                    tt('dve', qn[c_, :, 64:96], qn[c_, :, 64:96], bc(qrg[c_].unsqueeze(1), [CH, G, 32]), ALU.mult, [M_('qn'), 'qrg'], [M_('qn')])
                    do_rope(qfull[c_, :, 64:96], qn[c_, :, 64:96], CH, G, [M_('qn')], [M_('qfull')])
                    bt = psum('tr')
                    pb = PS[bt][:].bitcast(BF16)
                    tr_multi([(pb[0:96, r_ * 128:r_ * 128 + CH], qfull[c_, r_, :]) for r_ in range(G)], identb[0:CH, 0:CH], [M_('qfull'), 'cm'], [f'ps{bt}'])
                    cp('act', QT[0:96, :, c0:c0 + CH], pb[0:96, 0:G * 128].rearrange("p (r c) -> p r c", r=G, c=128)[:, :, 0:CH], [f'ps{bt}'], [QTN])
                if kind == 'p':
                    nvis = (tok0 + TT) // 128
                else:
                    nvis = NKT
                for r_ in range(G):
                    h = gi * G + r_
                    bo = psum('acc')
                    npv = 0
                    for kt in range(nvis):
                        n = keytile_len(kt)
                        off = keytile_off(kt)
                        cs = 0
                        if kind == 'p' and kt * 128 >= tok0:
                            cs = kt * 128 - tok0
                        b = psum('sc')
                        mm(PS[b][0:n, cs:TT], [(KT[0:96, r_, off:off + n], QT[0:96, r_, cs:TT])], [KTN, QTN], [f'ps{b}'])
                        pk = npv % 2
                        act(PT[pk][0:n, cs:TT], PS[b][0:n, cs:TT], AF.Exp, [f'ps{b}'], [f'PT{pk}'], scale=SCALE)
                        if kind == 'p' and kt * 128 >= tok0:
                            mset('pool', PT[pk][64:128, cs:cs + 64], 0.0, [], [f'PT{pk}'])
                        first = (kt == 0)
                        last = (kt == nvis - 1)
                        P.op('pe', lambda e, o=PS[bo][:, cs:TT], l_=vfull[0:n, kt, r_, :], rr=PT[pk][0:n, cs:TT], f=first, la=last:
                             e.matmul(o, lhsT=l_, rhs=rr, start=f, stop=la), [f'PT{pk}', VFN], [f'ps{bo}'])
                        npv += 1
                    lo, dn = (0, 64) if r_ % 2 == 0 else (64, 0)
                    act(rden[lo:lo + 64, 0:TT], PS[bo][dn:dn + 64, 0:TT], AF.Ln, [f'ps{bo}'], ['rden'])
                    act(rden[lo:lo + 64, 0:TT], rden[lo:lo + 64, 0:TT], AF.Exp, ['rden'], ['rden'], scale=-1.0)
                    tt('dve', rden[lo:lo + 64, 0:TT], PS[bo][lo:lo + 64, 0:TT], rden[lo:lo + 64, 0:TT], ALU.mult, [f'ps{bo}', 'rden'], ['rden'])
                    tt('pool', sg[lo:lo + 64, h // 2, tok0:tok0 + TT], sg[lo:lo + 64, h // 2, tok0:tok0 + TT], rden[lo:lo + 64, 0:TT],
                       ALU.mult, ['sg', 'rden'], ['sg'])
        P.barrier()
        P.curlabel = 'mla_end_' + kind
        set_rings({'mm': [0, 1, 2, 3], 'sc': [4], 'acc': [5], 'tr': [6, 7]})
        A.release(mP0)
        sq = alloc_common(T, TT)
        wout = A.alloc([8, D], BF16)
        dma('sp', wout, S_out_mla[j], [f'w_out_mla{j}'], ['wout'])
        pin = (I['p_prompt'] if kind == 'p' else I['p_sample'])[i, sidx]
        for ti in range(NT):
            tok0 = ti * TT
            for jo in range(8):
                b = psum('mm')
                mm(PS[b][:, 0:TT], [(wout[:, kc, jo * 128:(jo + 1) * 128], sg[:, kc, tok0:tok0 + TT]) for kc in range(8)], ['wout', 'sg'], [f'ps{b}'])
                tt('dve', hT[:, jo, tok0:tok0 + TT], hT[:, jo, tok0:tok0 + TT], PS[b][:, 0:TT], ALU.add, [HT(jo, tok0), f'ps{b}'], [HT(jo, tok0)])
            ple_tile(sq, i, pin, tok0, TT, NCT, CH)
        P.barrier()
        A.release(m0)

    def run_seq(kind, sidx):
        P.curlabel = 'io_' + kind
        T = TP if kind == 'p' else TS
        TT = min(cfg.tt, T)
        TTG[0] = TT
        CH = min(128, T)
        xin = (I['x_prompt'] if kind == 'p' else I['x_sample'])[sidx]
        yout_d = (O['y_prompt'] if kind == 'p' else O['y_sample'])[sidx]
        m0 = A.mark()
        xt = [A.alloc([D], F32) for _ in range(2)]
        for t_ in range(T // CH):
            k = t_ % 2
            dma('sp', xt[k][0:CH], xin[t_ * CH:(t_ + 1) * CH, :], [], [f'xt{k}'])
            for hf in range(2):
                b = psum('tr')
                tr_multi([(PS[b][:, q * 128:q * 128 + CH], xt[k][0:CH, (hf * 4 + q) * 128:(hf * 4 + q + 1) * 128]) for q in range(4)],
                         identf[0:CH, 0:CH], [f'xt{k}', 'identf'], [f'ps{b}'])
                cp('act' if hf else 'dve', hT[:, hf * 4:(hf + 1) * 4, t_ * CH:(t_ + 1) * CH],
                   PS[b][:, 0:512].rearrange("p (q c) -> p q c", q=4, c=128)[:, :, 0:CH], [f'ps{b}'], [HT(hf * 4 + q, t_ * CH) for q in range(4)])
        P.barrier()
        A.release(m0)
        for i in range(cfg.depth):
            if getattr(cfg, 'skip_layers', False):
                continue
            if i % 2 == 0:
                if not getattr(cfg, 'skip_ssd', False):
                    ssd_layer(kind, sidx, i, T, TT, CH)
            else:
                if not getattr(cfg, 'skip_mla', False):
                    mla_layer(kind, sidx, i, T, TT, CH)
        m0 = A.mark()
        yt = [A.alloc([D], F32) for _ in range(2)]
        for t_ in range(T // CH):
            k = t_ % 2
            for hf in range(2):
                b = psum('tr')
                tr_multi([(PS[b][0:CH, q * 128:(q + 1) * 128], hT[:, hf * 4 + q, t_ * CH:(t_ + 1) * CH]) for q in range(4)],
                         identf, [HT(hf * 4 + q, t_ * CH) for q in range(4)] + ['identf'], [f'ps{b}'])
                cp('act' if hf else 'dve', yt[k][0:CH, hf * 512:(hf + 1) * 512], PS[b][0:CH, 0:512], [f'ps{b}'], [f'yt{k}'])
            dma('pool', yout_d[t_ * CH:(t_ + 1) * CH, :], yt[k][0:CH], [f'yt{k}'], [])
        P.barrier()
        A.release(m0)

    for s_i in range(cfg.nseq_p):
        if not getattr(cfg, 'skip_p', False):
            run_seq('p', s_i)
    for s_i in range(cfg.nseq_s):
        if not getattr(cfg, 'skip_s', False):
            run_seq('s', s_i)
    P.barrier()

    P.finalize(reorder=getattr(cfg, 'reorder', True))
    semnames = sorted(P.cnt.keys())
    sem_ctx = {k: nc.semaphore("s_" + k) for k in semnames}
    sems = {k: c.__enter__() for k, c in sem_ctx.items()}
    engobj = {'pe': 'tensor', 'act': 'scalar', 'dve': 'vector', 'pool': 'gpsimd', 'sp': 'sync'}
    with nc.Block() as block:
        def emit(engname):
            def body(e):
                for item in P.streams[engname]:
                    if item[0] == 'wait':
                        e.wait_ge(sems[item[1]], item[2])
                    else:
                        _, fn, key, inc = item[:4]
                        fn(e).then_inc(sems[key], inc)
            return body
        block.tensor(emit('pe'))
        block.scalar(emit('act'))
        block.vector(emit('dve'))
        block.gpsimd(emit('pool'))
        block.sync(emit('sp'))
    for c in sem_ctx.values():
        c.__exit__(None, None, None)
    for t in reversed(ctx):
        t.__exit__(None, None, None)
    stats = {e: len(P.streams[e]) for e in ENGS}
    build_program.planner = P
    return nc, stats, A.peak


def make_consts(cfg):
    a = np.arange(128)
    identf = np.zeros((128, 2, 128), np.float32)
    identf[:, 0, :] = np.eye(128)
    identf[:, 1, :] = (a[:, None] > a[None, :])
    mats = np.zeros((128, 8, 128), np.float32)
    mats[:, 4:8, :] = np.where(a[None, :] < a[:, None], -30000.0, 0.0)[:, None, :]
    mats[:, 0, :] = np.eye(128)
    mats[:, 1, :] = (a[:, None] <= a[None, :])
    mats[:, 2, :] = (a[:, None] > a[None, :])
    mats[:, 3, :] = 1.0
    pos = np.concatenate([np.arange(cfg.t_p), cfg.past + np.arange(cfg.t_s)]).astype(np.float32)
    inv = (1.0 / (10000.0 ** (np.arange(0, 32, 2, dtype=np.float32) / 32.0))).astype(np.float32)
    ang = pos[:, None] * inv[None, :]
    rope = np.concatenate([np.cos(ang), np.sin(ang)], axis=1).astype(np.float32)
    return {'c_identf': identf, 'c_mats': mats.astype(ml_dtypes.bfloat16), 'c_rope': rope}


_BATCH_P = ['x_prompt']
_WEIGHTS = ['ln_w', 'ssd_in_w', 'ssd_conv_w', 'ssd_conv_b', 'ssd_dt_bias', 'ssd_A_log', 'ssd_D', 'ssd_norm_w', 'ssd_out_w',
            'mla_in_w', 'mla_q_a_norm', 'mla_q_b_w', 'mla_kv_a_norm', 'mla_kv_b_w', 'mla_q_nope_norm', 'mla_q_rope_norm',
            'mla_k_nope_norm', 'mla_k_rope_norm', 'mla_out_w', 'ple_up_w', 'ple_norm_w', 'ple_gate_w']


def shard_inputs(inputs, cfg, ncores):
    consts = make_consts(cfg)
    maps = []
    f = lambda a: np.ascontiguousarray(np.asarray(a, dtype=np.float32))
    for c in range(ncores):
        p0, p1 = c * cfg.nseq_p, (c + 1) * cfg.nseq_p
        s0, s1 = c * cfg.nseq_s, (c + 1) * cfg.nseq_s
        m = {
            'x_prompt': f(inputs['x_prompt'][p0:p1]),
            'x_sample': f(inputs['x_sample'][s0:s1]),
            'cache_conv': f(inputs['cache_conv'][:, s0:s1]),
            'state_ssm': f(inputs['state_ssm'][:, s0:s1]),
            'cache_kv_latent': f(inputs['cache_kv_latent'][:, s0:s1]),
            'cache_k_rope': f(inputs['cache_k_rope'][:, s0:s1]),
            'p_prompt': f(inputs['p_prompt'][:, p0:p1]),
            'p_sample': f(inputs['p_sample'][:, s0:s1]),
        }
        for wn in _WEIGHTS:
            m[wn] = f(inputs[wn])
        m.update(consts)
        maps.append(m)
    return maps


def gather_outputs(results, cfg):
    cat = lambda k, ax: np.concatenate([np.asarray(r[k], dtype=np.float32) for r in results], axis=ax)
    return (cat('y_prompt', 0), cat('y_sample', 0), cat('conv_prompt', 1), cat('ssm_prompt', 1),
            cat('lat_prompt', 1), cat('kr_prompt', 1), cat('conv_sample', 1), cat('ssm_sample', 1),
            cat('lat_sample', 1), cat('kr_sample', 1))


def kernel(**inputs):
    cfg = Cfg()
    ncores = 8
    nc, stats, peak = build_program(cfg)
    maps = shard_inputs(inputs, cfg, ncores)
    res = run_bass_kernel_spmd(nc, maps, core_ids=list(range(ncores)))
    return gather_outputs(res.results, cfg)
```
